# Optimizing a Trainium2 kernel written in Bass

```python
import math
import jax, jax.numpy as jnp
from jax import lax
import numpy as np

D_MODEL = 1024
BATCH = 8
SEQ = 2048
DEPTH = 4
DEC_BATCH = 32
DEC_SEQ = 2048
PAST_LEN = 128

DN_HEADS = 4
DN_DK = 128
DN_DV = 128
DN_QK_W = DN_HEADS * DN_DK
DN_WIDTH = DN_HEADS * DN_DV
CONV_K = 5
CHUNK = 64
DA_HEADS = 4
DA_DQK = 64
DA_DV = 2 * DA_DQK
DA_QK_W = DA_HEADS * 2 * DA_DQK
DA_WIDTH = DA_HEADS * DA_DV
Q_BLOCK = 128
D_FF = 2816
N_BRANCH = 2
NORM_EPS = 1e-6
SUBLN_EPS = 1e-5
SPLIT_SIZES = (DN_QK_W, DN_QK_W, DN_WIDTH, DN_WIDTH, 2 * DN_HEADS, 2 * DN_HEADS,
               DA_QK_W, DA_QK_W, DA_WIDTH, N_BRANCH * D_MODEL)
N_IN = 2 * DN_QK_W + 2 * DN_WIDTH + 4 * DN_HEADS + 2 * DA_QK_W + DA_WIDTH + N_BRANCH * D_MODEL
CONV_CH = 2 * DN_QK_W + DN_WIDTH

kernel_name = 'hybrid_bidir_deltanet_diffattn_encoder'


def rmsnorm(x, w, eps=NORM_EPS):
    xf = x.astype(jnp.float32)
    y = xf * lax.rsqrt(jnp.mean(xf * xf, axis=-1, keepdims=True) + eps)
    return (y * w.astype(jnp.float32)).astype(x.dtype)


def swiglu(x, wg, wu, wd):
    return (jax.nn.silu(x @ wg) * (x @ wu)) @ wd


def l2norm(x, eps=1e-6):
    return x * lax.rsqrt(jnp.sum(x * x, axis=-1, keepdims=True) + eps)


def short_conv(x, w):
    return lax.conv_general_dilated(
        x, w[:, None, :], window_strides=(1,),
        padding=[(CONV_K // 2, CONV_K // 2)],
        dimension_numbers=('NWC', 'WIO', 'NWC'),
        feature_group_count=x.shape[-1])


def alibi_slopes():
    return 2.0 ** (-8.0 * jnp.arange(1, DA_HEADS + 1, dtype=jnp.float32) / DA_HEADS)


def chunk_gated_delta(q, k, v, beta, g):
    B, S, H, DK = q.shape
    DV = v.shape[-1]
    N = S // CHUNK
    q, k, v = [t.reshape(B, N, CHUNK, H, t.shape[-1]).transpose(0, 3, 1, 2, 4) for t in (q, k, v)]
    beta, g = [t.reshape(B, N, CHUNK, H).transpose(0, 3, 1, 2) for t in (beta, g)]
    gc = jnp.cumsum(g, axis=-1)
    idx = jnp.arange(CHUNK)
    incl = idx[:, None] >= idx[None, :]
    strict = idx[:, None] > idx[None, :]
    decay = jnp.exp(jnp.where(incl, gc[..., :, None] - gc[..., None, :], -jnp.inf))
    kb = k * beta[..., None]
    lower = jnp.where(strict, jnp.einsum('bhncd,bhnsd->bhncs', kb, k) * decay, 0.0)
    eye = jnp.eye(CHUNK, dtype=q.dtype)
    rhs = jnp.concatenate([v * beta[..., None], kb * jnp.exp(gc)[..., None]], axis=-1)
    sol = lax.linalg.triangular_solve(lower + eye, rhs, left_side=True, lower=True,
                                      unit_diagonal=True)
    u, w = sol[..., :DV], sol[..., DV:]
    a_qk = jnp.einsum('bhncd,bhnsd->bhncs', q, k) * decay
    q_g = q * jnp.exp(gc)[..., None]
    g_last = gc[..., -1]
    k_d = k * jnp.exp(g_last[..., None] - gc)[..., None]

    def step(state, xs):
        q_c, a_c, u_c, w_c, k_c, gl_c = xs
        delta = u_c - jnp.einsum('bhcd,bhde->bhce', w_c, state)
        o = jnp.einsum('bhcd,bhde->bhce', q_c, state) + jnp.einsum('bhcs,bhse->bhce', a_c, delta)
        state = state * jnp.exp(gl_c)[..., None, None] + jnp.einsum('bhcd,bhce->bhde', k_c, delta)
        return state, o

    xs = tuple(jnp.moveaxis(t, 2, 0) for t in (q_g, a_qk, u, w, k_d, g_last))
    s0 = jnp.zeros((B, H, DK, DV), q.dtype)
    _, o = lax.scan(step, s0, xs)
    return o.transpose(1, 0, 3, 2, 4).reshape(B, S, H, DV)


def gated_deltanet(q, k, v, z, b, a, conv_w, a_log, dt_bias, norm_w):
    B, S, _ = q.shape
    f32 = jnp.float32
    qkv = jnp.concatenate([q, k, v], axis=-1).astype(f32)
    qkv = jax.nn.silu(short_conv(qkv, conv_w.astype(f32)))
    q, k, v = jnp.split(qkv, [DN_QK_W, 2 * DN_QK_W], axis=-1)
    q = l2norm(q.reshape(B, S, DN_HEADS, DN_DK)) * (DN_DK ** -0.5)
    k = l2norm(k.reshape(B, S, DN_HEADS, DN_DK))
    v = v.reshape(B, S, DN_HEADS, DN_DV)
    beta = jax.nn.sigmoid(b.astype(f32)).reshape(B, S, 2, DN_HEADS)
    gdec = -jnp.exp(a_log.astype(f32)) * jax.nn.softplus(
        a.astype(f32).reshape(B, S, 2, DN_HEADS) + dt_bias.astype(f32))
    o_fwd = chunk_gated_delta(q, k, v, beta[:, :, 0], gdec[:, :, 0])
    flip = lambda t: t[:, ::-1]
    o_bwd = flip(chunk_gated_delta(flip(q), flip(k), flip(v), flip(beta[:, :, 1]), flip(gdec[:, :, 1])))
    o = rmsnorm(o_fwd + o_bwd, norm_w) * jax.nn.silu(z.astype(f32).reshape(B, S, DN_HEADS, DN_DV))
    return o.reshape(B, S, DN_WIDTH)


def diff_attention(q, k, v, lam_params, subln_w, lambda_init):
    B, S, _ = q.shape
    q = q.reshape(B, S, DA_HEADS, 2, DA_DQK) * (DA_DQK ** -0.5)
    k = k.reshape(B, S, DA_HEADS, 2, DA_DQK)
    v = v.reshape(B, S, DA_HEADS, DA_DV)
    lp = lam_params.astype(jnp.float32)
    lam = jnp.exp(jnp.sum(lp[0] * lp[1])) - jnp.exp(jnp.sum(lp[2] * lp[3])) + lambda_init
    slopes = alibi_slopes()
    kpos = jnp.arange(S)
    nb = S // Q_BLOCK
    qb = q.reshape(B, nb, Q_BLOCK, DA_HEADS, 2, DA_DQK).transpose(1, 0, 2, 3, 4, 5)

    def block(args):
        q_blk, start = args
        qpos = start + jnp.arange(Q_BLOCK)
        bias = -slopes[:, None, None] * jnp.abs(qpos[:, None] - kpos[None, :]).astype(jnp.float32)
        s = jnp.einsum('bqhmd,bkhmd->bhmqk', q_blk, k).astype(jnp.float32) + bias[None, :, None]
        p = jax.nn.softmax(s, axis=-1)
        attn = p[:, :, 0] - lam * p[:, :, 1]
        return jnp.einsum('bhqk,bkhe->bqhe', attn.astype(v.dtype), v)

    o = lax.map(block, (qb, jnp.arange(nb) * Q_BLOCK))
    o = o.transpose(1, 0, 2, 3, 4).reshape(B, S, DA_HEADS, DA_DV)
    o = rmsnorm(o, subln_w, eps=SUBLN_EPS) * (1.0 - lambda_init)
    return o.reshape(B, S, DA_WIDTH)


def split_cols(proj):
    idx, acc = [], 0
    for s in SPLIT_SIZES[:-1]:
        acc += s
        idx.append(acc)
    return jnp.split(proj, idx, axis=-1)


def trunk(x, ffn1_norm, ffn1_wg, ffn1_wu, ffn1_wd, mix_norm, w_in, conv_w, dn_a_log,
          dn_dt_bias, dn_out_norm, diff_lambda, diff_subln, w_branch_dn, w_branch_da, w_out,
          ffn2_norm, ffn2_wg, ffn2_wu, ffn2_wd, final_norm):
    B, S, D = x.shape
    for l in range(DEPTH):
        x = x + 0.5 * swiglu(rmsnorm(x, ffn1_norm[l]), ffn1_wg[l], ffn1_wu[l], ffn1_wd[l])
        h = rmsnorm(x, mix_norm[l])
        (dq, dk, dv, dz, db, da, aq, ak, av, gates) = split_cols(h @ w_in[l])
        y_dn = gated_deltanet(dq, dk, dv, dz, db, da, conv_w[l], dn_a_log[l], dn_dt_bias[l],
                              dn_out_norm[l]).astype(x.dtype) @ w_branch_dn[l]
        lambda_init = 0.8 - 0.6 * math.exp(-0.3 * l)
        y_da = diff_attention(aq, ak, av, diff_lambda[l], diff_subln[l], lambda_init).astype(x.dtype) @ w_branch_da[l]
        g = jax.nn.sigmoid(gates.astype(jnp.float32)).reshape(B, S, N_BRANCH, D).astype(x.dtype)
        merged = g[:, :, 0] * y_dn + g[:, :, 1] * y_da
        x = x + merged @ w_out[l]
        x = x + 0.5 * swiglu(rmsnorm(x, ffn2_norm[l]), ffn2_wg[l], ffn2_wu[l], ffn2_wd[l])
    return rmsnorm(x, final_norm)


def setup_inputs(seed: int = 0) -> dict:
    key = jax.random.key(seed)
    ks = jax.random.split(key, 24)
    f32 = jnp.float32
    nrm = lambda k, shape, scale: jax.random.normal(k, shape, f32) * scale
    gain = lambda k, shape: 1.0 + 0.02 * jax.random.normal(k, shape, f32)
    dt = jnp.exp(jax.random.uniform(ks[9], (DEPTH, 2, DN_HEADS), f32,
                                    minval=math.log(1e-3), maxval=math.log(1e-1)))
    return {
        'x_prompt': nrm(ks[0], (BATCH, SEQ, D_MODEL), 1.0),
        'x_sample': nrm(ks[1], (DEC_BATCH, DEC_SEQ, D_MODEL), 1.0),
        'ffn1_norm': gain(ks[2], (DEPTH, D_MODEL)),
        'ffn1_wg': nrm(ks[3], (DEPTH, D_MODEL, D_FF), D_MODEL ** -0.5),
        'ffn1_wu': nrm(ks[4], (DEPTH, D_MODEL, D_FF), D_MODEL ** -0.5),
        'ffn1_wd': nrm(ks[5], (DEPTH, D_FF, D_MODEL), D_FF ** -0.5),
        'mix_norm': gain(ks[6], (DEPTH, D_MODEL)),
        'w_in': nrm(ks[7], (DEPTH, D_MODEL, N_IN), D_MODEL ** -0.5),
        'conv_w': nrm(ks[8], (DEPTH, CONV_K, CONV_CH), CONV_K ** -0.5),
        'dn_a_log': jnp.log(jax.random.uniform(ks[10], (DEPTH, 2, DN_HEADS), f32, minval=1.0, maxval=16.0)),
        'dn_dt_bias': dt + jnp.log(-jnp.expm1(-dt)),
        'dn_out_norm': gain(ks[11], (DEPTH, DN_DV)),
        'diff_lambda': nrm(ks[12], (DEPTH, 4, DA_DQK), 0.1),
        'diff_subln': gain(ks[13], (DEPTH, DA_DV)),
        'w_branch_dn': nrm(ks[14], (DEPTH, DN_WIDTH, D_MODEL), DN_WIDTH ** -0.5),
        'w_branch_da': nrm(ks[15], (DEPTH, DA_WIDTH, D_MODEL), DA_WIDTH ** -0.5),
        'w_out': nrm(ks[16], (DEPTH, D_MODEL, D_MODEL), D_MODEL ** -0.5),
        'ffn2_norm': gain(ks[17], (DEPTH, D_MODEL)),
        'ffn2_wg': nrm(ks[18], (DEPTH, D_MODEL, D_FF), D_MODEL ** -0.5),
        'ffn2_wu': nrm(ks[19], (DEPTH, D_MODEL, D_FF), D_MODEL ** -0.5),
        'ffn2_wd': nrm(ks[20], (DEPTH, D_FF, D_MODEL), D_FF ** -0.5),
        'final_norm': gain(ks[21], (D_MODEL,)),
    }


def reference(x_prompt, x_sample, ffn1_norm, ffn1_wg, ffn1_wu, ffn1_wd, mix_norm, w_in, conv_w,
              dn_a_log, dn_dt_bias, dn_out_norm, diff_lambda, diff_subln, w_branch_dn,
              w_branch_da, w_out, ffn2_norm, ffn2_wg, ffn2_wu, ffn2_wd, final_norm):
    weights = (ffn1_norm, ffn1_wg, ffn1_wu, ffn1_wd, mix_norm, w_in, conv_w, dn_a_log,
               dn_dt_bias, dn_out_norm, diff_lambda, diff_subln, w_branch_dn, w_branch_da,
               w_out, ffn2_norm, ffn2_wg, ffn2_wu, ffn2_wd, final_norm)
    y_prompt = trunk(x_prompt, *weights)
    y_sample = trunk(x_sample, *weights)
    return (y_prompt, y_sample)
```

```python
import math
import numpy as np
import concourse.bass as bass
import concourse.mybir as mybir
from concourse.bass_utils import run_bass_kernel_spmd

F32 = mybir.dt.float32
BF16 = mybir.dt.bfloat16
AF = mybir.ActivationFunctionType
ALU = mybir.AluOpType

D = 1024
S = 2048
DFF = 2816
NIN = 5648
DEPTH = 4
NCORES = 8
NSEQ = 5
KC = 8
FCH = 22
NDSLOT = 8
HEADS = [0, 1, 2, 3]

WSHAPES = [
    ("ffn1_norm", [DEPTH, D]), ("ffn1_wg", [DEPTH, D, DFF]), ("ffn1_wu", [DEPTH, D, DFF]),
    ("ffn1_wd", [DEPTH, DFF, D]), ("mix_norm", [DEPTH, D]), ("w_in", [DEPTH, D, NIN]),
    ("conv_w", [DEPTH, 5, 1536]), ("dn_a_log", [DEPTH, 2, 4]), ("dn_dt_bias", [DEPTH, 2, 4]),
    ("dn_out_norm", [DEPTH, 128]), ("diff_lambda", [DEPTH, 4, 64]), ("diff_subln", [DEPTH, 128]),
    ("w_branch_dn", [DEPTH, 512, D]), ("w_branch_da", [DEPTH, 512, D]), ("w_out", [DEPTH, D, D]),
    ("ffn2_norm", [DEPTH, D]), ("ffn2_wg", [DEPTH, D, DFF]), ("ffn2_wu", [DEPTH, D, DFF]),
    ("ffn2_wd", [DEPTH, DFF, D]), ("final_norm", [D]),
]


class Prog:
    def __init__(self, nc):
        self.nc = nc
        self.ops = []
        self.keys = set()

    def add(self, eng, fn, r=(), w=(), dma=False):
        self.ops.append((eng, fn, tuple(r), tuple(w), dma))
        self.keys.update(r)
        self.keys.update(w)

    def barrier(self, ap):
        self.add("dve", lambda e: e.memset(ap, 0.0), w=list(self.keys) + ["__BAR__"])

    def emit(self, same_engine_sync=True):
        nc = self.nc
        ops = self.ops
        n = len(ops)
        last_w = {}
        rd = {}
        deps = [None] * n
        needed = [False] * n
        cur_bar = None
        for i, (eng, fn, r, w, dma) in enumerate(ops):
            d = set()
            for k in r:
                j = last_w.get(k, cur_bar)
                if j is not None:
                    d.add(j)
            for k in w:
                j = last_w.get(k, cur_bar)
                if j is not None:
                    d.add(j)
                rr = rd.get(k)
                if rr:
                    d.update(rr.values())
            d.discard(i)
            dd = []
            for j in d:
                je = ops[j][0]
                jd = ops[j][4]
                if not jd and not dma and je == eng:
                    if eng == "pe" or not same_engine_sync:
                        continue
                dd.append(j)
            deps[i] = dd
            for j in dd:
                needed[j] = True
            tag = ("d", i) if dma else eng
            for k in r:
                rd.setdefault(k, {})[tag] = i
            for k in w:
                last_w[k] = i
                rd[k] = {}
            if "__BAR__" in w:
                cur_bar = i
                last_w = {}
                rd = {}
        cnt = {"pe": 0, "dve": 0, "act": 0, "pool": 0}
        dcnt = {"sp": 0, "pool": 0, "act": 0}
        sig = [None] * n
        dslot = [None] * n
        for i, (eng, fn, r, w, dma) in enumerate(ops):
            if dma:
                k = dcnt[eng]
                dcnt[eng] += 1
                slot = k % NDSLOT
                sig[i] = (("dma", eng, slot), 16 * (k // NDSLOT + 1))
                dslot[i] = (slot, 16 * (k // NDSLOT))
            elif needed[i]:
                cnt[eng] += 1
                sig[i] = (("c", eng), cnt[eng])
        per_eng = {"pe": [], "dve": [], "act": [], "pool": [], "sp": []}
        for i, op in enumerate(ops):
            per_eng[op[0]].append(i)
        semkeys = [("c", e) for e in ("pe", "dve", "act", "pool")]
        for q in ("sp", "pool", "act"):
            if dcnt[q]:
                semkeys += [("dma", q, s) for s in range(NDSLOT)]
        from contextlib import ExitStack
        with ExitStack() as es:
            sems = {}
            for sk in semkeys:
                sems[sk] = es.enter_context(nc.semaphore("s_" + "_".join(str(t) for t in sk)))
            block = es.enter_context(nc.Block())

            def run_engine(ename, e):
                waited = {}
                final = {}
                for i in per_eng[ename]:
                    eng, fn, r, w, dma = ops[i]
                    if dma:
                        slot, prev = dslot[i]
                        sk = ("dma", eng, slot)
                        if prev > 0 and waited.get(sk, 0) < prev:
                            e.wait_ge(sems[sk], prev)
                            waited[sk] = prev
                    need = {}
                    for j in deps[i]:
                        sk, v = sig[j]
                        if need.get(sk, 0) < v:
                            need[sk] = v
                    for sk, v in need.items():
                        if waited.get(sk, 0) < v:
                            e.wait_ge(sems[sk], v)
                            waited[sk] = v
                    ins = fn(e)
                    if sig[i] is not None:
                        sk, v = sig[i]
                        if dma:
                            ins.then_inc(sems[sk], 16)
                            final[sk] = v
                        else:
                            ins.then_inc(sems[sk], 1)
                for sk, v in final.items():
                    if waited.get(sk, 0) < v:
                        e.wait_ge(sems[sk], v)

            @block.tensor
            def _(e):
                run_engine("pe", e)

            @block.vector
            def _(e):
                run_engine("dve", e)

            @block.scalar
            def _(e):
                run_engine("act", e)

            @block.gpsimd
            def _(e):
                run_engine("pool", e)

            @block.sync
            def _(e):
                run_engine("sp", e)


def build(nseq=NSEQ, depth=DEPTH, do_ffn=True, do_mix=True, do_dn=True, do_da=True, same_engine_sync=True, debug=False):
    nc = bass.Bass("TRN2", target_bir_lowering=False)
    x = nc.dram_tensor("x", [nseq, S, D], F32, kind="ExternalInput").ap()
    y = nc.dram_tensor("y", [nseq, S, D], F32, kind="ExternalOutput").ap()
    W = {}
    for name, shape in WSHAPES:
        W[name] = nc.dram_tensor(name, shape, F32, kind="ExternalInput").ap()
    c_ident = nc.dram_tensor("c_ident", [128, 128], F32, kind="ExternalInput").ap()
    c_bs = nc.dram_tensor("c_bs", [128, 512], F32, kind="ExternalInput").ap()
    c_abs = nc.dram_tensor("c_abs", [128, 896], F32, kind="ExternalInput").ap()
    c_masks = nc.dram_tensor("c_masks", [4, 128, 128], F32, kind="ExternalInput").ap()

    P = Prog(nc)

    def dump(nm, ap):
        if not debug:
            return
        dt_ = nc.dram_tensor("dbg_" + nm, [ap.shape[0], ap.shape[1]], F32, kind="ExternalOutput").ap()
        P.add("pool", lambda e: e.dma_start(out=dt_, in_=ap), r=list(P.keys), w=[("dbg", nm)], dma=True)
    off = [16512]

    def sb(name, shape, dtype, at=None):
        esz = 4 if dtype == F32 else 2
        nb = esz
        for s_ in shape[1:]:
            nb *= s_
        o = off[0] if at is None else at
        t = nc.alloc_sbuf_tensor_at(name, list(shape), dtype, offset=o)
        if at is None:
            off[0] = o + ((nb + 31) // 32) * 32
        return t

    ident_f = sb("ident_f", [128, 128], F32)
    ident_b = sb("ident_b", [128, 128], BF16)
    ones_f = sb("ones_f", [128, 128], F32)
    ones_b = sb("ones_b", [128, 128], BF16)
    eps_t = sb("eps_t", [128, 1], F32)
    nw1 = sb("nw1", [128, DEPTH, KC], F32)
    nwm = sb("nwm", [128, DEPTH, KC], F32)
    nw2 = sb("nw2", [128, DEPTH, KC], F32)
    nwf = sb("nwf", [128, KC], F32)
    bar = sb("bar", [128, 8], F32)
    eps5_t = sb("eps5_t", [128, 1], F32)
    bs_t = sb("bs_t", [128, 512], F32)
    abs_t = sb("abs_t", [128, 896], F32)
    sublnw = sb("sublnw", [128, DEPTH], F32)
    neglam = sb("neglam", [128, DEPTH], F32)
    lam_s = sb("lam_s", [128, 4], F32)
    convw = sb("convw", [128, DEPTH, 12, 5], F32)
    wdn = sb("wdn", [128, DEPTH], F32)
    R_x = off[0]
    xT = sb("xT", [128, KC, S], F32)
    R_h = off[0]
    hT = sb("hT", [128, KC, S], BF16)
    R_in = off[0]
    off[0] += 48 * 1024
    R_o = off[0]
    off[0] += 32 * 1024
    R_free = off[0]
    assert off[0] <= 229344, off[0]
    aT = sb("aT", [128, FCH, 1024], BF16, at=R_in)
    o_ = R_o
    wgu = []
    for b in range(2):
        wgu.append(sb(f"wgu{b}", [128, KC, 2, 256], BF16, at=o_))
        o_ += 8192
    wdb = []
    for b in range(4):
        wdb.append(sb(f"wdb{b}", [128, D], BF16, at=o_))
        o_ += 2048
    sq = [sb(f"sq{b}", [128, 512], F32, at=o_ + 2048 * b) for b in range(2)]
    o_ += 4096
    sgt = [sb(f"sgt{b}", [128, 512], F32, at=o_ + 2048 * b) for b in range(2)]
    o_ += 4096
    assert o_ <= R_free
    lnt = sb("lnt", [128, 512], F32, at=R_free)
    rstd = sb("rstd", [128, 512], F32, at=R_free + 2048)
    xin = [sb(f"xin{b}", [128, D], F32, at=R_in + 4096 * b) for b in range(2)]
    yo = [sb(f"yo{b}", [128, D], F32, at=R_in + 8192 + 4096 * b) for b in range(2)]
    yTf = sb("yTf", [128, KC, 512], F32, at=R_h)

    ps = [nc.alloc_psum_tensor(f"ps{b}", [128, 512], F32) for b in range(8)]

    def PS(b):
        return ("ps", b)

    P.add("sp", lambda e: e.dma_start(out=ident_f[:], in_=c_ident), w=["ident_f"], dma=True)
    P.add("pool", lambda e: e.dma_start(out=ident_b[:], in_=c_ident), w=["ident_b"], dma=True)
    P.add("dve", lambda e: e.memset(ones_f[:], 1.0), w=["ones_f"])
    P.add("dve", lambda e: e.memset(ones_b[:], 1.0), w=["ones_b"])
    P.add("dve", lambda e: e.memset(eps_t[:], 1e-6), w=["eps_t"])
    for t_, nm in ((nw1, "ffn1_norm"), (nwm, "mix_norm"), (nw2, "ffn2_norm")):
        P.add("sp", lambda e, t_=t_, nm=nm: e.dma_start(
            out=t_[:], in_=W[nm].rearrange("l (c p) -> p l c", p=128), allow_slow_non_contiguous=True),
            w=[nm], dma=True)
    P.add("sp", lambda e: e.dma_start(out=nwf[:], in_=W["final_norm"].rearrange("(c p) -> p c", p=128),
                                      allow_slow_non_contiguous=True), w=["final_norm"], dma=True)

    P.add("dve", lambda e: e.memset(eps5_t[:], 1e-5), w=["eps5_t"])
    P.add("sp", lambda e: e.dma_start(out=bs_t[:], in_=c_bs), w=["bs_t"], dma=True)
    P.add("sp", lambda e: e.dma_start(out=abs_t[:], in_=c_abs), w=["abs_t"], dma=True)
    P.add("sp", lambda e: e.dma_start(out=sublnw[:], in_=W["diff_subln"].rearrange("l p -> p l"),
                                      allow_slow_non_contiguous=True), w=["sublnw"], dma=True)
    for l_ in range(DEPTH):
        for kk_ in range(5):
            P.add("sp", lambda e, l_=l_, kk_=kk_: e.dma_start(out=convw[:, l_, :, kk_], in_=W["conv_w"][l_, kk_].rearrange("(c p) -> p c", p=128),
                                                  allow_slow_non_contiguous=True), w=["convw"], dma=True)
    P.add("sp", lambda e: e.dma_start(out=wdn[:], in_=W["dn_out_norm"].rearrange("l p -> p l"),
                                      allow_slow_non_contiguous=True), w=["wdn"], dma=True)
    lpb = sb("lpb", [128, DEPTH, 4, 64], F32, at=R_free)
    lpt = sb("lpt", [128, 64], F32, at=R_free + 4096)
    P.add("sp", lambda e: e.dma_start(out=lpb[:].rearrange("p l a d -> p (l a d)"),
                                      in_=W["diff_lambda"].rearrange("l a d -> (l a d)").partition_broadcast(128)),
          w=["lpb"], dma=True)
    for l in range(DEPTH):
        li = 0.8 - 0.6 * math.exp(-0.3 * l)
        for a in range(2):
            P.add("dve", lambda e, l=l, a=a: e.tensor_tensor(out=lpt[:], in0=lpb[:, l, 2 * a, :], in1=lpb[:, l, 2 * a + 1, :],
                                                          op=ALU.mult), r=["lpb"], w=["lpt"])
            P.add("dve", lambda e, a=a: e.reduce_sum(lam_s[:, a:a + 1], lpt[:], axis=mybir.AxisListType.X),
                  r=["lpt"], w=["lam_s"])
        P.add("act", lambda e: e.activation(lam_s[:, 2:4], lam_s[:, 0:2], AF.Exp), r=["lam_s"], w=["lam_s"])
        P.add("dve", lambda e, l=l: e.tensor_tensor(out=neglam[:, l:l + 1], in0=lam_s[:, 3:4], in1=lam_s[:, 2:3],
                                                 op=ALU.subtract), r=["lam_s"], w=["neglam"])
        P.add("dve", lambda e, l=l, li=li: e.tensor_scalar(out=neglam[:, l:l + 1], in0=neglam[:, l:l + 1], scalar1=-li,
                                                        scalar2=None, op0=ALU.add), r=["neglam"], w=["neglam"])
        P.add("dve", lambda e, l=l, li=li: e.tensor_scalar(out=sublnw[:, l:l + 1], in0=sublnw[:, l:l + 1], scalar1=1.0 - li,
                                                        scalar2=None, op0=ALU.mult), r=["sublnw"], w=["sublnw"])
    P.barrier(bar[:])

    rr = {"evac": 0, "bank": 0}

    def evac_eng():
        rr["evac"] ^= 1
        return "dve" if rr["evac"] else "act"

    def copy_op(eng, out, in_, r, w):
        if eng == "act":
            P.add("act", lambda e: e.copy(out, in_), r=r, w=w)
        else:
            P.add(eng, lambda e: e.tensor_copy(out, in_), r=r, w=w)

    def xk(c, t):
        return ("x", c, t)

    def hk(c, t):
        return ("h", c, t)

    def load_x(s):
        for j in range(16):
            b = j % 2
            T = j // 4
            P.add("sp", lambda e, b=b, j=j: e.dma_start(out=xin[b][:], in_=x[s, j * 128:(j + 1) * 128, :]),
                  r=(), w=[("xin", b)], dma=True)
            for half in range(2):
                bank = (2 * j + half) % 4
                for i in range(4):
                    c = 4 * half + i
                    P.add("pe", lambda e, bank=bank, i=i, b=b, c=c: e.transpose(
                        ps[bank][:, i * 128:(i + 1) * 128], xin[b][:, c * 128:(c + 1) * 128], ident_f[:]),
                        r=[("xin", b), "ident_f"], w=[PS(bank)])
                copy_op(evac_eng(), xT[:, 4 * half:4 * half + 4, j * 128:(j + 1) * 128],
                        ps[bank][:].rearrange("p (a t) -> p a t", a=4),
                        r=[PS(bank)], w=[xk(4 * half + i, T) for i in range(4)])

    sqm = [sb(f"sqm{b}", [128, 512], F32, at=R_free + 4096 + 2048 * b) for b in range(2)]

    def rstd_for_tile(T, eps_tile, scale, sq=sq):
        tsl = slice(T * 512, (T + 1) * 512)
        for c in range(KC):
            b = c % 2
            P.add("act", lambda e, c=c, b=b: e.activation(sq[b][:], xT[:, c, tsl], AF.Square),
                  r=[xk(c, T)], w=[("sq", b)])
            P.add("pe", lambda e, c=c, b=b: e.matmul(ps[7][:], ones_f[:], sq[b][:], start=(c == 0), stop=(c == KC - 1)),
                  r=[("sq", b), "ones_f"], w=[PS(7)])
        P.add("act", lambda e: e.activation(lnt[:], ps[7][:], AF.Ln, bias=eps_tile[:], scale=scale),
              r=[PS(7), "eps_t"], w=["lnt"])
        P.add("act", lambda e: e.activation(rstd[:], lnt[:], AF.Exp, scale=-0.5), r=["lnt"], w=["rstd"])

    def norm_to_h(T, nw_ap, sq=sq):
        tsl = slice(T * 512, (T + 1) * 512)
        rstd_for_tile(T, eps_t, 1.0 / D, sq)
        for c in range(KC):
            P.add("dve", lambda e, c=c: e.scalar_tensor_tensor(
                out=hT[:, c, tsl], in0=xT[:, c, tsl], scalar=nw_ap[:, c:c + 1], op0=ALU.mult,
                in1=rstd[:], op1=ALU.mult), r=[xk(c, T), "rstd"], w=[hk(c, T)])

    def ffn(l, nw, wg, wu, wd):
        nw_ap = nw[:, l, :]
        for half in range(2):
            tiles = [2 * half, 2 * half + 1]
            for T in tiles:
                norm_to_h(T, nw_ap)
            for fg in range(FCH // 2):
                b = fg % 2
                for gi, wsrc in enumerate((wg, wu)):
                    P.add("pool", lambda e, b=b, gi=gi, wsrc=wsrc, fg=fg: e.dma_start(
                        out=wgu[b][:, :, gi, :],
                        in_=wsrc[l][:, fg * 256:(fg + 1) * 256].rearrange("(k p) f -> p k f", p=128)),
                        w=[("wgu", b, gi)], dma=True)
                for fc in range(2):
                    f = 2 * fg + fc
                    for T in tiles:
                        tsl = slice(T * 512, (T + 1) * 512)
                        tl = slice((T - 2 * half) * 512, (T - 2 * half + 1) * 512)
                        gb = rr["bank"] % 4
                        ub = 4 + rr["bank"] % 3
                        rr["bank"] += 1
                        for gi, bank in ((0, gb), (1, ub)):
                            for k in range(KC):
                                P.add("pe", lambda e, k=k, gi=gi, bank=bank, b=b, fc=fc, tsl=tsl: e.matmul(
                                    ps[bank][:], wgu[b][:, k, gi, fc * 128:(fc + 1) * 128], hT[:, k, tsl],
                                    start=(k == 0), stop=(k == KC - 1)),
                                    r=[("wgu", b, gi), hk(k, T)], w=[PS(bank)])
                        sb_ = rr["bank"] % 2
                        P.add("act", lambda e, gb=gb, sb_=sb_: e.activation(sgt[sb_][:], ps[gb][:], AF.Silu),
                              r=[PS(gb)], w=[("sgt", sb_)])
                        P.add("dve", lambda e, ub=ub, sb_=sb_, f=f, tl=tl: e.tensor_tensor(
                            out=aT[:, f, tl], in0=ps[ub][:], in1=sgt[sb_][:], op=ALU.mult),
                            r=[PS(ub), ("sgt", sb_)], w=[("a", f, T)])
            for T in tiles:
                tsl = slice(T * 512, (T + 1) * 512)
                tl = slice((T - 2 * half) * 512, (T - 2 * half + 1) * 512)
                for f in range(FCH):
                    b = rr.setdefault("wd", 0) % 4
                    rr["wd"] += 1
                    P.add("pool", lambda e, b=b, f=f: e.dma_start(out=wdb[b][:], in_=wd[l][f * 128:(f + 1) * 128, :]),
                          w=[("wdb", b)], dma=True)
                    for d in range(KC):
                        P.add("pe", lambda e, b=b, d=d, f=f, tl=tl: e.matmul(
                            ps[d][:], wdb[b][:, d * 128:(d + 1) * 128], aT[:, f, tl],
                            start=(f == 0), stop=(f == FCH - 1)),
                            r=[("wdb", b), ("a", f, T)], w=[PS(d)])
                for d in range(KC):
                    P.add("dve", lambda e, d=d, tsl=tsl: e.scalar_tensor_tensor(
                        out=xT[:, d, tsl], in0=ps[d][:], scalar=0.5, op0=ALU.mult, in1=xT[:, d, tsl], op1=ALU.add),
                        r=[PS(d), xk(d, T)], w=[xk(d, T)])

    def store_y(s):
        for T in range(4):
            tsl = slice(T * 512, (T + 1) * 512)
            rstd_for_tile(T, eps_t, 1.0 / D)
            for c in range(KC):
                P.add("dve", lambda e, c=c, tsl=tsl: e.scalar_tensor_tensor(
                    out=yTf[:, c, :], in0=xT[:, c, tsl], scalar=nwf[:, c:c + 1], op0=ALU.mult,
                    in1=rstd[:], op1=ALU.mult), r=[xk(c, T), "rstd"], w=[("yTf", c)])
            for j4 in range(4):
                j = 4 * T + j4
                b = j % 2
                for half in range(2):
                    bank = (2 * j + half) % 4
                    for i in range(4):
                        c = 4 * half + i
                        P.add("pe", lambda e, bank=bank, i=i, c=c, j4=j4: e.transpose(
                            ps[bank][:, i * 128:(i + 1) * 128], yTf[:, c, j4 * 128:(j4 + 1) * 128], ident_f[:]),
                            r=[("yTf", c), "ident_f"], w=[PS(bank)])
                    copy_op(evac_eng(), yo[b][:, half * 512:(half + 1) * 512], ps[bank][:],
                            r=[PS(bank)], w=[("yo", b, half)])
                P.add("sp", lambda e, b=b, j=j: e.dma_start(out=y[s, j * 128:(j + 1) * 128, :], in_=yo[b][:]),
                      r=[("yo", b, 0), ("yo", b, 1)], w=[("y", s, j)], dma=True)


    SLOPES = [2.0 ** (-8.0 * (h + 1) / 4) for h in range(4)]
    OQ, OK_, OV, OG = 2064, 2576, 3088, 3600
    o_daT = sb("o_daT", [128, 4, S], BF16, at=R_o)
    o_dnT = sb("o_dnT", [128, 4, S], BF16, at=R_o + 16384)
    a_o = R_in
    wqkv = []
    for b in range(2):
        wqkv.append(sb(f"wqkv{b}", [128, 3, KC, 128], BF16, at=a_o))
        a_o += 6144
    aqh = [sb(f"aqh{b}", [128, S], BF16, at=a_o + 4096 * b) for b in range(2)]
    a_o += 8192
    akh = [sb(f"akh{b}", [128, S], BF16, at=a_o + 4096 * b) for b in range(2)]
    a_o += 8192
    avh = [sb(f"avh{b}", [128, 16, 128], BF16, at=a_o + 4096 * b) for b in range(2)]
    a_o += 8192
    pT = [sb(f"pT{b}", [128, 512], BF16, at=a_o + 1024 * b) for b in range(4)]
    a_o += 4096
    tmpf = [sb(f"tmpf{b}", [128, 512], F32, at=a_o + 2048 * b) for b in range(3)]
    a_o += 6144
    assert a_o <= R_in + 48 * 1024
    f_o = R_free + 4096
    rz = sb("rz", [128, 512], F32, at=f_o)
    o0 = sb("o0", [128, 512], F32, at=f_o + 2048)
    t1 = sb("t1", [128, 512], F32, at=f_o + 4096)
    oc = sb("oc", [128, 512], F32, at=f_o + 6144)
    sqa = sb("sqa", [128, 512], F32, at=f_o + 8192)
    f_o += 10240
    assert f_o <= 229344, f_o

    def attention(l):
        for h in HEADS:
            b = h % 2
            for wi, o_col in enumerate((OQ, OK_, OV)):
                P.add("pool", lambda e, b=b, wi=wi, o_col=o_col, h=h: e.dma_start(
                    out=wqkv[b][:, wi, :, :],
                    in_=W["w_in"][l][:, o_col + h * 128:o_col + (h + 1) * 128].rearrange("(k p) f -> p k f", p=128)),
                    w=[("wqkv", b, wi)], dma=True)
            for wi, dst, scl in ((0, aqh[b], 0.125), (1, akh[b], 1.0)):
                for T in range(4):
                    tsl = slice(T * 512, (T + 1) * 512)
                    bank = rr["bank"] % 3
                    rr["bank"] += 1
                    for k in range(KC):
                        P.add("pe", lambda e, k=k, bank=bank, wi=wi, b=b, tsl=tsl: e.matmul(
                            ps[bank][:], wqkv[b][:, wi, k, :], hT[:, k, tsl], start=(k == 0), stop=(k == KC - 1)),
                            r=[("wqkv", b, wi), hk(k, T)], w=[PS(bank)])
                    P.add("act", lambda e, dst=dst, tsl=tsl, bank=bank, scl=scl: e.activation(
                        dst[:, tsl], ps[bank][:], AF.Copy, scale=scl), r=[PS(bank)], w=[("aqk", wi, b, T)])
            for g in range(4):
                bank = rr["bank"] % 3
                rr["bank"] += 1
                for jj in range(4):
                    j = 4 * g + jj
                    for k in range(KC):
                        P.add("pe", lambda e, k=k, bank=bank, jj=jj, j=j, b=b: e.matmul(
                            ps[bank][:, jj * 128:(jj + 1) * 128], hT[:, k, j * 128:(j + 1) * 128], wqkv[b][:, 2, k, :],
                            start=(k == 0), stop=(k == KC - 1)),
                            r=[("wqkv", b, 2), hk(k, j // 4)], w=[PS(bank)])
                P.add("dve", lambda e, bank=bank, g=g, b=b: e.tensor_copy(
                    avh[b][:, 4 * g:4 * g + 4, :], ps[bank][:].rearrange("p (a t) -> p a t", a=4)),
                    r=[PS(bank)], w=[("avh", b, g)])
            slope = SLOPES[h]
            for G in range(4):
                gsl = slice(G * 512, (G + 1) * 512)
                for m in range(2):
                    msl = slice(64 * m, 64 * m + 64)
                    ob, zb = 3 + m, 5 + m
                    plan = []
                    for j in range(16):
                        r_ = j - 4 * G
                        off_ = 512 * G - 128 * j
                        if 0 <= r_ <= 3:
                            src = abs_t[:, 384 - 128 * r_:896 - 128 * r_]
                            coef, cb = -slope, 0.0
                            dmin, dmax = 0, max(128 * r_ + 127, 511 - 128 * r_)
                        elif off_ > 0:
                            src = bs_t[:]
                            coef, cb = -slope, -slope * off_
                            dmin, dmax = off_ - 127, off_ + 511
                        else:
                            src = bs_t[:]
                            coef, cb = slope, slope * off_
                            dmin, dmax = -off_ - 511, -off_ + 127
                        if -slope * dmin < -80.0:
                            continue
                        plan.append((j, src, coef, cb, (-slope * dmax) < -80.0))
                    for (j, src, coef, cb, clamp) in plan:
                        first = (j == plan[0][0])
                        last = (j == plan[-1][0])
                        sbk = rr["bank"] % 3
                        rr["bank"] += 1
                        P.add("pe", lambda e, sbk=sbk, msl=msl, j=j, gsl=gsl, b=b: e.matmul(
                            ps[sbk][:], akh[b][msl, j * 128:(j + 1) * 128], aqh[b][msl, gsl], start=True, stop=True),
                            r=[("aqk", 0, b, G), ("aqk", 1, b, j // 4)], w=[PS(sbk)])
                        tb = rr["bank"] % 3
                        pb = rr["bank"] % 4
                        P.add("dve", lambda e, src=src, coef=coef, sbk=sbk, tb=tb: e.scalar_tensor_tensor(
                            out=tmpf[tb][:], in0=src, scalar=coef, op0=ALU.mult, in1=ps[sbk][:], op1=ALU.add),
                            r=[PS(sbk)], w=[("tmpf", tb)])
                        if clamp:
                            P.add("dve", lambda e, tb=tb, cb=cb: e.tensor_scalar(
                                out=tmpf[tb][:], in0=tmpf[tb][:], scalar1=-80.0 - cb, scalar2=None, op0=ALU.max),
                                r=[("tmpf", tb)], w=[("tmpf", tb)])
                        P.add("act", lambda e, tb=tb, pb=pb, cb=cb: e.activation(
                            pT[pb][:], tmpf[tb][:], AF.Exp, bias=cb), r=[("tmpf", tb)], w=[("pT", pb)])
                        P.add("pe", lambda e, ob=ob, j=j, pb=pb, b=b, first=first, last=last: e.matmul(
                            ps[ob][:], avh[b][:, j, :], pT[pb][:], start=first, stop=last),
                            r=[("avh", b, j // 4), ("pT", pb)], w=[PS(ob)])
                        P.add("pe", lambda e, zb=zb, pb=pb, first=first, last=last: e.matmul(
                            ps[zb][:], ones_b[:], pT[pb][:], start=first, stop=last),
                            r=[("pT", pb)], w=[PS(zb)])
                    P.add("dve", lambda e, zb=zb: e.reciprocal(rz[:], ps[zb][:]), r=[PS(zb)], w=["rz"])
                    if m == 0:
                        P.add("dve", lambda e, ob=ob: e.tensor_tensor(out=o0[:], in0=ps[ob][:], in1=rz[:], op=ALU.mult),
                              r=[PS(ob), "rz"], w=["o0"])
                    else:
                        P.add("dve", lambda e, ob=ob: e.tensor_tensor(out=t1[:], in0=ps[ob][:], in1=rz[:], op=ALU.mult),
                              r=[PS(ob), "rz"], w=["t1"])
                        P.add("dve", lambda e: e.scalar_tensor_tensor(
                            out=oc[:], in0=t1[:], scalar=neglam[:, l:l + 1], op0=ALU.mult, in1=o0[:], op1=ALU.add),
                            r=["t1", "o0", "neglam"], w=["oc"])
                P.add("act", lambda e: e.activation(sqa[:], oc[:], AF.Square), r=["oc"], w=["sqa"])
                P.add("pe", lambda e: e.matmul(ps[7][:], ones_f[:], sqa[:], start=True, stop=True),
                      r=["sqa"], w=[PS(7)])
                P.add("act", lambda e: e.activation(lnt[:], ps[7][:], AF.Ln, bias=eps5_t[:], scale=1.0 / 128),
                      r=[PS(7)], w=["lnt"])
                P.add("act", lambda e: e.activation(rstd[:], lnt[:], AF.Exp, scale=-0.5), r=["lnt"], w=["rstd"])
                P.add("dve", lambda e, h=h, gsl=gsl: e.scalar_tensor_tensor(
                    out=o_daT[:, h, gsl], in0=oc[:], scalar=sublnw[:, l:l + 1], op0=ALU.mult, in1=rstd[:], op1=ALU.mult),
                    r=["oc", "rstd", "sublnw"], w=[("oda", h, G)])

    mT = sb("mT", [128, KC, S], BF16, at=R_in)
    wbr = [sb(f"wbr{i}", [128, 4, D], BF16, at=R_in + 32768 + 8192 * i) for i in range(2)]
    m_o = R_free + 4096
    gw = [sb(f"gw{b}", [128, KC, 2, 128], BF16, at=m_o + 4096 * b) for b in range(2)]
    m_o += 8192
    sg_ = [sb(f"sg{b}", [128, 512], F32, at=m_o + 2048 * b) for b in range(4)]
    m_o += 8192
    assert m_o <= 229344, m_o

    def merge(l, use_dn, use_da):
        for i, nm in enumerate(("w_branch_dn", "w_branch_da")):
            P.add("pool", lambda e, i=i, nm=nm: e.dma_start(
                out=wbr[i][:], in_=W[nm][l].rearrange("(k p) f -> p k f", p=128)), w=[("wbr", i)], dma=True)
        for dc in range(KC):
            b = dc % 2
            for gi in range(2):
                P.add("pool", lambda e, b=b, gi=gi, dc=dc: e.dma_start(
                    out=gw[b][:, :, gi, :],
                    in_=W["w_in"][l][:, OG + gi * D + dc * 128:OG + gi * D + (dc + 1) * 128].rearrange(
                        "(k p) f -> p k f", p=128)), w=[("gw", b, gi)], dma=True)
            for T in range(4):
                tsl = slice(T * 512, (T + 1) * 512)
                base = 4 * (rr["bank"] % 2)
                rr["bank"] += 1
                for i, (src, key) in enumerate(((o_dnT, "odn"), (o_daT, "oda"))):
                    for hc in range(4):
                        P.add("pe", lambda e, i=i, hc=hc, src=src, base=base, dc=dc, tsl=tsl: e.matmul(
                            ps[base + i][:], wbr[i][:, hc, dc * 128:(dc + 1) * 128], src[:, hc, tsl],
                            start=(hc == 0), stop=(hc == 3)),
                            r=[("wbr", i), (key, hc, T)], w=[PS(base + i)])
                for gi in range(2):
                    for k in range(KC):
                        P.add("pe", lambda e, gi=gi, k=k, b=b, base=base, tsl=tsl: e.matmul(
                            ps[base + 2 + gi][:], gw[b][:, k, gi, :], hT[:, k, tsl], start=(k == 0), stop=(k == KC - 1)),
                            r=[("gw", b, gi), hk(k, T)], w=[PS(base + 2 + gi)])
                for gi in range(2):
                    P.add("act", lambda e, gi=gi, base=base: e.activation(sg_[gi][:], ps[base + 2 + gi][:], AF.Sigmoid),
                          r=[PS(base + 2 + gi)], w=[("sg", gi)])
                for gi in range(2):
                    P.add("dve", lambda e, gi=gi, base=base: e.tensor_tensor(
                        out=sg_[2 + gi][:], in0=ps[base + gi][:], in1=sg_[gi][:], op=ALU.mult),
                        r=[PS(base + gi), ("sg", gi)], w=[("sg", 2 + gi)])
                P.add("dve", lambda e, dc=dc, tsl=tsl: e.tensor_tensor(
                    out=mT[:, dc, tsl], in0=sg_[2][:], in1=sg_[3][:], op=ALU.add),
                    r=[("sg", 2), ("sg", 3)], w=[("m", dc, T)])
        for do in range(KC):
            b = do % 2
            P.add("pool", lambda e, b=b, do=do: e.dma_start(
                out=gw[b][:].rearrange("p k g f -> p k (g f)")[:, :, 0:128],
                in_=W["w_out"][l][:, do * 128:(do + 1) * 128].rearrange("(k p) f -> p k f", p=128)),
                w=[("gw", b, 0), ("gw", b, 1)], dma=True)
            for T in range(4):
                tsl = slice(T * 512, (T + 1) * 512)
                bank = rr["bank"] % 7
                rr["bank"] += 1
                for k in range(KC):
                    P.add("pe", lambda e, k=k, b=b, bank=bank, tsl=tsl: e.matmul(
                        ps[bank][:], gw[b][:].rearrange("p k g f -> p k (g f)")[:, k, 0:128], mT[:, k, tsl],
                        start=(k == 0), stop=(k == KC - 1)),
                        r=[("gw", b, 0), ("gw", b, 1), ("m", k, T)], w=[PS(bank)])
                P.add("dve", lambda e, do=do, tsl=tsl, bank=bank: e.tensor_tensor(
                    out=xT[:, do, tsl], in0=ps[bank][:], in1=xT[:, do, tsl], op=ALU.add),
                    r=[PS(bank), xk(do, T)], w=[xk(do, T)])


    qT = sb("qT", [128, 4, S], BF16, at=R_in)
    kT = sb("kT", [128, 4, S], BF16, at=R_in + 16384)
    vT = sb("vT", [128, 4, S], BF16, at=R_in + 32768)
    segs = [[R_o, R_o + 16384], [R_free + 4096, 229344]]

    def dalloc(segs_, name, shape, dtype):
        esz = 4 if dtype == F32 else 2
        nb = esz
        for s_ in shape[1:]:
            nb *= s_
        nb = ((nb + 31) // 32) * 32
        for sg in segs_:
            if sg[0] + nb <= sg[1]:
                t = nc.alloc_sbuf_tensor_at(name, list(shape), dtype, offset=sg[0])
                sg[0] += nb
                return t
        raise RuntimeError("dn scratch full: " + name)

    ba = dalloc(segs[1:], "dn_ba", [128, 16, 16], F32)
    ctail = segs[1][0]
    wc = [dalloc(segs, f"dn_wc{b}", [128, KC, 128], BF16) for b in range(2)]
    pc = dalloc(segs, "dn_pc", [128, S + 4], F32)
    cv = dalloc(segs, "dn_cv", [128, S], F32)
    wba = dalloc(segs, "dn_wba", [128, KC, 16], BF16)
    dsq = dalloc(segs, "dn_sq", [128, 512], F32)
    csegs = [[R_h, R_h + 32768], [R_o, R_o + 16384], [R_free, R_free + 4096], [ctail, 229344]]
    o_tok = dalloc(csegs, "dn_otok", [128, 16, 512], BF16)
    Sf = dalloc(csegs, "dn_Sf", [128, 8, 128], F32)
    Sb = dalloc(csegs, "dn_Sb", [128, 8, 128], BF16)
    mk_f = dalloc(csegs, "dn_mkf", [128, 2, 128], F32)
    mk4 = dalloc(csegs, "dn_mk4", [128, 4, 512], BF16)
    sc = {}
    for nm in ("beta", "nbeta", "g", "gc", "gtot", "egc", "negc", "ekd", "egl", "tmpa"):
        sc[nm] = dalloc(csegs, "dn_" + nm, [128, 16, 8], F32)
    abc = dalloc(csegs, "dn_abc", [128, 2, 8], F32)
    Wt = [[dalloc(csegs, f"dn_W{d}{b}", [128, 4, 128], BF16) for b in range(2)] for d in range(2)]
    aTt = [[dalloc(csegs, f"dn_aT{d}{b}", [128, 4, 128], BF16) for b in range(2)] for d in range(2)]
    kdt = [[dalloc(csegs, f"dn_kd{d}{b}", [128, 4, 128], BF16) for b in range(2)] for d in range(2)]
    vtk = [[dalloc(csegs, f"dn_vt{d}{b}", [128, 4, 128], BF16) for b in range(2)] for d in range(2)]
    Mt = [dalloc(csegs, f"dn_M{b}", [128, 4, 128], F32) for b in range(2)]
    Nt = [dalloc(csegs, f"dn_N{b}", [128, 4, 128], F32) for b in range(2)]
    Wf = dalloc(csegs, "dn_Wf", [128, 4, 128], F32)
    Gbc = dalloc(csegs, "dn_Gbc", [128, 4, 128], F32)
    Et = dalloc(csegs, "dn_Et", [128, 4, 128], F32)
    tinc = Et
    tstr = dalloc(csegs, "dn_tstr", [128, 4, 128], F32)
    r0t = dalloc(csegs, "dn_r0", [128, 4, 128], BF16)
    dlt = dalloc(csegs, "dn_dl", [128, 4, 128], BF16)
    qsg = Gbc
    otf = tstr
    junk = Gbc
    onb = r0t
    ssn = dalloc(csegs, "dn_ssn", [128, 8], F32)
    wz = [sb(f"dn_wz{b}", [128, KC, 128], BF16, at=R_o + 2048 * b) for b in range(2)]
    zt = [sb(f"dn_zt{b}", [128, 512], F32, at=R_o + 4096 + 2048 * b) for b in range(2)]

    def psb(b_):
        return ps[b_][:].bitcast(BF16)

    def nb_():
        rr["bank"] += 1
        return rr["bank"] % 8

    def deltanet(l):
        P.add("dve", lambda e: e.memset(pc[:, 0:2], 0.0), w=["pcpad"])
        P.add("dve", lambda e: e.memset(pc[:, S + 2:S + 4], 0.0), w=["pcpad"])
        for cc in range(12):
            wb = cc % 2
            h = cc % 4
            P.add("pool", lambda e, wb=wb, cc=cc: e.dma_start(
                out=wc[wb][:], in_=W["w_in"][l][:, cc * 128:(cc + 1) * 128].rearrange("(k p) f -> p k f", p=128)),
                w=[("dwc", wb)], dma=True)
            for T in range(4):
                tsl = slice(T * 512, (T + 1) * 512)
                bank = nb_() % 4
                for k in range(KC):
                    P.add("pe", lambda e, k=k, bank=bank, wb=wb, tsl=tsl: e.matmul(
                        ps[bank][:], wc[wb][:, k, :], hT[:, k, tsl], start=(k == 0), stop=(k == KC - 1)),
                        r=[("dwc", wb), hk(k, T)], w=[PS(bank)])
                P.add("act", lambda e, bank=bank, T=T: e.copy(pc[:, 2 + T * 512:2 + (T + 1) * 512], ps[bank][:]),
                      r=[PS(bank)], w=[("pc", T)])
            pck = [("pc", T) for T in range(4)] + ["pcpad"]
            P.add("dve", lambda e, cc=cc: e.tensor_scalar(out=cv[:], in0=pc[:, 0:S], scalar1=convw[:, l, cc, 0:1],
                                                         scalar2=None, op0=ALU.mult), r=pck, w=["cv"])
            for kk in range(1, 5):
                P.add("dve", lambda e, cc=cc, kk=kk: e.scalar_tensor_tensor(
                    out=cv[:], in0=pc[:, kk:kk + S], scalar=convw[:, l, cc, kk:kk + 1], op0=ALU.mult,
                    in1=cv[:], op1=ALU.add), r=pck + ["cv"], w=["cv"])
            if cc >= 8:
                P.add("act", lambda e, h=h: e.activation(vT[:, h, :], cv[:], AF.Silu), r=["cv"],
                      w=[("vT", h, T) for T in range(4)])
            else:
                dst = qT if cc < 4 else kT
                nm = "qT" if cc < 4 else "kT"
                scl = (128.0 ** -0.5) if cc < 4 else 1.0
                P.add("act", lambda e: e.activation(cv[:], cv[:], AF.Silu), r=["cv"], w=["cv"])
                for T in range(4):
                    tsl = slice(T * 512, (T + 1) * 512)
                    P.add("act", lambda e, tsl=tsl: e.activation(dsq[:], cv[:, tsl], AF.Square), r=["cv"], w=["dsq"])
                    P.add("pe", lambda e: e.matmul(ps[7][:], ones_f[:], dsq[:], start=True, stop=True),
                          r=["dsq"], w=[PS(7)])
                    P.add("act", lambda e: e.activation(lnt[:], ps[7][:], AF.Ln, bias=eps_t[:], scale=1.0),
                          r=[PS(7)], w=["lnt"])
                    P.add("act", lambda e: e.activation(rstd[:], lnt[:], AF.Exp, scale=-0.5), r=["lnt"], w=["rstd"])
                    P.add("dve", lambda e, dst=dst, h=h, tsl=tsl, scl=scl: e.scalar_tensor_tensor(
                        out=dst[:, h, tsl], in0=cv[:, tsl], scalar=scl, op0=ALU.mult, in1=rstd[:], op1=ALU.mult),
                        r=["cv", "rstd"], w=[(nm, h, T)])
        P.add("pool", lambda e: e.dma_start(out=wba[:], in_=W["w_in"][l][:, 2048:2064].rearrange("(k p) f -> p k f", p=128)),
              w=["wba"], dma=True)
        for j in range(16):
            for k in range(KC):
                P.add("pe", lambda e, j=j, k=k: e.matmul(ps[3][:, j * 16:(j + 1) * 16], hT[:, k, j * 128:(j + 1) * 128],
                                                        wba[:, k, :], start=(k == 0), stop=(k == KC - 1)),
                      r=["wba", hk(k, j // 4)], w=[PS(3)])
        P.add("dve", lambda e: e.tensor_copy(ba[:].rearrange("p j c -> p (j c)"), ps[3][:, 0:256]), r=[PS(3)], w=["ba"])
        P.barrier(bar[:])
        P.add("sp", lambda e: e.dma_start(out=mk_f[:, 0, :], in_=c_masks[0]), w=["mk_f"], dma=True)
        P.add("sp", lambda e: e.dma_start(out=mk_f[:, 1, :], in_=c_masks[2]), w=["mk_f"], dma=True)
        for mi in range(4):
            for h in range(4):
                P.add("pool", lambda e, mi=mi, h=h: e.dma_start(out=mk4[:, mi, h * 128:(h + 1) * 128], in_=c_masks[mi]),
                      w=["mk4"], dma=True)
        P.add("sp", lambda e: e.dma_start(out=abc[:, 0, :], in_=W["dn_a_log"][l].rearrange("a b -> (a b)").partition_broadcast(128)),
              w=["abc"], dma=True)
        P.add("sp", lambda e: e.dma_start(out=abc[:, 1, :], in_=W["dn_dt_bias"][l].rearrange("a b -> (a b)").partition_broadcast(128)),
              w=["abc"], dma=True)
        P.add("act", lambda e: e.activation(abc[:, 0, :], abc[:, 0, :], AF.Exp), r=["abc"], w=["abc"])
        P.add("dve", lambda e: e.tensor_scalar(out=abc[:, 0, :], in0=abc[:, 0, :], scalar1=-1.0, scalar2=None, op0=ALU.mult),
              r=["abc"], w=["abc"])
        for j in range(16):
            P.add("dve", lambda e, j=j: e.tensor_tensor(out=sc["tmpa"][:, j, :], in0=ba[:, j, 8:16], in1=abc[:, 1, :], op=ALU.add),
                  r=["ba", "abc"], w=["tmpa"])
        P.add("act", lambda e: e.activation(sc["tmpa"][:], sc["tmpa"][:], AF.Exp), r=["tmpa"], w=["tmpa"])
        P.add("act", lambda e: e.activation(sc["tmpa"][:], sc["tmpa"][:], AF.Ln, bias=1.0), r=["tmpa"], w=["tmpa"])
        for j in range(16):
            P.add("dve", lambda e, j=j: e.tensor_tensor(out=sc["g"][:, j, :], in0=sc["tmpa"][:, j, :], in1=abc[:, 0, :], op=ALU.mult),
                  r=["tmpa", "abc"], w=["g"])
        for j in range(16):
            P.add("act", lambda e, j=j: e.activation(sc["beta"][:, j, :], ba[:, j, 0:8], AF.Exp, scale=-1.0), r=["ba"], w=["beta"])
        P.add("dve", lambda e: e.tensor_scalar(out=sc["beta"][:], in0=sc["beta"][:], scalar1=1.0, scalar2=None, op0=ALU.add),
              r=["beta"], w=["beta"])
        P.add("dve", lambda e: e.reciprocal(sc["beta"][:], sc["beta"][:]), r=["beta"], w=["beta"])
        P.add("dve", lambda e: e.tensor_scalar(out=sc["nbeta"][:], in0=sc["beta"][:], scalar1=-1.0, scalar2=None, op0=ALU.mult),
              r=["beta"], w=["nbeta"])
        for j in range(16):
            P.add("pe", lambda e, j=j: e.matmul(ps[3][:, j * 8:j * 8 + 4], mk_f[:, 0, :], sc["g"][:, j, 0:4], start=True, stop=True),
                  r=["g", "mk_f"], w=[PS(3)])
            P.add("pe", lambda e, j=j: e.matmul(ps[3][:, j * 8 + 4:j * 8 + 8], mk_f[:, 1, :], sc["g"][:, j, 4:8], start=True, stop=True),
                  r=["g", "mk_f"], w=[PS(3)])
            P.add("pe", lambda e, j=j: e.matmul(ps[4][:, j * 8:j * 8 + 8], ones_f[:], sc["g"][:, j, :], start=True, stop=True),
                  r=["g"], w=[PS(4)])
        P.add("dve", lambda e: e.tensor_copy(sc["gc"][:].rearrange("p j c -> p (j c)"), ps[3][:, 0:128]), r=[PS(3)], w=["gc"])
        P.add("dve", lambda e: e.tensor_copy(sc["gtot"][:].rearrange("p j c -> p (j c)"), ps[4][:, 0:128]), r=[PS(4)], w=["gtot"])
        P.add("act", lambda e: e.activation(sc["egc"][:], sc["gc"][:], AF.Exp), r=["gc"], w=["egc"])
        P.add("dve", lambda e: e.tensor_scalar(out=sc["negc"][:], in0=sc["egc"][:], scalar1=-1.0, scalar2=None, op0=ALU.mult),
              r=["egc"], w=["negc"])
        P.add("dve", lambda e: e.tensor_tensor(out=sc["ekd"][:], in0=sc["gtot"][:], in1=sc["gc"][:], op=ALU.subtract),
              r=["gtot", "gc"], w=["ekd"])
        P.add("act", lambda e: e.activation(sc["ekd"][:], sc["ekd"][:], AF.Exp), r=["ekd"], w=["ekd"])
        P.add("act", lambda e: e.activation(sc["egl"][:], sc["gtot"][:], AF.Exp), r=["gtot"], w=["egl"])
        P.add("dve", lambda e: e.memset(Sf[:], 0.0), w=[("Sf", d) for d in range(2)])
        P.add("dve", lambda e: e.memset(Sb[:], 0.0), w=[("Sb", d) for d in range(2)])

        def pre(dr, j, bf):
            jsl = slice(j * 128, (j + 1) * 128)
            Tj = j // 4
            inc4 = mk4[:, 0 + 2 * dr, :]
            str4 = mk4[:, 1 + 2 * dr, :]
            bk = nb_()
            for h in range(4):
                P.add("pe", lambda e, h=h, bk=bk: e.transpose(psb(bk)[:, h * 128:(h + 1) * 128], kT[:, h, jsl], ident_b[:]),
                      r=[("kT", h, Tj)], w=[PS(bk)])
            for h in range(4):
                P.add("act", lambda e, h=h, bk=bk: e.activation(kdt[dr][bf][:, h, :], psb(bk)[:, h * 128:(h + 1) * 128], AF.Copy,
                                                              scale=sc["ekd"][:, j, dr * 4 + h:dr * 4 + h + 1]),
                      r=[PS(bk), "ekd"], w=[("kd", dr, bf)])
            bv = nb_()
            for h in range(4):
                P.add("pe", lambda e, h=h, bv=bv: e.transpose(psb(bv)[:, h * 128:(h + 1) * 128], vT[:, h, jsl], ident_b[:]),
                      r=[("vT", h, Tj)], w=[PS(bv)])
            P.add("dve", lambda e, bv=bv: e.tensor_copy(vtk[dr][bf][:].rearrange("p h e -> p (h e)"), psb(bv)[:, 0:512]),
                  r=[PS(bv)], w=[("vt", dr, bf)])
            for h in range(4):
                P.add("dve", lambda e, h=h: e.tensor_scalar(out=Gbc[:, h, :], in0=ones_f[:], scalar1=sc["g"][:, j, dr * 4 + h:dr * 4 + h + 1],
                                                          scalar2=None, op0=ALU.mult), r=["g"], w=["Gbc"])
            bg = nb_()
            for h in range(4):
                P.add("pe", lambda e, h=h, bg=bg: e.matmul(ps[bg][:, h * 128:(h + 1) * 128], Gbc[:, h, :], mk_f[:, dr, :],
                                                          start=True, stop=True), r=["Gbc", "mk_f"], w=[PS(bg)])
            for h in range(4):
                P.add("dve", lambda e, h=h, bg=bg: e.tensor_scalar(
                    out=Et[:, h, :], in0=ps[bg][:, h * 128:(h + 1) * 128], scalar1=sc["gc"][:, j, dr * 4 + h:dr * 4 + h + 1],
                    scalar2=0.0, op0=ALU.subtract, op1=ALU.min), r=[PS(bg), "gc"], w=["Et"])
            P.add("act", lambda e: e.activation(Et[:], Et[:], AF.Exp), r=["Et"], w=["Et"])
            Etf = Et[:].rearrange("p h c -> p (h c)")
            P.add("dve", lambda e: e.tensor_tensor(out=tstr[:].rearrange("p h c -> p (h c)"), in0=Etf, in1=str4, op=ALU.mult),
                  r=["Et", "mk4"], w=["tstr"])
            P.add("dve", lambda e: e.tensor_tensor(out=Etf, in0=Etf, in1=inc4, op=ALU.mult),
                  r=["Et", "mk4"], w=["Et"])
            bkk, bkq = nb_(), nb_()
            for h in range(4):
                P.add("pe", lambda e, h=h, bkk=bkk: e.matmul(ps[bkk][:, h * 128:(h + 1) * 128], kT[:, h, jsl], kT[:, h, jsl],
                                                            start=True, stop=True), r=[("kT", h, Tj)], w=[PS(bkk)])
            for h in range(4):
                P.add("pe", lambda e, h=h, bkq=bkq: e.matmul(ps[bkq][:, h * 128:(h + 1) * 128], kT[:, h, jsl], qT[:, h, jsl],
                                                            start=True, stop=True), r=[("kT", h, Tj), ("qT", h, Tj)], w=[PS(bkq)])
            P.add("dve", lambda e, bkq=bkq: e.tensor_tensor(out=aTt[dr][bf][:].rearrange("p h c -> p (h c)"), in0=ps[bkq][:],
                                                          in1=tinc[:].rearrange("p h c -> p (h c)"), op=ALU.mult),
                  r=[PS(bkq), "Et"], w=[("aT", dr, bf)])
            for h in range(4):
                P.add("dve", lambda e, h=h, bkk=bkk: e.scalar_tensor_tensor(
                    out=Mt[0][:, h, :], in0=ps[bkk][:, h * 128:(h + 1) * 128], scalar=sc["nbeta"][:, j, dr * 4 + h:dr * 4 + h + 1],
                    op0=ALU.mult, in1=tstr[:, h, :], op1=ALU.mult), r=[PS(bkk), "tstr", "nbeta"], w=[("M", 0)])
            bn = nb_()
            for h in range(4):
                P.add("pe", lambda e, h=h, bn=bn: e.transpose(ps[bn][:, h * 128:(h + 1) * 128], Mt[0][:, h, :], ident_f[:]),
                      r=[("M", 0)], w=[PS(bn)])
            P.add("act", lambda e, bn=bn: e.copy(Nt[0][:].rearrange("p h c -> p (h c)"), ps[bn][:]), r=[PS(bn)], w=[("N", 0)])
            Wc = Wf
            for h in range(4):
                P.add("dve", lambda e, h=h: e.tensor_tensor(out=Wc[:, h, :], in0=Mt[0][:, h, :], in1=ident_f[:], op=ALU.add),
                      r=[("M", 0)], w=["Wf"])
            cur = 0
            for lev in range(6):
                nxt = 1 - cur
                if lev < 5:
                    bx = nb_()
                    for h in range(4):
                        P.add("pe", lambda e, h=h, bx=bx, cur=cur: e.matmul(ps[bx][:, h * 128:(h + 1) * 128], Nt[cur][:, h, :], Mt[cur][:, h, :],
                                                                          start=True, stop=True), r=[("M", cur), ("N", cur)], w=[PS(bx)])
                by = nb_()
                for h in range(4):
                    P.add("pe", lambda e, h=h, by=by, cur=cur: e.matmul(ps[by][:, h * 128:(h + 1) * 128], Mt[cur][:, h, :], Nt[cur][:, h, :],
                                                                      start=True, stop=True), r=[("M", cur), ("N", cur)], w=[PS(by)])
                if lev < 5:
                    P.add("dve", lambda e, bx=bx, nxt=nxt: e.tensor_copy(Mt[nxt][:].rearrange("p h c -> p (h c)"), ps[bx][:]),
                          r=[PS(bx)], w=[("M", nxt)])
                P.add("act", lambda e, by=by, nxt=nxt: e.copy(Nt[nxt][:].rearrange("p h c -> p (h c)"), ps[by][:]),
                      r=[PS(by)], w=[("N", nxt)])
                bz = nb_()
                for h in range(4):
                    P.add("pe", lambda e, h=h, bz=bz, nxt=nxt: e.matmul(ps[bz][:, h * 128:(h + 1) * 128], Nt[nxt][:, h, :], Wc[:, h, :],
                                                                      start=True, stop=True), r=[("N", nxt), "Wf"], w=[PS(bz)])
                P.add("dve", lambda e, bz=bz: e.tensor_tensor(out=Wc[:].rearrange("p h c -> p (h c)"), in0=ps[bz][:],
                                                            in1=Wc[:].rearrange("p h c -> p (h c)"), op=ALU.add),
                      r=[PS(bz), "Wf"], w=["Wf"])
                cur = nxt
            P.add("act", lambda e: e.copy(Wt[dr][bf][:], Wf[:]), r=["Wf"], w=[("W", dr, bf)])

        def chain(dr, j, bf, second):
            jsl = slice(j * 128, (j + 1) * 128)
            Tj = j // 4
            ba_, bb_ = nb_(), nb_()
            for h in range(4):
                P.add("pe", lambda e, h=h, ba_=ba_: e.matmul(ps[ba_][:, h * 128:(h + 1) * 128], kT[:, h, jsl], Sb[:, dr * 4 + h, :],
                                                            start=True, stop=True), r=[("kT", h, Tj), ("Sb", dr)], w=[PS(ba_)])
            for h in range(4):
                P.add("pe", lambda e, h=h, bb_=bb_: e.matmul(ps[bb_][:, h * 128:(h + 1) * 128], qT[:, h, jsl], Sb[:, dr * 4 + h, :],
                                                            start=True, stop=True), r=[("qT", h, Tj), ("Sb", dr)], w=[PS(bb_)])
            for h in range(4):
                P.add("dve", lambda e, h=h, ba_=ba_: e.scalar_tensor_tensor(
                    out=r0t[:, h, :], in0=ps[ba_][:, h * 128:(h + 1) * 128], scalar=sc["negc"][:, j, dr * 4 + h:dr * 4 + h + 1],
                    op0=ALU.mult, in1=vtk[dr][bf][:, h, :], op1=ALU.add), r=[PS(ba_), ("vt", dr, bf), "negc"], w=["r0"])
            bc_ = nb_()
            for h in range(4):
                P.add("pe", lambda e, h=h, bc_=bc_: e.matmul(ps[bc_][:, h * 128:(h + 1) * 128], Wt[dr][bf][:, h, :], r0t[:, h, :],
                                                            start=True, stop=True), r=[("W", dr, bf), "r0"], w=[PS(bc_)])
            for h in range(4):
                P.add("act", lambda e, h=h, bc_=bc_: e.activation(dlt[:, h, :], ps[bc_][:, h * 128:(h + 1) * 128], AF.Copy,
                                                                scale=sc["beta"][:, j, dr * 4 + h:dr * 4 + h + 1]),
                      r=[PS(bc_), "beta"], w=["dl"])
            bd_, be_ = nb_(), nb_()
            for h in range(4):
                P.add("pe", lambda e, h=h, bd_=bd_: e.matmul(ps[bd_][:, h * 128:(h + 1) * 128], kdt[dr][bf][:, h, :], dlt[:, h, :],
                                                            start=True, stop=True), r=[("kd", dr, bf), "dl"], w=[PS(bd_)])
            for h in range(4):
                P.add("pe", lambda e, h=h, be_=be_: e.matmul(ps[be_][:, h * 128:(h + 1) * 128], aTt[dr][bf][:, h, :], dlt[:, h, :],
                                                            start=True, stop=True), r=[("aT", dr, bf), "dl"], w=[PS(be_)])
            for h in range(4):
                P.add("act", lambda e, h=h, bb_=bb_: e.activation(qsg[:, h, :], ps[bb_][:, h * 128:(h + 1) * 128], AF.Copy,
                                                                scale=sc["egc"][:, j, dr * 4 + h:dr * 4 + h + 1]),
                      r=[PS(bb_), "egc"], w=["Gbc"])
            if not second:
                P.add("dve", lambda e, be_=be_: e.tensor_tensor(out=o_tok[:, j, :], in0=ps[be_][:], in1=qsg[:].rearrange("p h c -> p (h c)"),
                                                              op=ALU.add), r=[PS(be_), "Gbc"], w=[("otok", j)])
            else:
                P.add("dve", lambda e, be_=be_: e.tensor_tensor(out=otf[:].rearrange("p h c -> p (h c)"), in0=ps[be_][:],
                                                              in1=qsg[:].rearrange("p h c -> p (h c)"), op=ALU.add),
                      r=[PS(be_), "Gbc"], w=["tstr"])
                P.add("dve", lambda e: e.tensor_tensor(out=otf[:].rearrange("p h c -> p (h c)"), in0=otf[:].rearrange("p h c -> p (h c)"),
                                                     in1=o_tok[:, j, :], op=ALU.add), r=["tstr", ("otok", j)], w=["tstr"])
            for h in range(4):
                P.add("dve", lambda e, h=h, bd_=bd_: e.scalar_tensor_tensor(
                    out=Sf[:, dr * 4 + h, :], in0=Sf[:, dr * 4 + h, :], scalar=sc["egl"][:, j, dr * 4 + h:dr * 4 + h + 1],
                    op0=ALU.mult, in1=ps[bd_][:, h * 128:(h + 1) * 128], op1=ALU.add), r=[PS(bd_), ("Sf", dr), "egl"], w=[("Sf", dr)])
            P.add("act", lambda e: e.copy(Sb[:, dr * 4:dr * 4 + 4, :], Sf[:, dr * 4:dr * 4 + 4, :]), r=[("Sf", dr)], w=[("Sb", dr)])
            if second:
                for h in range(4):
                    P.add("act", lambda e, h=h: e.activation(junk[:, h, :], otf[:, h, :], AF.Square, accum_out=ssn[:, h:h + 1]),
                          r=["tstr"], w=["Gbc", "ssn"])
                P.add("act", lambda e: e.activation(ssn[:, 4:8], ssn[:, 0:4], AF.Ln, bias=eps_t[:], scale=1.0 / 128),
                      r=["ssn"], w=["ssn"])
                P.add("act", lambda e: e.activation(ssn[:, 4:8], ssn[:, 4:8], AF.Exp, scale=-0.5), r=["ssn"], w=["ssn"])
                for h in range(4):
                    P.add("dve", lambda e, h=h: e.tensor_scalar(out=onb[:, h, :], in0=otf[:, h, :], scalar1=ssn[:, 4 + h:5 + h],
                                                              scalar2=None, op0=ALU.mult), r=["tstr", "ssn"], w=["r0"])
                bt = nb_()
                for h in range(4):
                    P.add("pe", lambda e, h=h, bt=bt: e.transpose(psb(bt)[:, h * 128:(h + 1) * 128], onb[:, h, :], ident_b[:]),
                          r=["r0"], w=[PS(bt)])
                P.add("dve", lambda e, bt=bt: e.tensor_scalar(
                    out=o_dnT[:, :, jsl], in0=psb(bt)[:, 0:512].rearrange("p (h c) -> p h c", h=4), scalar1=wdn[:, l:l + 1],
                    scalar2=None, op0=ALU.mult), r=[PS(bt), "wdn"], w=[("odn", h, Tj) for h in range(4)])

        for step in range(17):
            if step < 16:
                pre(0, step, step % 2)
                pre(1, 15 - step, step % 2)
            if step > 0:
                st = step - 1
                chain(0, st, st % 2, st >= 8)
                chain(1, 15 - st, st % 2, st >= 8)
        if l == 0:
            dump("qT", qT[:].rearrange("p h s -> p (h s)")); dump("kT", kT[:].rearrange("p h s -> p (h s)"))
            dump("vT", vT[:].rearrange("p h s -> p (h s)"))
            for nm_ in ("beta", "g", "gc", "gtot", "egc", "ekd", "egl"):
                dump("sc_" + nm_, sc[nm_][:].rearrange("p j c -> p (j c)"))
            dump("ba", ba[:].rearrange("p j c -> p (j c)"))
            dump("W00", Wt[0][0][:].rearrange("p h c -> p (h c)")); dump("aT00", aTt[0][0][:].rearrange("p h c -> p (h c)"))
            dump("kd00", kdt[0][0][:].rearrange("p h c -> p (h c)")); dump("Sf", Sf[:].rearrange("p h c -> p (h c)"))
            dump("otok", o_tok[:].rearrange("p j c -> p (j c)")); dump("odnT", o_dnT[:].rearrange("p h s -> p (h s)"))
            dump("Et", Et[:].rearrange("p h c -> p (h c)")); dump("M0", Mt[0][:].rearrange("p h c -> p (h c)"))
        P.barrier(bar[:])
        for T in range(4):
            norm_to_h(T, nwm[:, l, :], sqm)
        for h in range(4):
            b = h % 2
            P.add("pool", lambda e, b=b, h=h: e.dma_start(
                out=wz[b][:], in_=W["w_in"][l][:, 1536 + h * 128:1536 + (h + 1) * 128].rearrange("(k p) f -> p k f", p=128)),
                w=[("wz", b)], dma=True)
            for T in range(4):
                tsl = slice(T * 512, (T + 1) * 512)
                bank = nb_() % 4
                zb_ = nb_() % 2
                for k in range(KC):
                    P.add("pe", lambda e, k=k, bank=bank, b=b, tsl=tsl: e.matmul(
                        ps[bank][:], wz[b][:, k, :], hT[:, k, tsl], start=(k == 0), stop=(k == KC - 1)),
                        r=[("wz", b), hk(k, T)], w=[PS(bank)])
                P.add("act", lambda e, bank=bank, zb_=zb_: e.activation(zt[zb_][:], ps[bank][:], AF.Silu), r=[PS(bank)], w=[("zt", zb_)])
                P.add("dve", lambda e, h=h, tsl=tsl, zb_=zb_: e.tensor_tensor(out=o_dnT[:, h, tsl], in0=o_dnT[:, h, tsl], in1=zt[zb_][:],
                                                                           op=ALU.mult), r=[("zt", zb_), ("odn", h, T)], w=[("odn", h, T)])

    def mixer(l):
        for T in range(4):
            norm_to_h(T, nwm[:, l, :], sqm)
        P.barrier(bar[:])
        if do_dn:
            deltanet(l)
            if l == 0:
                dump("odnT2", o_dnT[:].rearrange("p h s -> p (h s)")); dump("zt0", zt[0][:]); dump("hT2", hT[:].rearrange("p k s -> p (k s)"))
        else:
            for h in range(4):
                for T in range(4):
                    P.add("dve", lambda e, h=h, T=T: e.memset(o_dnT[:, h, T * 512:(T + 1) * 512], 0.0),
                          w=[("odn", h, T)])
        P.barrier(bar[:])
        if do_da:
            attention(l)
        else:
            for h in range(4):
                for T in range(4):
                    P.add("dve", lambda e, h=h, T=T: e.memset(o_daT[:, h, T * 512:(T + 1) * 512], 0.0),
                          w=[("oda", h, T)])
        P.barrier(bar[:])
        if l == 0:
            dump("wq", wqkv[0][:, 0, :, :].rearrange("p k f -> p (k f)")); dump("wv", wqkv[0][:, 2, :, :].rearrange("p k f -> p (k f)"))
            dump("aq0", aqh[0][:]); dump("ak0", akh[0][:]); dump("av0", avh[0][:].rearrange("p a b -> p (a b)"))
            dump("pT0", pT[0][:]); dump("tmpf0", tmpf[0][:]); dump("rz", rz[:]); dump("o0", o0[:]); dump("oc", oc[:])
            dump("t1", t1[:]); dump("rstd", rstd[:]); dump("odaT", o_daT[:].rearrange("p h s -> p (h s)"))
            dump("hT", hT[:].rearrange("p k s -> p (k s)")); dump("neglam", neglam[:]); dump("sublnw", sublnw[:])
            P.barrier(bar[:])
        merge(l, do_dn, do_da)
        P.barrier(bar[:])

    for s in range(nseq):
        load_x(s)
        P.barrier(bar[:])
        for l in range(depth):
            if do_ffn:
                ffn(l, nw1, W["ffn1_wg"], W["ffn1_wu"], W["ffn1_wd"])
                P.barrier(bar[:])
            if do_mix:
                mixer(l)
            if do_ffn:
                ffn(l, nw2, W["ffn2_wg"], W["ffn2_wu"], W["ffn2_wd"])
                P.barrier(bar[:])
        store_y(s)

    P.emit(same_engine_sync=same_engine_sync)
    return nc, len(P.ops)


def make_consts():
    k = np.arange(128, dtype=np.float32)[:, None]
    q = np.arange(512, dtype=np.float32)[None, :]
    jj = np.arange(896, dtype=np.float32)[None, :]
    i = np.arange(128)
    incu = (i[:, None] <= i[None, :]).astype(np.float32)
    stru = (i[:, None] < i[None, :]).astype(np.float32)
    masks = np.stack([incu, stru, incu.T.copy(), stru.T.copy()]).astype(np.float32)
    return {"c_ident": np.eye(128, dtype=np.float32), "c_bs": np.ascontiguousarray(q - k),
            "c_abs": np.ascontiguousarray(np.abs(jj - 384.0 - k)), "c_masks": masks}


_CACHE = {}


def kernel(**inputs):
    xs = np.concatenate([np.asarray(inputs["x_prompt"], np.float32), np.asarray(inputs["x_sample"], np.float32)], axis=0)
    nb_p = inputs["x_prompt"].shape[0]
    if "nc" not in _CACHE:
        _CACHE["nc"] = build()[0]
    nc = _CACHE["nc"]
    wmap = {name: np.ascontiguousarray(np.asarray(inputs[name], np.float32)) for name, _ in WSHAPES}
    consts = make_consts()
    in_maps = []
    for c in range(NCORES):
        m = {"x": np.ascontiguousarray(xs[c * NSEQ:(c + 1) * NSEQ])}
        m.update(wmap)
        m.update(consts)
        in_maps.append(m)
    res = run_bass_kernel_spmd(nc, in_maps, core_ids=list(range(NCORES)))
    yfull = np.concatenate([np.asarray(r["y"], np.float32) for r in res.results], axis=0)
    return (np.ascontiguousarray(yfull[:nb_p]), np.ascontiguousarray(yfull[nb_p:]))
```

```python
import math
import numpy as np
import concourse.bass as bass
import concourse.mybir as mybir
from concourse.bass_utils import run_bass_kernel_spmd

F32 = mybir.dt.float32
BF16 = mybir.dt.bfloat16
AF = mybir.ActivationFunctionType
ALU = mybir.AluOpType

D = 1024
S = 2048
DFF = 2816
NIN = 5648
DEPTH = 4
NCORES = 8
NSEQ = 5
KC = 8
FCH = 22
NDSLOT = 8
HEADS = [0, 1, 2, 3]
ATT_LA, ATT_ED = 2, 2

WSHAPES = [
    ("ffn1_norm", [DEPTH, D]), ("ffn1_wg", [DEPTH, D, DFF]), ("ffn1_wu", [DEPTH, D, DFF]),
    ("ffn1_wd", [DEPTH, DFF, D]), ("mix_norm", [DEPTH, D]), ("w_in", [DEPTH, D, NIN]),
    ("conv_w", [DEPTH, 5, 1536]), ("dn_a_log", [DEPTH, 2, 4]), ("dn_dt_bias", [DEPTH, 2, 4]),
    ("dn_out_norm", [DEPTH, 128]), ("diff_lambda", [DEPTH, 4, 64]), ("diff_subln", [DEPTH, 128]),
    ("w_branch_dn", [DEPTH, 512, D]), ("w_branch_da", [DEPTH, 512, D]), ("w_out", [DEPTH, D, D]),
    ("ffn2_norm", [DEPTH, D]), ("ffn2_wg", [DEPTH, D, DFF]), ("ffn2_wu", [DEPTH, D, DFF]),
    ("ffn2_wd", [DEPTH, DFF, D]), ("final_norm", [D]),
]


class Prog:
    def __init__(self, nc):
        self.nc = nc
        self.ops = []
        self.keys = set()

    def add(self, eng, fn, r=(), w=(), dma=False):
        self.ops.append((eng, fn, tuple(r), tuple(w), dma))
        self.keys.update(r)
        self.keys.update(w)

    def barrier(self, ap):
        self.add("dve", lambda e: e.memset(ap, 0.0), w=list(self.keys) + ["__BAR__"])

    def emit(self, same_engine_sync=True):
        nc = self.nc
        ops = self.ops
        n = len(ops)
        last_w = {}
        rd = {}
        deps = [None] * n
        needed = [False] * n
        cur_bar = None
        for i, (eng, fn, r, w, dma) in enumerate(ops):
            d = set()
            for k in r:
                j = last_w.get(k, cur_bar)
                if j is not None:
                    d.add(j)
            for k in w:
                j = last_w.get(k, cur_bar)
                if j is not None:
                    d.add(j)
                rr = rd.get(k)
                if rr:
                    d.update(rr.values())
            d.discard(i)
            dd = []
            for j in d:
                je = ops[j][0]
                jd = ops[j][4]
                if not jd and not dma and je == eng:
                    if eng == "pe" or not same_engine_sync:
                        continue
                dd.append(j)
            deps[i] = dd
            for j in dd:
                needed[j] = True
            tag = ("d", i) if dma else eng
            for k in r:
                rd.setdefault(k, {})[tag] = i
            for k in w:
                last_w[k] = i
                rd[k] = {}
            if "__BAR__" in w:
                cur_bar = i
                last_w = {}
                rd = {}
        cnt = {"pe": 0, "dve": 0, "act": 0, "pool": 0}
        dcnt = {"sp": 0, "pool": 0, "act": 0}
        sig = [None] * n
        dslot = [None] * n
        for i, (eng, fn, r, w, dma) in enumerate(ops):
            if dma:
                k = dcnt[eng]
                dcnt[eng] += 1
                slot = k % NDSLOT
                sig[i] = (("dma", eng, slot), 16 * (k // NDSLOT + 1))
                dslot[i] = (slot, 16 * (k // NDSLOT))
            elif needed[i]:
                cnt[eng] += 1
                sig[i] = (("c", eng), cnt[eng])
        per_eng = {"pe": [], "dve": [], "act": [], "pool": [], "sp": []}
        for i, op in enumerate(ops):
            per_eng[op[0]].append(i)
        semkeys = [("c", e) for e in ("pe", "dve", "act", "pool")]
        for q in ("sp", "pool", "act"):
            if dcnt[q]:
                semkeys += [("dma", q, s) for s in range(NDSLOT)]
        from contextlib import ExitStack
        with ExitStack() as es:
            sems = {}
            for sk in semkeys:
                sems[sk] = es.enter_context(nc.semaphore("s_" + "_".join(str(t) for t in sk)))
            block = es.enter_context(nc.Block())

            def run_engine(ename, e):
                waited = {}
                final = {}
                for i in per_eng[ename]:
                    eng, fn, r, w, dma = ops[i]
                    if dma:
                        slot, prev = dslot[i]
                        sk = ("dma", eng, slot)
                        if prev > 0 and waited.get(sk, 0) < prev:
                            e.wait_ge(sems[sk], prev)
                            waited[sk] = prev
                    need = {}
                    for j in deps[i]:
                        sk, v = sig[j]
                        if need.get(sk, 0) < v:
                            need[sk] = v
                    for sk, v in need.items():
                        if waited.get(sk, 0) < v:
                            e.wait_ge(sems[sk], v)
                            waited[sk] = v
                    ins = fn(e)
                    if sig[i] is not None:
                        sk, v = sig[i]
                        if dma:
                            ins.then_inc(sems[sk], 16)
                            final[sk] = v
                        else:
                            ins.then_inc(sems[sk], 1)
                for sk, v in final.items():
                    if waited.get(sk, 0) < v:
                        e.wait_ge(sems[sk], v)

            @block.tensor
            def _(e):
                run_engine("pe", e)

            @block.vector
            def _(e):
                run_engine("dve", e)

            @block.scalar
            def _(e):
                run_engine("act", e)

            @block.gpsimd
            def _(e):
                run_engine("pool", e)

            @block.sync
            def _(e):
                run_engine("sp", e)


def build(nseq=NSEQ, depth=DEPTH, do_ffn=True, do_mix=True, do_dn=True, do_da=True, same_engine_sync=True, debug=False):
    nc = bass.Bass("TRN2", target_bir_lowering=False)
    x = nc.dram_tensor("x", [nseq, S, D], F32, kind="ExternalInput").ap()
    y = nc.dram_tensor("y", [nseq, S, D], F32, kind="ExternalOutput").ap()
    W = {}
    for name, shape in WSHAPES:
        W[name] = nc.dram_tensor(name, shape, F32, kind="ExternalInput").ap()
    c_ident = nc.dram_tensor("c_ident", [128, 128], F32, kind="ExternalInput").ap()
    c_bs = nc.dram_tensor("c_bs", [128, 512], F32, kind="ExternalInput").ap()
    c_abs = nc.dram_tensor("c_abs", [128, 896], F32, kind="ExternalInput").ap()
    c_masks = nc.dram_tensor("c_masks", [4, 128, 128], F32, kind="ExternalInput").ap()

    P = Prog(nc)

    def dump(nm, ap):
        if not debug:
            return
        dt_ = nc.dram_tensor("dbg_" + nm, [ap.shape[0], ap.shape[1]], F32, kind="ExternalOutput").ap()
        P.add("pool", lambda e: e.dma_start(out=dt_, in_=ap), r=list(P.keys), w=[("dbg", nm)], dma=True)
    off = [16512]

    def sb(name, shape, dtype, at=None):
        esz = 4 if dtype == F32 else 2
        nb = esz
        for s_ in shape[1:]:
            nb *= s_
        o = off[0] if at is None else at
        t = nc.alloc_sbuf_tensor_at(name, list(shape), dtype, offset=o)
        if at is None:
            off[0] = o + ((nb + 31) // 32) * 32
        return t

    ident_f = sb("ident_f", [128, 128], F32)
    ident_b = sb("ident_b", [128, 128], BF16)
    ones_f = sb("ones_f", [128, 128], F32)
    ones_b = sb("ones_b", [128, 128], BF16)
    eps_t = sb("eps_t", [128, 1], F32)
    nw1 = sb("nw1", [128, DEPTH, KC], F32)
    nwm = sb("nwm", [128, DEPTH, KC], F32)
    nw2 = sb("nw2", [128, DEPTH, KC], F32)
    nwf = sb("nwf", [128, KC], F32)
    bar = sb("bar", [128, 8], F32)
    eps5_t = sb("eps5_t", [128, 1], F32)
    bs_t = sb("bs_t", [128, 512], F32)
    abs_t = sb("abs_t", [128, 896], F32)
    sublnw = sb("sublnw", [128, DEPTH], F32)
    neglam = sb("neglam", [128, DEPTH], F32)
    lam_s = sb("lam_s", [128, 4], F32)
    convw = sb("convw", [128, DEPTH, 12, 5], F32)
    wdn = sb("wdn", [128, DEPTH], F32)
    R_x = off[0]
    xT = sb("xT", [128, KC, S], F32)
    R_h = off[0]
    hT = sb("hT", [128, KC, S], BF16)
    R_in = off[0]
    off[0] += 48 * 1024
    R_o = off[0]
    off[0] += 32 * 1024
    R_free = off[0]
    assert off[0] <= 229344, off[0]
    aT = sb("aT", [128, FCH, 1024], BF16, at=R_in)
    o_ = R_o
    wgu = []
    for b in range(2):
        wgu.append(sb(f"wgu{b}", [128, KC, 2, 256], BF16, at=o_))
        o_ += 8192
    wdb = []
    for b in range(4):
        wdb.append(sb(f"wdb{b}", [128, D], BF16, at=o_))
        o_ += 2048
    sq = [sb(f"sq{b}", [128, 512], F32, at=o_ + 2048 * b) for b in range(2)]
    o_ += 4096
    sgt = [sb(f"sgt{b}", [128, 512], F32, at=o_ + 2048 * b) for b in range(2)]
    o_ += 4096
    assert o_ <= R_free
    lnt = sb("lnt", [128, 512], F32, at=R_free)
    rstd = sb("rstd", [128, 512], F32, at=R_free + 2048)
    xin = [sb(f"xin{b}", [128, D], F32, at=R_in + 4096 * b) for b in range(2)]
    yo = [sb(f"yo{b}", [128, D], F32, at=R_in + 8192 + 4096 * b) for b in range(2)]
    yTf = sb("yTf", [128, KC, 512], F32, at=R_h)

    ps = [nc.alloc_psum_tensor(f"ps{b}", [128, 512], F32) for b in range(8)]

    def PS(b):
        return ("ps", b)

    P.add("sp", lambda e: e.dma_start(out=ident_f[:], in_=c_ident), w=["ident_f"], dma=True)
    P.add("pool", lambda e: e.dma_start(out=ident_b[:], in_=c_ident), w=["ident_b"], dma=True)
    P.add("dve", lambda e: e.memset(ones_f[:], 1.0), w=["ones_f"])
    P.add("dve", lambda e: e.memset(ones_b[:], 1.0), w=["ones_b"])
    P.add("dve", lambda e: e.memset(eps_t[:], 1e-6), w=["eps_t"])
    for t_, nm in ((nw1, "ffn1_norm"), (nwm, "mix_norm"), (nw2, "ffn2_norm")):
        P.add("sp", lambda e, t_=t_, nm=nm: e.dma_start(
            out=t_[:], in_=W[nm].rearrange("l (c p) -> p l c", p=128), allow_slow_non_contiguous=True),
            w=[nm], dma=True)
    P.add("sp", lambda e: e.dma_start(out=nwf[:], in_=W["final_norm"].rearrange("(c p) -> p c", p=128),
                                      allow_slow_non_contiguous=True), w=["final_norm"], dma=True)

    P.add("dve", lambda e: e.memset(eps5_t[:], 1e-5), w=["eps5_t"])
    P.add("sp", lambda e: e.dma_start(out=bs_t[:], in_=c_bs), w=["bs_t"], dma=True)
    P.add("sp", lambda e: e.dma_start(out=abs_t[:], in_=c_abs), w=["abs_t"], dma=True)
    P.add("sp", lambda e: e.dma_start(out=sublnw[:], in_=W["diff_subln"].rearrange("l p -> p l"),
                                      allow_slow_non_contiguous=True), w=["sublnw"], dma=True)
    for l_ in range(DEPTH):
        for kk_ in range(5):
            P.add("sp", lambda e, l_=l_, kk_=kk_: e.dma_start(out=convw[:, l_, :, kk_], in_=W["conv_w"][l_, kk_].rearrange("(c p) -> p c", p=128),
                                                  allow_slow_non_contiguous=True), w=["convw"], dma=True)
    P.add("sp", lambda e: e.dma_start(out=wdn[:], in_=W["dn_out_norm"].rearrange("l p -> p l"),
                                      allow_slow_non_contiguous=True), w=["wdn"], dma=True)
    lpb = sb("lpb", [128, DEPTH, 4, 64], F32, at=R_free)
    lpt = sb("lpt", [128, 64], F32, at=R_free + 4096)
    P.add("sp", lambda e: e.dma_start(out=lpb[:].rearrange("p l a d -> p (l a d)"),
                                      in_=W["diff_lambda"].rearrange("l a d -> (l a d)").partition_broadcast(128)),
          w=["lpb"], dma=True)
    for l in range(DEPTH):
        li = 0.8 - 0.6 * math.exp(-0.3 * l)
        for a in range(2):
            P.add("dve", lambda e, l=l, a=a: e.tensor_tensor(out=lpt[:], in0=lpb[:, l, 2 * a, :], in1=lpb[:, l, 2 * a + 1, :],
                                                          op=ALU.mult), r=["lpb"], w=["lpt"])
            P.add("dve", lambda e, a=a: e.reduce_sum(lam_s[:, a:a + 1], lpt[:], axis=mybir.AxisListType.X),
                  r=["lpt"], w=["lam_s"])
        P.add("act", lambda e: e.activation(lam_s[:, 2:4], lam_s[:, 0:2], AF.Exp), r=["lam_s"], w=["lam_s"])
        P.add("dve", lambda e, l=l: e.tensor_tensor(out=neglam[:, l:l + 1], in0=lam_s[:, 3:4], in1=lam_s[:, 2:3],
                                                 op=ALU.subtract), r=["lam_s"], w=["neglam"])
        P.add("dve", lambda e, l=l, li=li: e.tensor_scalar(out=neglam[:, l:l + 1], in0=neglam[:, l:l + 1], scalar1=-li,
                                                        scalar2=None, op0=ALU.add), r=["neglam"], w=["neglam"])
        P.add("dve", lambda e, l=l, li=li: e.tensor_scalar(out=sublnw[:, l:l + 1], in0=sublnw[:, l:l + 1], scalar1=1.0 - li,
                                                        scalar2=None, op0=ALU.mult), r=["sublnw"], w=["sublnw"])
    P.barrier(bar[:])

    rr = {"evac": 0, "bank": 0}

    def evac_eng():
        rr["evac"] ^= 1
        return "dve" if rr["evac"] else "act"

    def copy_op(eng, out, in_, r, w):
        if eng == "act":
            P.add("act", lambda e: e.copy(out, in_), r=r, w=w)
        else:
            P.add(eng, lambda e: e.tensor_copy(out, in_), r=r, w=w)

    def xk(c, t):
        return ("x", c, t)

    def hk(c, t):
        return ("h", c, t)

    def load_x(s):
        for j in range(16):
            b = j % 2
            T = j // 4
            P.add("sp", lambda e, b=b, j=j: e.dma_start(out=xin[b][:], in_=x[s, j * 128:(j + 1) * 128, :]),
                  r=(), w=[("xin", b)], dma=True)
            for half in range(2):
                bank = (2 * j + half) % 4
                for i in range(4):
                    c = 4 * half + i
                    P.add("pe", lambda e, bank=bank, i=i, b=b, c=c: e.transpose(
                        ps[bank][:, i * 128:(i + 1) * 128], xin[b][:, c * 128:(c + 1) * 128], ident_f[:]),
                        r=[("xin", b), "ident_f"], w=[PS(bank)])
                copy_op(evac_eng(), xT[:, 4 * half:4 * half + 4, j * 128:(j + 1) * 128],
                        ps[bank][:].rearrange("p (a t) -> p a t", a=4),
                        r=[PS(bank)], w=[xk(4 * half + i, T) for i in range(4)])

    sqm = [sb(f"sqm{b}", [128, 512], F32, at=R_free + 4096 + 2048 * b) for b in range(2)]

    def rstd_for_tile(T, eps_tile, scale, sq=sq):
        tsl = slice(T * 512, (T + 1) * 512)
        for c in range(KC):
            b = c % 2
            P.add("act", lambda e, c=c, b=b: e.activation(sq[b][:], xT[:, c, tsl], AF.Square),
                  r=[xk(c, T)], w=[("sq", b)])
            P.add("pe", lambda e, c=c, b=b: e.matmul(ps[7][:], ones_f[:], sq[b][:], start=(c == 0), stop=(c == KC - 1)),
                  r=[("sq", b), "ones_f"], w=[PS(7)])
        P.add("act", lambda e: e.activation(lnt[:], ps[7][:], AF.Ln, bias=eps_tile[:], scale=scale),
              r=[PS(7), "eps_t"], w=["lnt"])
        P.add("act", lambda e: e.activation(rstd[:], lnt[:], AF.Exp, scale=-0.5), r=["lnt"], w=["rstd"])

    def norm_to_h(T, nw_ap, sq=sq):
        tsl = slice(T * 512, (T + 1) * 512)
        rstd_for_tile(T, eps_t, 1.0 / D, sq)
        for c in range(KC):
            P.add("dve", lambda e, c=c: e.scalar_tensor_tensor(
                out=hT[:, c, tsl], in0=xT[:, c, tsl], scalar=nw_ap[:, c:c + 1], op0=ALU.mult,
                in1=rstd[:], op1=ALU.mult), r=[xk(c, T), "rstd"], w=[hk(c, T)])

    def ffn(l, nw, wg, wu, wd):
        nw_ap = nw[:, l, :]
        for half in range(2):
            tiles = [2 * half, 2 * half + 1]
            for T in tiles:
                norm_to_h(T, nw_ap)
            for fg in range(FCH // 2):
                b = fg % 2
                for gi, wsrc in enumerate((wg, wu)):
                    P.add("pool", lambda e, b=b, gi=gi, wsrc=wsrc, fg=fg: e.dma_start(
                        out=wgu[b][:, :, gi, :],
                        in_=wsrc[l][:, fg * 256:(fg + 1) * 256].rearrange("(k p) f -> p k f", p=128)),
                        w=[("wgu", b, gi)], dma=True)
                for fc in range(2):
                    f = 2 * fg + fc
                    for T in tiles:
                        tsl = slice(T * 512, (T + 1) * 512)
                        tl = slice((T - 2 * half) * 512, (T - 2 * half + 1) * 512)
                        gb = rr["bank"] % 4
                        ub = 4 + rr["bank"] % 3
                        rr["bank"] += 1
                        for gi, bank in ((0, gb), (1, ub)):
                            for k in range(KC):
                                P.add("pe", lambda e, k=k, gi=gi, bank=bank, b=b, fc=fc, tsl=tsl: e.matmul(
                                    ps[bank][:], wgu[b][:, k, gi, fc * 128:(fc + 1) * 128], hT[:, k, tsl],
                                    start=(k == 0), stop=(k == KC - 1)),
                                    r=[("wgu", b, gi), hk(k, T)], w=[PS(bank)])
                        sb_ = rr["bank"] % 2
                        P.add("act", lambda e, gb=gb, sb_=sb_: e.activation(sgt[sb_][:], ps[gb][:], AF.Silu),
                              r=[PS(gb)], w=[("sgt", sb_)])
                        P.add("dve", lambda e, ub=ub, sb_=sb_, f=f, tl=tl: e.tensor_tensor(
                            out=aT[:, f, tl], in0=ps[ub][:], in1=sgt[sb_][:], op=ALU.mult),
                            r=[PS(ub), ("sgt", sb_)], w=[("a", f, T)])
            for T in tiles:
                tsl = slice(T * 512, (T + 1) * 512)
                tl = slice((T - 2 * half) * 512, (T - 2 * half + 1) * 512)
                for f in range(FCH):
                    b = rr.setdefault("wd", 0) % 4
                    rr["wd"] += 1
                    P.add("pool", lambda e, b=b, f=f: e.dma_start(out=wdb[b][:], in_=wd[l][f * 128:(f + 1) * 128, :]),
                          w=[("wdb", b)], dma=True)
                    for d in range(KC):
                        P.add("pe", lambda e, b=b, d=d, f=f, tl=tl: e.matmul(
                            ps[d][:], wdb[b][:, d * 128:(d + 1) * 128], aT[:, f, tl],
                            start=(f == 0), stop=(f == FCH - 1)),
                            r=[("wdb", b), ("a", f, T)], w=[PS(d)])
                for d in range(KC):
                    P.add("dve", lambda e, d=d, tsl=tsl: e.scalar_tensor_tensor(
                        out=xT[:, d, tsl], in0=ps[d][:], scalar=0.5, op0=ALU.mult, in1=xT[:, d, tsl], op1=ALU.add),
                        r=[PS(d), xk(d, T)], w=[xk(d, T)])

    def store_y(s):
        for T in range(4):
            tsl = slice(T * 512, (T + 1) * 512)
            rstd_for_tile(T, eps_t, 1.0 / D)
            for c in range(KC):
                P.add("dve", lambda e, c=c, tsl=tsl: e.scalar_tensor_tensor(
                    out=yTf[:, c, :], in0=xT[:, c, tsl], scalar=nwf[:, c:c + 1], op0=ALU.mult,
                    in1=rstd[:], op1=ALU.mult), r=[xk(c, T), "rstd"], w=[("yTf", c)])
            for j4 in range(4):
                j = 4 * T + j4
                b = j % 2
                for half in range(2):
                    bank = (2 * j + half) % 4
                    for i in range(4):
                        c = 4 * half + i
                        P.add("pe", lambda e, bank=bank, i=i, c=c, j4=j4: e.transpose(
                            ps[bank][:, i * 128:(i + 1) * 128], yTf[:, c, j4 * 128:(j4 + 1) * 128], ident_f[:]),
                            r=[("yTf", c), "ident_f"], w=[PS(bank)])
                    copy_op(evac_eng(), yo[b][:, half * 512:(half + 1) * 512], ps[bank][:],
                            r=[PS(bank)], w=[("yo", b, half)])
                P.add("sp", lambda e, b=b, j=j: e.dma_start(out=y[s, j * 128:(j + 1) * 128, :], in_=yo[b][:]),
                      r=[("yo", b, 0), ("yo", b, 1)], w=[("y", s, j)], dma=True)


    SLOPES = [2.0 ** (-8.0 * (h + 1) / 4) for h in range(4)]
    OQ, OK_, OV, OG = 2064, 2576, 3088, 3600
    o_daT = sb("o_daT", [128, 4, S], BF16, at=R_o)
    o_dnT = sb("o_dnT", [128, 4, S], BF16, at=R_o + 16384)
    a_o = R_in
    wqkv = []
    for b in range(2):
        wqkv.append(sb(f"wqkv{b}", [128, 3, KC, 128], BF16, at=a_o))
        a_o += 6144
    aqh = [sb(f"aqh{b}", [128, S], BF16, at=a_o + 4096 * b) for b in range(2)]
    a_o += 8192
    akh = [sb(f"akh{b}", [128, S], BF16, at=a_o + 4096 * b) for b in range(2)]
    a_o += 8192
    avh = [sb(f"avh{b}", [128, 16, 128], BF16, at=a_o + 4096 * b) for b in range(2)]
    a_o += 8192
    pT = [sb(f"pT{b}", [128, 512], BF16, at=a_o + 1024 * b) for b in range(4)]
    a_o += 4096
    tmpf = [sb(f"tmpf{b}", [128, 512], F32, at=a_o + 2048 * b) for b in range(3)]
    a_o += 6144
    assert a_o <= R_in + 48 * 1024
    f_o = R_free + 4096
    rz = sb("rz", [128, 512], F32, at=f_o)
    o0 = sb("o0", [128, 512], F32, at=f_o + 2048)
    t1 = sb("t1", [128, 512], F32, at=f_o + 4096)
    oc = sb("oc", [128, 512], F32, at=f_o + 6144)
    sqa = sb("sqa", [128, 512], F32, at=f_o + 8192)
    f_o += 10240
    assert f_o <= 229344, f_o

    def attention(l):
        for h in HEADS:
            b = h % 2
            for wi, o_col in enumerate((OQ, OK_, OV)):
                P.add("pool", lambda e, b=b, wi=wi, o_col=o_col, h=h: e.dma_start(
                    out=wqkv[b][:, wi, :, :],
                    in_=W["w_in"][l][:, o_col + h * 128:o_col + (h + 1) * 128].rearrange("(k p) f -> p k f", p=128)),
                    w=[("wqkv", b, wi)], dma=True)
            for wi, dst, scl in ((0, aqh[b], 0.125), (1, akh[b], 1.0)):
                for T in range(4):
                    tsl = slice(T * 512, (T + 1) * 512)
                    bank = rr["bank"] % 3
                    rr["bank"] += 1
                    for k in range(KC):
                        P.add("pe", lambda e, k=k, bank=bank, wi=wi, b=b, tsl=tsl: e.matmul(
                            ps[bank][:], wqkv[b][:, wi, k, :], hT[:, k, tsl], start=(k == 0), stop=(k == KC - 1)),
                            r=[("wqkv", b, wi), hk(k, T)], w=[PS(bank)])
                    P.add("act", lambda e, dst=dst, tsl=tsl, bank=bank, scl=scl: e.activation(
                        dst[:, tsl], ps[bank][:], AF.Copy, scale=scl), r=[PS(bank)], w=[("aqk", wi, b, T)])
            for g in range(4):
                bank = rr["bank"] % 3
                rr["bank"] += 1
                for jj in range(4):
                    j = 4 * g + jj
                    for k in range(KC):
                        P.add("pe", lambda e, k=k, bank=bank, jj=jj, j=j, b=b: e.matmul(
                            ps[bank][:, jj * 128:(jj + 1) * 128], hT[:, k, j * 128:(j + 1) * 128], wqkv[b][:, 2, k, :],
                            start=(k == 0), stop=(k == KC - 1)),
                            r=[("wqkv", b, 2), hk(k, j // 4)], w=[PS(bank)])
                P.add("dve", lambda e, bank=bank, g=g, b=b: e.tensor_copy(
                    avh[b][:, 4 * g:4 * g + 4, :], ps[bank][:].rearrange("p (a t) -> p a t", a=4)),
                    r=[PS(bank)], w=[("avh", b, g)])
            slope = SLOPES[h]
            tiles = []
            for G in range(4):
                gsl = slice(G * 512, (G + 1) * 512)
                for m in range(2):
                    msl = slice(64 * m, 64 * m + 64)
                    ob, zb = 3 + m, 5 + m
                    plan = []
                    for j in range(16):
                        r_ = j - 4 * G
                        off_ = 512 * G - 128 * j
                        if 0 <= r_ <= 3:
                            src = abs_t[:, 384 - 128 * r_:896 - 128 * r_]
                            coef, cb = -slope, 0.0
                            dmin, dmax = 0, max(128 * r_ + 127, 511 - 128 * r_)
                        elif off_ > 0:
                            src = bs_t[:]
                            coef, cb = -slope, -slope * off_
                            dmin, dmax = off_ - 127, off_ + 511
                        else:
                            src = bs_t[:]
                            coef, cb = slope, slope * off_
                            dmin, dmax = -off_ - 511, -off_ + 127
                        if -slope * dmin < -80.0:
                            continue
                        plan.append((j, src, coef, cb, (-slope * dmax) < -80.0))
                    for (j, src, coef, cb, clamp) in plan:
                        first = (j == plan[0][0])
                        last = (j == plan[-1][0])
                        ti = len(tiles)
                        sbk, tb, pb = ti % 3, ti % 3, ti % 4

                        def fA(sbk=sbk, msl=msl, j=j, gsl=gsl, G=G, b=b):
                            P.add("pe", lambda e: e.matmul(
                                ps[sbk][:], akh[b][msl, j * 128:(j + 1) * 128], aqh[b][msl, gsl], start=True, stop=True),
                                r=[("aqk", 0, b, G), ("aqk", 1, b, j // 4)], w=[PS(sbk)])

                        def fB(src=src, coef=coef, sbk=sbk, tb=tb, pb=pb, cb=cb, clamp=clamp):
                            P.add("dve", lambda e: e.scalar_tensor_tensor(
                                out=tmpf[tb][:], in0=src, scalar=coef, op0=ALU.mult, in1=ps[sbk][:], op1=ALU.add),
                                r=[PS(sbk)], w=[("tmpf", tb)])
                            if clamp:
                                P.add("dve", lambda e: e.tensor_scalar(
                                    out=tmpf[tb][:], in0=tmpf[tb][:], scalar1=-80.0 - cb, scalar2=None, op0=ALU.max),
                                    r=[("tmpf", tb)], w=[("tmpf", tb)])
                            P.add("act", lambda e: e.activation(
                                pT[pb][:], tmpf[tb][:], AF.Exp, bias=cb), r=[("tmpf", tb)], w=[("pT", pb)])

                        def fC(ob=ob, zb=zb, j=j, pb=pb, first=first, last=last, b=b):
                            P.add("pe", lambda e: e.matmul(
                                ps[ob][:], avh[b][:, j, :], pT[pb][:], start=first, stop=last),
                                r=[("avh", b, j // 4), ("pT", pb)], w=[PS(ob)])
                            P.add("pe", lambda e: e.matmul(
                                ps[zb][:], ones_b[:], pT[pb][:], start=first, stop=last),
                                r=[("pT", pb)], w=[PS(zb)])

                        fE = None
                        if last:
                            def fE(ob=ob, zb=zb, m=m, gsl=gsl, G=G, h=h):
                                P.add("dve", lambda e: e.reciprocal(rz[:], ps[zb][:]), r=[PS(zb)], w=["rz"])
                                if m == 0:
                                    P.add("dve", lambda e: e.tensor_tensor(out=o0[:], in0=ps[ob][:], in1=rz[:], op=ALU.mult),
                                          r=[PS(ob), "rz"], w=["o0"])
                                    return
                                P.add("dve", lambda e: e.tensor_tensor(out=t1[:], in0=ps[ob][:], in1=rz[:], op=ALU.mult),
                                      r=[PS(ob), "rz"], w=["t1"])
                                P.add("dve", lambda e: e.scalar_tensor_tensor(
                                    out=oc[:], in0=t1[:], scalar=neglam[:, l:l + 1], op0=ALU.mult, in1=o0[:], op1=ALU.add),
                                    r=["t1", "o0", "neglam"], w=["oc"])
                                P.add("act", lambda e: e.activation(sqa[:], oc[:], AF.Square), r=["oc"], w=["sqa"])
                                P.add("pe", lambda e: e.matmul(ps[7][:], ones_f[:], sqa[:], start=True, stop=True),
                                      r=["sqa"], w=[PS(7)])
                                P.add("act", lambda e: e.activation(lnt[:], ps[7][:], AF.Ln, bias=eps5_t[:], scale=1.0 / 128),
                                      r=[PS(7)], w=["lnt"])
                                P.add("act", lambda e: e.activation(rstd[:], lnt[:], AF.Exp, scale=-0.5), r=["lnt"], w=["rstd"])
                                P.add("dve", lambda e: e.scalar_tensor_tensor(
                                    out=o_daT[:, h, gsl], in0=oc[:], scalar=sublnw[:, l:l + 1], op0=ALU.mult, in1=rstd[:],
                                    op1=ALU.mult), r=["oc", "rstd", "sublnw"], w=[("oda", h, G)])
                        tiles.append((fA, fB, fC, fE))
            LA, ED = ATT_LA, ATT_ED
            nt = len(tiles)
            pend = []
            for idx in range(nt + LA + ED + 1):
                if idx < nt:
                    tiles[idx][0]()
                i = idx - LA
                if 0 <= i < nt:
                    tiles[i][1]()
                    tiles[i][2]()
                    if tiles[i][3] is not None:
                        pend.append((idx + ED, tiles[i][3]))
                while pend and pend[0][0] <= idx:
                    pend.pop(0)[1]()

    mT = sb("mT", [128, KC, S], BF16, at=R_in)
    wbr = [sb(f"wbr{i}", [128, 4, D], BF16, at=R_in + 32768 + 8192 * i) for i in range(2)]
    m_o = R_free + 4096
    gw = [sb(f"gw{b}", [128, KC, 2, 128], BF16, at=m_o + 4096 * b) for b in range(2)]
    m_o += 8192
    sg_ = [sb(f"sg{b}", [128, 512], F32, at=m_o + 2048 * b) for b in range(4)]
    m_o += 8192
    assert m_o <= 229344, m_o

    def merge(l, use_dn, use_da):
        for i, nm in enumerate(("w_branch_dn", "w_branch_da")):
            P.add("pool", lambda e, i=i, nm=nm: e.dma_start(
                out=wbr[i][:], in_=W[nm][l].rearrange("(k p) f -> p k f", p=128)), w=[("wbr", i)], dma=True)
        for dc in range(KC):
            b = dc % 2
            for gi in range(2):
                P.add("pool", lambda e, b=b, gi=gi, dc=dc: e.dma_start(
                    out=gw[b][:, :, gi, :],
                    in_=W["w_in"][l][:, OG + gi * D + dc * 128:OG + gi * D + (dc + 1) * 128].rearrange(
                        "(k p) f -> p k f", p=128)), w=[("gw", b, gi)], dma=True)
            for T in range(4):
                tsl = slice(T * 512, (T + 1) * 512)
                base = 4 * (rr["bank"] % 2)
                rr["bank"] += 1
                for i, (src, key) in enumerate(((o_dnT, "odn"), (o_daT, "oda"))):
                    for hc in range(4):
                        P.add("pe", lambda e, i=i, hc=hc, src=src, base=base, dc=dc, tsl=tsl: e.matmul(
                            ps[base + i][:], wbr[i][:, hc, dc * 128:(dc + 1) * 128], src[:, hc, tsl],
                            start=(hc == 0), stop=(hc == 3)),
                            r=[("wbr", i), (key, hc, T)], w=[PS(base + i)])
                for gi in range(2):
                    for k in range(KC):
                        P.add("pe", lambda e, gi=gi, k=k, b=b, base=base, tsl=tsl: e.matmul(
                            ps[base + 2 + gi][:], gw[b][:, k, gi, :], hT[:, k, tsl], start=(k == 0), stop=(k == KC - 1)),
                            r=[("gw", b, gi), hk(k, T)], w=[PS(base + 2 + gi)])
                for gi in range(2):
                    P.add("act", lambda e, gi=gi, base=base: e.activation(sg_[gi][:], ps[base + 2 + gi][:], AF.Sigmoid),
                          r=[PS(base + 2 + gi)], w=[("sg", gi)])
                for gi in range(2):
                    P.add("dve", lambda e, gi=gi, base=base: e.tensor_tensor(
                        out=sg_[2 + gi][:], in0=ps[base + gi][:], in1=sg_[gi][:], op=ALU.mult),
                        r=[PS(base + gi), ("sg", gi)], w=[("sg", 2 + gi)])
                P.add("dve", lambda e, dc=dc, tsl=tsl: e.tensor_tensor(
                    out=mT[:, dc, tsl], in0=sg_[2][:], in1=sg_[3][:], op=ALU.add),
                    r=[("sg", 2), ("sg", 3)], w=[("m", dc, T)])
        for do in range(KC):
            b = do % 2
            P.add("pool", lambda e, b=b, do=do: e.dma_start(
                out=gw[b][:].rearrange("p k g f -> p k (g f)")[:, :, 0:128],
                in_=W["w_out"][l][:, do * 128:(do + 1) * 128].rearrange("(k p) f -> p k f", p=128)),
                w=[("gw", b, 0), ("gw", b, 1)], dma=True)
            for T in range(4):
                tsl = slice(T * 512, (T + 1) * 512)
                bank = rr["bank"] % 7
                rr["bank"] += 1
                for k in range(KC):
                    P.add("pe", lambda e, k=k, b=b, bank=bank, tsl=tsl: e.matmul(
                        ps[bank][:], gw[b][:].rearrange("p k g f -> p k (g f)")[:, k, 0:128], mT[:, k, tsl],
                        start=(k == 0), stop=(k == KC - 1)),
                        r=[("gw", b, 0), ("gw", b, 1), ("m", k, T)], w=[PS(bank)])
                P.add("dve", lambda e, do=do, tsl=tsl, bank=bank: e.tensor_tensor(
                    out=xT[:, do, tsl], in0=ps[bank][:], in1=xT[:, do, tsl], op=ALU.add),
                    r=[PS(bank), xk(do, T)], w=[xk(do, T)])


    qT = sb("qT", [128, 4, S], BF16, at=R_in)
    kT = sb("kT", [128, 4, S], BF16, at=R_in + 16384)
    vT = sb("vT", [128, 4, S], BF16, at=R_in + 32768)
    segs = [[R_o, R_o + 16384], [R_free + 4096, 229344]]

    def dalloc(segs_, name, shape, dtype):
        esz = 4 if dtype == F32 else 2
        nb = esz
        for s_ in shape[1:]:
            nb *= s_
        nb = ((nb + 31) // 32) * 32
        for sg in segs_:
            if sg[0] + nb <= sg[1]:
                t = nc.alloc_sbuf_tensor_at(name, list(shape), dtype, offset=sg[0])
                sg[0] += nb
                return t
        raise RuntimeError("dn scratch full: " + name)

    ba = dalloc(segs[1:], "dn_ba", [128, 16, 16], F32)
    ctail = segs[1][0]
    wc = [dalloc(segs, f"dn_wc{b}", [128, KC, 128], BF16) for b in range(2)]
    pc = dalloc(segs, "dn_pc", [128, S + 4], F32)
    cv = dalloc(segs, "dn_cv", [128, S], F32)
    wba = dalloc(segs, "dn_wba", [128, KC, 16], BF16)
    dsq = dalloc(segs, "dn_sq", [128, 512], F32)
    csegs = [[R_h, R_h + 32768], [R_o, R_o + 16384], [R_free, R_free + 4096], [ctail, 229344]]
    o_tok = dalloc(csegs, "dn_otok", [128, 16, 512], BF16)
    Sf = dalloc(csegs, "dn_Sf", [128, 8, 128], F32)
    Sb = dalloc(csegs, "dn_Sb", [128, 8, 128], BF16)
    mk_f = dalloc(csegs, "dn_mkf", [128, 2, 128], F32)
    mk4 = dalloc(csegs, "dn_mk4", [128, 4, 512], BF16)
    sc = {}
    for nm in ("beta", "nbeta", "g", "gc", "gtot", "egc", "negc", "ekd", "egl", "tmpa"):
        sc[nm] = dalloc(csegs, "dn_" + nm, [128, 16, 8], F32)
    abc = dalloc(csegs, "dn_abc", [128, 2, 8], F32)
    Wt = [[dalloc(csegs, f"dn_W{d}{b}", [128, 4, 128], BF16) for b in range(2)] for d in range(2)]
    aTt = [[dalloc(csegs, f"dn_aT{d}{b}", [128, 4, 128], BF16) for b in range(2)] for d in range(2)]
    kdt = [[dalloc(csegs, f"dn_kd{d}{b}", [128, 4, 128], BF16) for b in range(2)] for d in range(2)]
    vtk = [[dalloc(csegs, f"dn_vt{d}{b}", [128, 4, 128], BF16) for b in range(2)] for d in range(2)]
    Mt = [dalloc(csegs, f"dn_M{b}", [128, 4, 128], F32) for b in range(2)]
    Nt = [dalloc(csegs, f"dn_N{b}", [128, 4, 128], F32) for b in range(2)]
    Wf = dalloc(csegs, "dn_Wf", [128, 4, 128], F32)
    Gbc = dalloc(csegs, "dn_Gbc", [128, 4, 128], F32)
    Et = dalloc(csegs, "dn_Et", [128, 4, 128], F32)
    tinc = Et
    tstr = dalloc(csegs, "dn_tstr", [128, 4, 128], F32)
    r0t = dalloc(csegs, "dn_r0", [128, 4, 128], BF16)
    dlt = dalloc(csegs, "dn_dl", [128, 4, 128], BF16)
    qsg = Gbc
    otf = tstr
    junk = Gbc
    onb = r0t
    ssn = dalloc(csegs, "dn_ssn", [128, 8], F32)
    wz = [sb(f"dn_wz{b}", [128, KC, 128], BF16, at=R_o + 2048 * b) for b in range(2)]
    zt = [sb(f"dn_zt{b}", [128, 512], F32, at=R_o + 4096 + 2048 * b) for b in range(2)]

    def psb(b_):
        return ps[b_][:].bitcast(BF16)

    def nb_():
        rr["bank"] += 1
        return rr["bank"] % 8

    def deltanet(l):
        P.add("dve", lambda e: e.memset(pc[:, 0:2], 0.0), w=["pcpad"])
        P.add("dve", lambda e: e.memset(pc[:, S + 2:S + 4], 0.0), w=["pcpad"])
        for cc in range(12):
            wb = cc % 2
            h = cc % 4
            P.add("pool", lambda e, wb=wb, cc=cc: e.dma_start(
                out=wc[wb][:], in_=W["w_in"][l][:, cc * 128:(cc + 1) * 128].rearrange("(k p) f -> p k f", p=128)),
                w=[("dwc", wb)], dma=True)
            for T in range(4):
                tsl = slice(T * 512, (T + 1) * 512)
                bank = nb_() % 4
                for k in range(KC):
                    P.add("pe", lambda e, k=k, bank=bank, wb=wb, tsl=tsl: e.matmul(
                        ps[bank][:], wc[wb][:, k, :], hT[:, k, tsl], start=(k == 0), stop=(k == KC - 1)),
                        r=[("dwc", wb), hk(k, T)], w=[PS(bank)])
                P.add("act", lambda e, bank=bank, T=T: e.copy(pc[:, 2 + T * 512:2 + (T + 1) * 512], ps[bank][:]),
                      r=[PS(bank)], w=[("pc", T)])
            pck = [("pc", T) for T in range(4)] + ["pcpad"]
            P.add("dve", lambda e, cc=cc: e.tensor_scalar(out=cv[:], in0=pc[:, 0:S], scalar1=convw[:, l, cc, 0:1],
                                                         scalar2=None, op0=ALU.mult), r=pck, w=["cv"])
            for kk in range(1, 5):
                P.add("dve", lambda e, cc=cc, kk=kk: e.scalar_tensor_tensor(
                    out=cv[:], in0=pc[:, kk:kk + S], scalar=convw[:, l, cc, kk:kk + 1], op0=ALU.mult,
                    in1=cv[:], op1=ALU.add), r=pck + ["cv"], w=["cv"])
            if cc >= 8:
                P.add("act", lambda e, h=h: e.activation(vT[:, h, :], cv[:], AF.Silu), r=["cv"],
                      w=[("vT", h, T) for T in range(4)])
            else:
                dst = qT if cc < 4 else kT
                nm = "qT" if cc < 4 else "kT"
                scl = (128.0 ** -0.5) if cc < 4 else 1.0
                P.add("act", lambda e: e.activation(cv[:], cv[:], AF.Silu), r=["cv"], w=["cv"])
                for T in range(4):
                    tsl = slice(T * 512, (T + 1) * 512)
                    P.add("act", lambda e, tsl=tsl: e.activation(dsq[:], cv[:, tsl], AF.Square), r=["cv"], w=["dsq"])
                    P.add("pe", lambda e: e.matmul(ps[7][:], ones_f[:], dsq[:], start=True, stop=True),
                          r=["dsq"], w=[PS(7)])
                    P.add("act", lambda e: e.activation(lnt[:], ps[7][:], AF.Ln, bias=eps_t[:], scale=1.0),
                          r=[PS(7)], w=["lnt"])
                    P.add("act", lambda e: e.activation(rstd[:], lnt[:], AF.Exp, scale=-0.5), r=["lnt"], w=["rstd"])
                    P.add("dve", lambda e, dst=dst, h=h, tsl=tsl, scl=scl: e.scalar_tensor_tensor(
                        out=dst[:, h, tsl], in0=cv[:, tsl], scalar=scl, op0=ALU.mult, in1=rstd[:], op1=ALU.mult),
                        r=["cv", "rstd"], w=[(nm, h, T)])
        P.add("pool", lambda e: e.dma_start(out=wba[:], in_=W["w_in"][l][:, 2048:2064].rearrange("(k p) f -> p k f", p=128)),
              w=["wba"], dma=True)
        for j in range(16):
            for k in range(KC):
                P.add("pe", lambda e, j=j, k=k: e.matmul(ps[3][:, j * 16:(j + 1) * 16], hT[:, k, j * 128:(j + 1) * 128],
                                                        wba[:, k, :], start=(k == 0), stop=(k == KC - 1)),
                      r=["wba", hk(k, j // 4)], w=[PS(3)])
        P.add("dve", lambda e: e.tensor_copy(ba[:].rearrange("p j c -> p (j c)"), ps[3][:, 0:256]), r=[PS(3)], w=["ba"])
        P.barrier(bar[:])
        P.add("sp", lambda e: e.dma_start(out=mk_f[:, 0, :], in_=c_masks[0]), w=["mk_f"], dma=True)
        P.add("sp", lambda e: e.dma_start(out=mk_f[:, 1, :], in_=c_masks[2]), w=["mk_f"], dma=True)
        for mi in range(4):
            for h in range(4):
                P.add("pool", lambda e, mi=mi, h=h: e.dma_start(out=mk4[:, mi, h * 128:(h + 1) * 128], in_=c_masks[mi]),
                      w=["mk4"], dma=True)
        P.add("sp", lambda e: e.dma_start(out=abc[:, 0, :], in_=W["dn_a_log"][l].rearrange("a b -> (a b)").partition_broadcast(128)),
              w=["abc"], dma=True)
        P.add("sp", lambda e: e.dma_start(out=abc[:, 1, :], in_=W["dn_dt_bias"][l].rearrange("a b -> (a b)").partition_broadcast(128)),
              w=["abc"], dma=True)
        P.add("act", lambda e: e.activation(abc[:, 0, :], abc[:, 0, :], AF.Exp), r=["abc"], w=["abc"])
        P.add("dve", lambda e: e.tensor_scalar(out=abc[:, 0, :], in0=abc[:, 0, :], scalar1=-1.0, scalar2=None, op0=ALU.mult),
              r=["abc"], w=["abc"])
        for j in range(16):
            P.add("dve", lambda e, j=j: e.tensor_tensor(out=sc["tmpa"][:, j, :], in0=ba[:, j, 8:16], in1=abc[:, 1, :], op=ALU.add),
                  r=["ba", "abc"], w=["tmpa"])
        P.add("act", lambda e: e.activation(sc["tmpa"][:], sc["tmpa"][:], AF.Exp), r=["tmpa"], w=["tmpa"])
        P.add("act", lambda e: e.activation(sc["tmpa"][:], sc["tmpa"][:], AF.Ln, bias=1.0), r=["tmpa"], w=["tmpa"])
        for j in range(16):
            P.add("dve", lambda e, j=j: e.tensor_tensor(out=sc["g"][:, j, :], in0=sc["tmpa"][:, j, :], in1=abc[:, 0, :], op=ALU.mult),
                  r=["tmpa", "abc"], w=["g"])
        for j in range(16):
            P.add("act", lambda e, j=j: e.activation(sc["beta"][:, j, :], ba[:, j, 0:8], AF.Exp, scale=-1.0), r=["ba"], w=["beta"])
        P.add("dve", lambda e: e.tensor_scalar(out=sc["beta"][:], in0=sc["beta"][:], scalar1=1.0, scalar2=None, op0=ALU.add),
              r=["beta"], w=["beta"])
        P.add("dve", lambda e: e.reciprocal(sc["beta"][:], sc["beta"][:]), r=["beta"], w=["beta"])
        P.add("dve", lambda e: e.tensor_scalar(out=sc["nbeta"][:], in0=sc["beta"][:], scalar1=-1.0, scalar2=None, op0=ALU.mult),
              r=["beta"], w=["nbeta"])
        for j in range(16):
            P.add("pe", lambda e, j=j: e.matmul(ps[3][:, j * 8:j * 8 + 4], mk_f[:, 0, :], sc["g"][:, j, 0:4], start=True, stop=True),
                  r=["g", "mk_f"], w=[PS(3)])
            P.add("pe", lambda e, j=j: e.matmul(ps[3][:, j * 8 + 4:j * 8 + 8], mk_f[:, 1, :], sc["g"][:, j, 4:8], start=True, stop=True),
                  r=["g", "mk_f"], w=[PS(3)])
            P.add("pe", lambda e, j=j: e.matmul(ps[4][:, j * 8:j * 8 + 8], ones_f[:], sc["g"][:, j, :], start=True, stop=True),
                  r=["g"], w=[PS(4)])
        P.add("dve", lambda e: e.tensor_copy(sc["gc"][:].rearrange("p j c -> p (j c)"), ps[3][:, 0:128]), r=[PS(3)], w=["gc"])
        P.add("dve", lambda e: e.tensor_copy(sc["gtot"][:].rearrange("p j c -> p (j c)"), ps[4][:, 0:128]), r=[PS(4)], w=["gtot"])
        P.add("act", lambda e: e.activation(sc["egc"][:], sc["gc"][:], AF.Exp), r=["gc"], w=["egc"])
        P.add("dve", lambda e: e.tensor_scalar(out=sc["negc"][:], in0=sc["egc"][:], scalar1=-1.0, scalar2=None, op0=ALU.mult),
              r=["egc"], w=["negc"])
        P.add("dve", lambda e: e.tensor_tensor(out=sc["ekd"][:], in0=sc["gtot"][:], in1=sc["gc"][:], op=ALU.subtract),
              r=["gtot", "gc"], w=["ekd"])
        P.add("act", lambda e: e.activation(sc["ekd"][:], sc["ekd"][:], AF.Exp), r=["ekd"], w=["ekd"])
        P.add("act", lambda e: e.activation(sc["egl"][:], sc["gtot"][:], AF.Exp), r=["gtot"], w=["egl"])
        P.add("dve", lambda e: e.memset(Sf[:], 0.0), w=[("Sf", d) for d in range(2)])
        P.add("dve", lambda e: e.memset(Sb[:], 0.0), w=[("Sb", d) for d in range(2)])

        def pre(dr, j, bf):
            jsl = slice(j * 128, (j + 1) * 128)
            Tj = j // 4
            inc4 = mk4[:, 0 + 2 * dr, :]
            str4 = mk4[:, 1 + 2 * dr, :]
            bk = nb_()
            for h in range(4):
                P.add("pe", lambda e, h=h, bk=bk: e.transpose(psb(bk)[:, h * 128:(h + 1) * 128], kT[:, h, jsl], ident_b[:]),
                      r=[("kT", h, Tj)], w=[PS(bk)])
            for h in range(4):
                P.add("act", lambda e, h=h, bk=bk: e.activation(kdt[dr][bf][:, h, :], psb(bk)[:, h * 128:(h + 1) * 128], AF.Copy,
                                                              scale=sc["ekd"][:, j, dr * 4 + h:dr * 4 + h + 1]),
                      r=[PS(bk), "ekd"], w=[("kd", dr, bf)])
            bv = nb_()
            for h in range(4):
                P.add("pe", lambda e, h=h, bv=bv: e.transpose(psb(bv)[:, h * 128:(h + 1) * 128], vT[:, h, jsl], ident_b[:]),
                      r=[("vT", h, Tj)], w=[PS(bv)])
            P.add("dve", lambda e, bv=bv: e.tensor_copy(vtk[dr][bf][:].rearrange("p h e -> p (h e)"), psb(bv)[:, 0:512]),
                  r=[PS(bv)], w=[("vt", dr, bf)])
            for h in range(4):
                P.add("dve", lambda e, h=h: e.tensor_scalar(out=Gbc[:, h, :], in0=ones_f[:], scalar1=sc["g"][:, j, dr * 4 + h:dr * 4 + h + 1],
                                                          scalar2=None, op0=ALU.mult), r=["g"], w=["Gbc"])
            bg = nb_()
            for h in range(4):
                P.add("pe", lambda e, h=h, bg=bg: e.matmul(ps[bg][:, h * 128:(h + 1) * 128], Gbc[:, h, :], mk_f[:, dr, :],
                                                          start=True, stop=True), r=["Gbc", "mk_f"], w=[PS(bg)])
            for h in range(4):
                P.add("dve", lambda e, h=h, bg=bg: e.tensor_scalar(
                    out=Et[:, h, :], in0=ps[bg][:, h * 128:(h + 1) * 128], scalar1=sc["gc"][:, j, dr * 4 + h:dr * 4 + h + 1],
                    scalar2=0.0, op0=ALU.subtract, op1=ALU.min), r=[PS(bg), "gc"], w=["Et"])
            P.add("act", lambda e: e.activation(Et[:], Et[:], AF.Exp), r=["Et"], w=["Et"])
            Etf = Et[:].rearrange("p h c -> p (h c)")
            P.add("dve", lambda e: e.tensor_tensor(out=tstr[:].rearrange("p h c -> p (h c)"), in0=Etf, in1=str4, op=ALU.mult),
                  r=["Et", "mk4"], w=["tstr"])
            P.add("dve", lambda e: e.tensor_tensor(out=Etf, in0=Etf, in1=inc4, op=ALU.mult),
                  r=["Et", "mk4"], w=["Et"])
            bkk, bkq = nb_(), nb_()
            for h in range(4):
                P.add("pe", lambda e, h=h, bkk=bkk: e.matmul(ps[bkk][:, h * 128:(h + 1) * 128], kT[:, h, jsl], kT[:, h, jsl],
                                                            start=True, stop=True), r=[("kT", h, Tj)], w=[PS(bkk)])
            for h in range(4):
                P.add("pe", lambda e, h=h, bkq=bkq: e.matmul(ps[bkq][:, h * 128:(h + 1) * 128], kT[:, h, jsl], qT[:, h, jsl],
                                                            start=True, stop=True), r=[("kT", h, Tj), ("qT", h, Tj)], w=[PS(bkq)])
            P.add("dve", lambda e, bkq=bkq: e.tensor_tensor(out=aTt[dr][bf][:].rearrange("p h c -> p (h c)"), in0=ps[bkq][:],
                                                          in1=tinc[:].rearrange("p h c -> p (h c)"), op=ALU.mult),
                  r=[PS(bkq), "Et"], w=[("aT", dr, bf)])
            for h in range(4):
                P.add("dve", lambda e, h=h, bkk=bkk: e.scalar_tensor_tensor(
                    out=Mt[0][:, h, :], in0=ps[bkk][:, h * 128:(h + 1) * 128], scalar=sc["nbeta"][:, j, dr * 4 + h:dr * 4 + h + 1],
                    op0=ALU.mult, in1=tstr[:, h, :], op1=ALU.mult), r=[PS(bkk), "tstr", "nbeta"], w=[("M", 0)])
            bn = nb_()
            for h in range(4):
                P.add("pe", lambda e, h=h, bn=bn: e.transpose(ps[bn][:, h * 128:(h + 1) * 128], Mt[0][:, h, :], ident_f[:]),
                      r=[("M", 0)], w=[PS(bn)])
            P.add("act", lambda e, bn=bn: e.copy(Nt[0][:].rearrange("p h c -> p (h c)"), ps[bn][:]), r=[PS(bn)], w=[("N", 0)])
            Wc = Wf
            for h in range(4):
                P.add("dve", lambda e, h=h: e.tensor_tensor(out=Wc[:, h, :], in0=Mt[0][:, h, :], in1=ident_f[:], op=ALU.add),
                      r=[("M", 0)], w=["Wf"])
            cur = 0
            for lev in range(6):
                nxt = 1 - cur
                if lev < 5:
                    bx = nb_()
                    for h in range(4):
                        P.add("pe", lambda e, h=h, bx=bx, cur=cur: e.matmul(ps[bx][:, h * 128:(h + 1) * 128], Nt[cur][:, h, :], Mt[cur][:, h, :],
                                                                          start=True, stop=True), r=[("M", cur), ("N", cur)], w=[PS(bx)])
                by = nb_()
                for h in range(4):
                    P.add("pe", lambda e, h=h, by=by, cur=cur: e.matmul(ps[by][:, h * 128:(h + 1) * 128], Mt[cur][:, h, :], Nt[cur][:, h, :],
                                                                      start=True, stop=True), r=[("M", cur), ("N", cur)], w=[PS(by)])
                if lev < 5:
                    P.add("dve", lambda e, bx=bx, nxt=nxt: e.tensor_copy(Mt[nxt][:].rearrange("p h c -> p (h c)"), ps[bx][:]),
                          r=[PS(bx)], w=[("M", nxt)])
                P.add("act", lambda e, by=by, nxt=nxt: e.copy(Nt[nxt][:].rearrange("p h c -> p (h c)"), ps[by][:]),
                      r=[PS(by)], w=[("N", nxt)])
                bz = nb_()
                for h in range(4):
                    P.add("pe", lambda e, h=h, bz=bz, nxt=nxt: e.matmul(ps[bz][:, h * 128:(h + 1) * 128], Nt[nxt][:, h, :], Wc[:, h, :],
                                                                      start=True, stop=True), r=[("N", nxt), "Wf"], w=[PS(bz)])
                P.add("dve", lambda e, bz=bz: e.tensor_tensor(out=Wc[:].rearrange("p h c -> p (h c)"), in0=ps[bz][:],
                                                            in1=Wc[:].rearrange("p h c -> p (h c)"), op=ALU.add),
                      r=[PS(bz), "Wf"], w=["Wf"])
                cur = nxt
            P.add("act", lambda e: e.copy(Wt[dr][bf][:], Wf[:]), r=["Wf"], w=[("W", dr, bf)])

        def chain(dr, j, bf, second):
            jsl = slice(j * 128, (j + 1) * 128)
            Tj = j // 4
            ba_, bb_ = nb_(), nb_()
            for h in range(4):
                P.add("pe", lambda e, h=h, ba_=ba_: e.matmul(ps[ba_][:, h * 128:(h + 1) * 128], kT[:, h, jsl], Sb[:, dr * 4 + h, :],
                                                            start=True, stop=True), r=[("kT", h, Tj), ("Sb", dr)], w=[PS(ba_)])
            for h in range(4):
                P.add("pe", lambda e, h=h, bb_=bb_: e.matmul(ps[bb_][:, h * 128:(h + 1) * 128], qT[:, h, jsl], Sb[:, dr * 4 + h, :],
                                                            start=True, stop=True), r=[("qT", h, Tj), ("Sb", dr)], w=[PS(bb_)])
            for h in range(4):
                P.add("dve", lambda e, h=h, ba_=ba_: e.scalar_tensor_tensor(
                    out=r0t[:, h, :], in0=ps[ba_][:, h * 128:(h + 1) * 128], scalar=sc["negc"][:, j, dr * 4 + h:dr * 4 + h + 1],
                    op0=ALU.mult, in1=vtk[dr][bf][:, h, :], op1=ALU.add), r=[PS(ba_), ("vt", dr, bf), "negc"], w=["r0"])
            bc_ = nb_()
            for h in range(4):
                P.add("pe", lambda e, h=h, bc_=bc_: e.matmul(ps[bc_][:, h * 128:(h + 1) * 128], Wt[dr][bf][:, h, :], r0t[:, h, :],
                                                            start=True, stop=True), r=[("W", dr, bf), "r0"], w=[PS(bc_)])
            for h in range(4):
                P.add("act", lambda e, h=h, bc_=bc_: e.activation(dlt[:, h, :], ps[bc_][:, h * 128:(h + 1) * 128], AF.Copy,
                                                                scale=sc["beta"][:, j, dr * 4 + h:dr * 4 + h + 1]),
                      r=[PS(bc_), "beta"], w=["dl"])
            bd_, be_ = nb_(), nb_()
            for h in range(4):
                P.add("pe", lambda e, h=h, bd_=bd_: e.matmul(ps[bd_][:, h * 128:(h + 1) * 128], kdt[dr][bf][:, h, :], dlt[:, h, :],
                                                            start=True, stop=True), r=[("kd", dr, bf), "dl"], w=[PS(bd_)])
            for h in range(4):
                P.add("pe", lambda e, h=h, be_=be_: e.matmul(ps[be_][:, h * 128:(h + 1) * 128], aTt[dr][bf][:, h, :], dlt[:, h, :],
                                                            start=True, stop=True), r=[("aT", dr, bf), "dl"], w=[PS(be_)])
            for h in range(4):
                P.add("act", lambda e, h=h, bb_=bb_: e.activation(qsg[:, h, :], ps[bb_][:, h * 128:(h + 1) * 128], AF.Copy,
                                                                scale=sc["egc"][:, j, dr * 4 + h:dr * 4 + h + 1]),
                      r=[PS(bb_), "egc"], w=["Gbc"])
            if not second:
                P.add("dve", lambda e, be_=be_: e.tensor_tensor(out=o_tok[:, j, :], in0=ps[be_][:], in1=qsg[:].rearrange("p h c -> p (h c)"),
                                                              op=ALU.add), r=[PS(be_), "Gbc"], w=[("otok", j)])
            else:
                P.add("dve", lambda e, be_=be_: e.tensor_tensor(out=otf[:].rearrange("p h c -> p (h c)"), in0=ps[be_][:],
                                                              in1=qsg[:].rearrange("p h c -> p (h c)"), op=ALU.add),
                      r=[PS(be_), "Gbc"], w=["tstr"])
                P.add("dve", lambda e: e.tensor_tensor(out=otf[:].rearrange("p h c -> p (h c)"), in0=otf[:].rearrange("p h c -> p (h c)"),
                                                     in1=o_tok[:, j, :], op=ALU.add), r=["tstr", ("otok", j)], w=["tstr"])
            for h in range(4):
                P.add("dve", lambda e, h=h, bd_=bd_: e.scalar_tensor_tensor(
                    out=Sf[:, dr * 4 + h, :], in0=Sf[:, dr * 4 + h, :], scalar=sc["egl"][:, j, dr * 4 + h:dr * 4 + h + 1],
                    op0=ALU.mult, in1=ps[bd_][:, h * 128:(h + 1) * 128], op1=ALU.add), r=[PS(bd_), ("Sf", dr), "egl"], w=[("Sf", dr)])
            P.add("act", lambda e: e.copy(Sb[:, dr * 4:dr * 4 + 4, :], Sf[:, dr * 4:dr * 4 + 4, :]), r=[("Sf", dr)], w=[("Sb", dr)])
            if second:
                for h in range(4):
                    P.add("act", lambda e, h=h: e.activation(junk[:, h, :], otf[:, h, :], AF.Square, accum_out=ssn[:, h:h + 1]),
                          r=["tstr"], w=["Gbc", "ssn"])
                P.add("act", lambda e: e.activation(ssn[:, 4:8], ssn[:, 0:4], AF.Ln, bias=eps_t[:], scale=1.0 / 128),
                      r=["ssn"], w=["ssn"])
                P.add("act", lambda e: e.activation(ssn[:, 4:8], ssn[:, 4:8], AF.Exp, scale=-0.5), r=["ssn"], w=["ssn"])
                for h in range(4):
                    P.add("dve", lambda e, h=h: e.tensor_scalar(out=onb[:, h, :], in0=otf[:, h, :], scalar1=ssn[:, 4 + h:5 + h],
                                                              scalar2=None, op0=ALU.mult), r=["tstr", "ssn"], w=["r0"])
                bt = nb_()
                for h in range(4):
                    P.add("pe", lambda e, h=h, bt=bt: e.transpose(psb(bt)[:, h * 128:(h + 1) * 128], onb[:, h, :], ident_b[:]),
                          r=["r0"], w=[PS(bt)])
                P.add("dve", lambda e, bt=bt: e.tensor_scalar(
                    out=o_dnT[:, :, jsl], in0=psb(bt)[:, 0:512].rearrange("p (h c) -> p h c", h=4), scalar1=wdn[:, l:l + 1],
                    scalar2=None, op0=ALU.mult), r=[PS(bt), "wdn"], w=[("odn", h, Tj) for h in range(4)])

        for step in range(17):
            if step < 16:
                pre(0, step, step % 2)
                pre(1, 15 - step, step % 2)
            if step > 0:
                st = step - 1
                chain(0, st, st % 2, st >= 8)
                chain(1, 15 - st, st % 2, st >= 8)
        if l == 0:
            dump("qT", qT[:].rearrange("p h s -> p (h s)")); dump("kT", kT[:].rearrange("p h s -> p (h s)"))
            dump("vT", vT[:].rearrange("p h s -> p (h s)"))
            for nm_ in ("beta", "g", "gc", "gtot", "egc", "ekd", "egl"):
                dump("sc_" + nm_, sc[nm_][:].rearrange("p j c -> p (j c)"))
            dump("ba", ba[:].rearrange("p j c -> p (j c)"))
            dump("W00", Wt[0][0][:].rearrange("p h c -> p (h c)")); dump("aT00", aTt[0][0][:].rearrange("p h c -> p (h c)"))
            dump("kd00", kdt[0][0][:].rearrange("p h c -> p (h c)")); dump("Sf", Sf[:].rearrange("p h c -> p (h c)"))
            dump("otok", o_tok[:].rearrange("p j c -> p (j c)")); dump("odnT", o_dnT[:].rearrange("p h s -> p (h s)"))
            dump("Et", Et[:].rearrange("p h c -> p (h c)")); dump("M0", Mt[0][:].rearrange("p h c -> p (h c)"))
        P.barrier(bar[:])
        for T in range(4):
            norm_to_h(T, nwm[:, l, :], sqm)
        for h in range(4):
            b = h % 2
            P.add("pool", lambda e, b=b, h=h: e.dma_start(
                out=wz[b][:], in_=W["w_in"][l][:, 1536 + h * 128:1536 + (h + 1) * 128].rearrange("(k p) f -> p k f", p=128)),
                w=[("wz", b)], dma=True)
            for T in range(4):
                tsl = slice(T * 512, (T + 1) * 512)
                bank = nb_() % 4
                zb_ = nb_() % 2
                for k in range(KC):
                    P.add("pe", lambda e, k=k, bank=bank, b=b, tsl=tsl: e.matmul(
                        ps[bank][:], wz[b][:, k, :], hT[:, k, tsl], start=(k == 0), stop=(k == KC - 1)),
                        r=[("wz", b), hk(k, T)], w=[PS(bank)])
                P.add("act", lambda e, bank=bank, zb_=zb_: e.activation(zt[zb_][:], ps[bank][:], AF.Silu), r=[PS(bank)], w=[("zt", zb_)])
                P.add("dve", lambda e, h=h, tsl=tsl, zb_=zb_: e.tensor_tensor(out=o_dnT[:, h, tsl], in0=o_dnT[:, h, tsl], in1=zt[zb_][:],
                                                                           op=ALU.mult), r=[("zt", zb_), ("odn", h, T)], w=[("odn", h, T)])

    def mixer(l):
        for T in range(4):
            norm_to_h(T, nwm[:, l, :], sqm)
        P.barrier(bar[:])
        if do_dn:
            deltanet(l)
            if l == 0:
                dump("odnT2", o_dnT[:].rearrange("p h s -> p (h s)")); dump("zt0", zt[0][:]); dump("hT2", hT[:].rearrange("p k s -> p (k s)"))
        else:
            for h in range(4):
                for T in range(4):
                    P.add("dve", lambda e, h=h, T=T: e.memset(o_dnT[:, h, T * 512:(T + 1) * 512], 0.0),
                          w=[("odn", h, T)])
        P.barrier(bar[:])
        if do_da:
            attention(l)
        else:
            for h in range(4):
                for T in range(4):
                    P.add("dve", lambda e, h=h, T=T: e.memset(o_daT[:, h, T * 512:(T + 1) * 512], 0.0),
                          w=[("oda", h, T)])
        P.barrier(bar[:])
        if l == 0:
            dump("wq", wqkv[0][:, 0, :, :].rearrange("p k f -> p (k f)")); dump("wv", wqkv[0][:, 2, :, :].rearrange("p k f -> p (k f)"))
            dump("aq0", aqh[0][:]); dump("ak0", akh[0][:]); dump("av0", avh[0][:].rearrange("p a b -> p (a b)"))
            dump("pT0", pT[0][:]); dump("tmpf0", tmpf[0][:]); dump("rz", rz[:]); dump("o0", o0[:]); dump("oc", oc[:])
            dump("t1", t1[:]); dump("rstd", rstd[:]); dump("odaT", o_daT[:].rearrange("p h s -> p (h s)"))
            dump("hT", hT[:].rearrange("p k s -> p (k s)")); dump("neglam", neglam[:]); dump("sublnw", sublnw[:])
            P.barrier(bar[:])
        merge(l, do_dn, do_da)
        P.barrier(bar[:])

    for s in range(nseq):
        load_x(s)
        P.barrier(bar[:])
        for l in range(depth):
            if do_ffn:
                ffn(l, nw1, W["ffn1_wg"], W["ffn1_wu"], W["ffn1_wd"])
                P.barrier(bar[:])
            if do_mix:
                mixer(l)
            if do_ffn:
                ffn(l, nw2, W["ffn2_wg"], W["ffn2_wu"], W["ffn2_wd"])
                P.barrier(bar[:])
        store_y(s)

    P.emit(same_engine_sync=same_engine_sync)
    return nc, len(P.ops)


def make_consts():
    k = np.arange(128, dtype=np.float32)[:, None]
    q = np.arange(512, dtype=np.float32)[None, :]
    jj = np.arange(896, dtype=np.float32)[None, :]
    i = np.arange(128)
    incu = (i[:, None] <= i[None, :]).astype(np.float32)
    stru = (i[:, None] < i[None, :]).astype(np.float32)
    masks = np.stack([incu, stru, incu.T.copy(), stru.T.copy()]).astype(np.float32)
    return {"c_ident": np.eye(128, dtype=np.float32), "c_bs": np.ascontiguousarray(q - k),
            "c_abs": np.ascontiguousarray(np.abs(jj - 384.0 - k)), "c_masks": masks}


_CACHE = {}


def kernel(**inputs):
    xs = np.concatenate([np.asarray(inputs["x_prompt"], np.float32), np.asarray(inputs["x_sample"], np.float32)], axis=0)
    nb_p = inputs["x_prompt"].shape[0]
    if "nc" not in _CACHE:
        _CACHE["nc"] = build()[0]
    nc = _CACHE["nc"]
    wmap = {name: np.ascontiguousarray(np.asarray(inputs[name], np.float32)) for name, _ in WSHAPES}
    consts = make_consts()
    in_maps = []
    for c in range(NCORES):
        m = {"x": np.ascontiguousarray(xs[c * NSEQ:(c + 1) * NSEQ])}
        m.update(wmap)
        m.update(consts)
        in_maps.append(m)
    res = run_bass_kernel_spmd(nc, in_maps, core_ids=list(range(NCORES)))
    yfull = np.concatenate([np.asarray(r["y"], np.float32) for r in res.results], axis=0)
    return (np.ascontiguousarray(yfull[:nb_p]), np.ascontiguousarray(yfull[nb_p:]))
```

```python
import math
import numpy as np
import concourse.bass as bass
import concourse.mybir as mybir
from concourse.bass_utils import run_bass_kernel_spmd

F32 = mybir.dt.float32
BF16 = mybir.dt.bfloat16
AF = mybir.ActivationFunctionType
ALU = mybir.AluOpType

D = 1024
S = 2048
DFF = 2816
NIN = 5648
DEPTH = 4
NCORES = 8
NSEQ = 5
KC = 8
FCH = 22
NDSLOT = 8
HEADS = [0, 1, 2, 3]
ATT_LA, ATT_ED = 2, 2

WSHAPES = [
    ("ffn1_norm", [DEPTH, D]), ("ffn1_wg", [DEPTH, D, DFF]), ("ffn1_wu", [DEPTH, D, DFF]),
    ("ffn1_wd", [DEPTH, DFF, D]), ("mix_norm", [DEPTH, D]), ("w_in", [DEPTH, D, NIN]),
    ("conv_w", [DEPTH, 5, 1536]), ("dn_a_log", [DEPTH, 2, 4]), ("dn_dt_bias", [DEPTH, 2, 4]),
    ("dn_out_norm", [DEPTH, 128]), ("diff_lambda", [DEPTH, 4, 64]), ("diff_subln", [DEPTH, 128]),
    ("w_branch_dn", [DEPTH, 512, D]), ("w_branch_da", [DEPTH, 512, D]), ("w_out", [DEPTH, D, D]),
    ("ffn2_norm", [DEPTH, D]), ("ffn2_wg", [DEPTH, D, DFF]), ("ffn2_wu", [DEPTH, D, DFF]),
    ("ffn2_wd", [DEPTH, DFF, D]), ("final_norm", [D]),
]


class Prog:
    def __init__(self, nc):
        self.nc = nc
        self.ops = []
        self.keys = set()

    def add(self, eng, fn, r=(), w=(), dma=False):
        self.ops.append((eng, fn, tuple(r), tuple(w), dma))
        self.keys.update(r)
        self.keys.update(w)

    def barrier(self, ap):
        self.add("dve", lambda e: e.memset(ap, 0.0), w=list(self.keys) + ["__BAR__"])

    def emit(self, same_engine_sync=True):
        nc = self.nc
        ops = self.ops
        n = len(ops)
        last_w = {}
        rd = {}
        deps = [None] * n
        needed = [False] * n
        cur_bar = None
        for i, (eng, fn, r, w, dma) in enumerate(ops):
            d = set()
            for k in r:
                j = last_w.get(k, cur_bar)
                if j is not None:
                    d.add(j)
            for k in w:
                j = last_w.get(k, cur_bar)
                if j is not None:
                    d.add(j)
                rr = rd.get(k)
                if rr:
                    d.update(rr.values())
            d.discard(i)
            dd = []
            for j in d:
                je = ops[j][0]
                jd = ops[j][4]
                if not jd and not dma and je == eng:
                    if eng == "pe" or not same_engine_sync:
                        continue
                dd.append(j)
            deps[i] = dd
            for j in dd:
                needed[j] = True
            tag = ("d", i) if dma else eng
            for k in r:
                rd.setdefault(k, {})[tag] = i
            for k in w:
                last_w[k] = i
                rd[k] = {}
            if "__BAR__" in w:
                cur_bar = i
                last_w = {}
                rd = {}
        cnt = {"pe": 0, "dve": 0, "act": 0, "pool": 0}
        dcnt = {"sp": 0, "pool": 0, "act": 0}
        sig = [None] * n
        dslot = [None] * n
        for i, (eng, fn, r, w, dma) in enumerate(ops):
            if dma:
                k = dcnt[eng]
                dcnt[eng] += 1
                slot = k % NDSLOT
                sig[i] = (("dma", eng, slot), 16 * (k // NDSLOT + 1))
                dslot[i] = (slot, 16 * (k // NDSLOT))
            elif needed[i]:
                cnt[eng] += 1
                sig[i] = (("c", eng), cnt[eng])
        per_eng = {"pe": [], "dve": [], "act": [], "pool": [], "sp": []}
        for i, op in enumerate(ops):
            per_eng[op[0]].append(i)
        semkeys = [("c", e) for e in ("pe", "dve", "act", "pool")]
        for q in ("sp", "pool", "act"):
            if dcnt[q]:
                semkeys += [("dma", q, s) for s in range(NDSLOT)]
        from contextlib import ExitStack
        with ExitStack() as es:
            sems = {}
            for sk in semkeys:
                sems[sk] = es.enter_context(nc.semaphore("s_" + "_".join(str(t) for t in sk)))
            block = es.enter_context(nc.Block())

            def run_engine(ename, e):
                waited = {}
                final = {}
                for i in per_eng[ename]:
                    eng, fn, r, w, dma = ops[i]
                    if dma:
                        slot, prev = dslot[i]
                        sk = ("dma", eng, slot)
                        if prev > 0 and waited.get(sk, 0) < prev:
                            e.wait_ge(sems[sk], prev)
                            waited[sk] = prev
                    need = {}
                    for j in deps[i]:
                        sk, v = sig[j]
                        if need.get(sk, 0) < v:
                            need[sk] = v
                    for sk, v in need.items():
                        if waited.get(sk, 0) < v:
                            e.wait_ge(sems[sk], v)
                            waited[sk] = v
                    ins = fn(e)
                    if sig[i] is not None:
                        sk, v = sig[i]
                        if dma:
                            ins.then_inc(sems[sk], 16)
                            final[sk] = v
                        else:
                            ins.then_inc(sems[sk], 1)
                for sk, v in final.items():
                    if waited.get(sk, 0) < v:
                        e.wait_ge(sems[sk], v)

            @block.tensor
            def _(e):
                run_engine("pe", e)

            @block.vector
            def _(e):
                run_engine("dve", e)

            @block.scalar
            def _(e):
                run_engine("act", e)

            @block.gpsimd
            def _(e):
                run_engine("pool", e)

            @block.sync
            def _(e):
                run_engine("sp", e)


def build(nseq=NSEQ, depth=DEPTH, do_ffn=True, do_mix=True, do_dn=True, do_da=True, same_engine_sync=True, debug=False):
    nc = bass.Bass("TRN2", target_bir_lowering=False)
    x = nc.dram_tensor("x", [nseq, S, D], F32, kind="ExternalInput").ap()
    y = nc.dram_tensor("y", [nseq, S, D], F32, kind="ExternalOutput").ap()
    W = {}
    for name, shape in WSHAPES:
        W[name] = nc.dram_tensor(name, shape, F32, kind="ExternalInput").ap()
    c_ident = nc.dram_tensor("c_ident", [128, 128], F32, kind="ExternalInput").ap()
    c_bs = nc.dram_tensor("c_bs", [128, 512], F32, kind="ExternalInput").ap()
    c_abs = nc.dram_tensor("c_abs", [128, 896], F32, kind="ExternalInput").ap()
    c_masks = nc.dram_tensor("c_masks", [4, 128, 128], F32, kind="ExternalInput").ap()

    P = Prog(nc)

    def dump(nm, ap):
        if not debug:
            return
        dt_ = nc.dram_tensor("dbg_" + nm, [ap.shape[0], ap.shape[1]], F32, kind="ExternalOutput").ap()
        P.add("pool", lambda e: e.dma_start(out=dt_, in_=ap), r=list(P.keys), w=[("dbg", nm)], dma=True)
    off = [16512]

    def sb(name, shape, dtype, at=None):
        esz = 4 if dtype == F32 else 2
        nb = esz
        for s_ in shape[1:]:
            nb *= s_
        o = off[0] if at is None else at
        t = nc.alloc_sbuf_tensor_at(name, list(shape), dtype, offset=o)
        if at is None:
            off[0] = o + ((nb + 31) // 32) * 32
        return t

    ident_f = sb("ident_f", [128, 128], F32)
    ident_b = sb("ident_b", [128, 128], BF16)
    ones_f = sb("ones_f", [128, 128], F32)
    ones_b = sb("ones_b", [128, 128], BF16)
    eps_t = sb("eps_t", [128, 1], F32)
    nw1 = sb("nw1", [128, DEPTH, KC], F32)
    nwm = sb("nwm", [128, DEPTH, KC], F32)
    nw2 = sb("nw2", [128, DEPTH, KC], F32)
    nwf = sb("nwf", [128, KC], F32)
    bar = sb("bar", [128, 8], F32)
    eps5_t = sb("eps5_t", [128, 1], F32)
    bs_t = sb("bs_t", [128, 512], F32)
    abs_t = sb("abs_t", [128, 896], F32)
    sublnw = sb("sublnw", [128, DEPTH], F32)
    neglam = sb("neglam", [128, DEPTH], F32)
    lam_s = sb("lam_s", [128, 4], F32)
    convw = sb("convw", [128, DEPTH, 12, 5], F32)
    wdn = sb("wdn", [128, DEPTH], F32)
    R_x = off[0]
    xT = sb("xT", [128, KC, S], F32)
    R_h = off[0]
    hT = sb("hT", [128, KC, S], BF16)
    R_in = off[0]
    off[0] += 48 * 1024
    R_o = off[0]
    off[0] += 32 * 1024
    R_free = off[0]
    assert off[0] <= 229344, off[0]
    aT = sb("aT", [128, FCH, 1024], BF16, at=R_in)
    o_ = R_o
    wgu = []
    for b in range(2):
        wgu.append(sb(f"wgu{b}", [128, KC, 2, 256], BF16, at=o_))
        o_ += 8192
    wdb = []
    for b in range(4):
        wdb.append(sb(f"wdb{b}", [128, D], BF16, at=o_))
        o_ += 2048
    sq = [sb(f"sq{b}", [128, 512], F32, at=o_ + 2048 * b) for b in range(2)]
    o_ += 4096
    sgt = [sb(f"sgt{b}", [128, 512], F32, at=o_ + 2048 * b) for b in range(2)]
    o_ += 4096
    assert o_ <= R_free
    lnt = sb("lnt", [128, 512], F32, at=R_free)
    rstd = sb("rstd", [128, 512], F32, at=R_free + 2048)
    xin = [sb(f"xin{b}", [128, D], F32, at=R_in + 4096 * b) for b in range(2)]
    yo = [sb(f"yo{b}", [128, D], F32, at=R_in + 8192 + 4096 * b) for b in range(2)]
    yTf = sb("yTf", [128, KC, 512], F32, at=R_h)

    ps = [nc.alloc_psum_tensor(f"ps{b}", [128, 512], F32) for b in range(8)]

    def PS(b):
        return ("ps", b)

    P.add("sp", lambda e: e.dma_start(out=ident_f[:], in_=c_ident), w=["ident_f"], dma=True)
    P.add("pool", lambda e: e.dma_start(out=ident_b[:], in_=c_ident), w=["ident_b"], dma=True)
    P.add("dve", lambda e: e.memset(ones_f[:], 1.0), w=["ones_f"])
    P.add("dve", lambda e: e.memset(ones_b[:], 1.0), w=["ones_b"])
    P.add("dve", lambda e: e.memset(eps_t[:], 1e-6), w=["eps_t"])
    for t_, nm in ((nw1, "ffn1_norm"), (nwm, "mix_norm"), (nw2, "ffn2_norm")):
        P.add("sp", lambda e, t_=t_, nm=nm: e.dma_start(
            out=t_[:], in_=W[nm].rearrange("l (c p) -> p l c", p=128), allow_slow_non_contiguous=True),
            w=[nm], dma=True)
    P.add("sp", lambda e: e.dma_start(out=nwf[:], in_=W["final_norm"].rearrange("(c p) -> p c", p=128),
                                      allow_slow_non_contiguous=True), w=["final_norm"], dma=True)

    P.add("dve", lambda e: e.memset(eps5_t[:], 1e-5), w=["eps5_t"])
    P.add("sp", lambda e: e.dma_start(out=bs_t[:], in_=c_bs), w=["bs_t"], dma=True)
    P.add("sp", lambda e: e.dma_start(out=abs_t[:], in_=c_abs), w=["abs_t"], dma=True)
    P.add("sp", lambda e: e.dma_start(out=sublnw[:], in_=W["diff_subln"].rearrange("l p -> p l"),
                                      allow_slow_non_contiguous=True), w=["sublnw"], dma=True)
    for l_ in range(DEPTH):
        for kk_ in range(5):
            P.add("sp", lambda e, l_=l_, kk_=kk_: e.dma_start(out=convw[:, l_, :, kk_], in_=W["conv_w"][l_, kk_].rearrange("(c p) -> p c", p=128),
                                                  allow_slow_non_contiguous=True), w=["convw"], dma=True)
    P.add("sp", lambda e: e.dma_start(out=wdn[:], in_=W["dn_out_norm"].rearrange("l p -> p l"),
                                      allow_slow_non_contiguous=True), w=["wdn"], dma=True)
    lpb = sb("lpb", [128, DEPTH, 4, 64], F32, at=R_free)
    lpt = sb("lpt", [128, 64], F32, at=R_free + 4096)
    P.add("sp", lambda e: e.dma_start(out=lpb[:].rearrange("p l a d -> p (l a d)"),
                                      in_=W["diff_lambda"].rearrange("l a d -> (l a d)").partition_broadcast(128)),
          w=["lpb"], dma=True)
    for l in range(DEPTH):
        li = 0.8 - 0.6 * math.exp(-0.3 * l)
        for a in range(2):
            P.add("dve", lambda e, l=l, a=a: e.tensor_tensor(out=lpt[:], in0=lpb[:, l, 2 * a, :], in1=lpb[:, l, 2 * a + 1, :],
                                                          op=ALU.mult), r=["lpb"], w=["lpt"])
            P.add("dve", lambda e, a=a: e.reduce_sum(lam_s[:, a:a + 1], lpt[:], axis=mybir.AxisListType.X),
                  r=["lpt"], w=["lam_s"])
        P.add("act", lambda e: e.activation(lam_s[:, 2:4], lam_s[:, 0:2], AF.Exp), r=["lam_s"], w=["lam_s"])
        P.add("dve", lambda e, l=l: e.tensor_tensor(out=neglam[:, l:l + 1], in0=lam_s[:, 3:4], in1=lam_s[:, 2:3],
                                                 op=ALU.subtract), r=["lam_s"], w=["neglam"])
        P.add("dve", lambda e, l=l, li=li: e.tensor_scalar(out=neglam[:, l:l + 1], in0=neglam[:, l:l + 1], scalar1=-li,
                                                        scalar2=None, op0=ALU.add), r=["neglam"], w=["neglam"])
        P.add("dve", lambda e, l=l, li=li: e.tensor_scalar(out=sublnw[:, l:l + 1], in0=sublnw[:, l:l + 1], scalar1=1.0 - li,
                                                        scalar2=None, op0=ALU.mult), r=["sublnw"], w=["sublnw"])
    P.barrier(bar[:])

    rr = {"evac": 0, "bank": 0}

    def evac_eng():
        rr["evac"] ^= 1
        return "dve" if rr["evac"] else "act"

    def copy_op(eng, out, in_, r, w):
        if eng == "act":
            P.add("act", lambda e: e.copy(out, in_), r=r, w=w)
        else:
            P.add(eng, lambda e: e.tensor_copy(out, in_), r=r, w=w)

    def xk(c, t):
        return ("x", c, t)

    def hk(c, t):
        return ("h", c, t)

    def load_x(s):
        for j in range(16):
            b = j % 2
            T = j // 4
            P.add("sp", lambda e, b=b, j=j: e.dma_start(out=xin[b][:], in_=x[s, j * 128:(j + 1) * 128, :]),
                  r=(), w=[("xin", b)], dma=True)
            for half in range(2):
                bank = (2 * j + half) % 4
                for i in range(4):
                    c = 4 * half + i
                    P.add("pe", lambda e, bank=bank, i=i, b=b, c=c: e.transpose(
                        ps[bank][:, i * 128:(i + 1) * 128], xin[b][:, c * 128:(c + 1) * 128], ident_f[:]),
                        r=[("xin", b), "ident_f"], w=[PS(bank)])
                copy_op(evac_eng(), xT[:, 4 * half:4 * half + 4, j * 128:(j + 1) * 128],
                        ps[bank][:].rearrange("p (a t) -> p a t", a=4),
                        r=[PS(bank)], w=[xk(4 * half + i, T) for i in range(4)])

    sqm = [sb(f"sqm{b}", [128, 512], F32, at=R_free + 4096 + 2048 * b) for b in range(2)]

    def rstd_for_tile(T, eps_tile, scale, sq=sq):
        tsl = slice(T * 512, (T + 1) * 512)
        for c in range(KC):
            b = c % 2
            P.add("act", lambda e, c=c, b=b: e.activation(sq[b][:], xT[:, c, tsl], AF.Square),
                  r=[xk(c, T)], w=[("sq", b)])
            P.add("pe", lambda e, c=c, b=b: e.matmul(ps[7][:], ones_f[:], sq[b][:], start=(c == 0), stop=(c == KC - 1)),
                  r=[("sq", b), "ones_f"], w=[PS(7)])
        P.add("act", lambda e: e.activation(lnt[:], ps[7][:], AF.Ln, bias=eps_tile[:], scale=scale),
              r=[PS(7), "eps_t"], w=["lnt"])
        P.add("act", lambda e: e.activation(rstd[:], lnt[:], AF.Exp, scale=-0.5), r=["lnt"], w=["rstd"])

    def norm_to_h(T, nw_ap, sq=sq):
        tsl = slice(T * 512, (T + 1) * 512)
        rstd_for_tile(T, eps_t, 1.0 / D, sq)
        for c in range(KC):
            P.add("dve", lambda e, c=c: e.scalar_tensor_tensor(
                out=hT[:, c, tsl], in0=xT[:, c, tsl], scalar=nw_ap[:, c:c + 1], op0=ALU.mult,
                in1=rstd[:], op1=ALU.mult), r=[xk(c, T), "rstd"], w=[hk(c, T)])

    def ffn(l, nw, wg, wu, wd):
        nw_ap = nw[:, l, :]
        for half in range(2):
            tiles = [2 * half, 2 * half + 1]
            for T in tiles:
                norm_to_h(T, nw_ap)
            for fg in range(FCH // 2):
                b = fg % 2
                for gi, wsrc in enumerate((wg, wu)):
                    P.add("pool", lambda e, b=b, gi=gi, wsrc=wsrc, fg=fg: e.dma_start(
                        out=wgu[b][:, :, gi, :],
                        in_=wsrc[l][:, fg * 256:(fg + 1) * 256].rearrange("(k p) f -> p k f", p=128)),
                        w=[("wgu", b, gi)], dma=True)
                for fc in range(2):
                    f = 2 * fg + fc
                    for T in tiles:
                        tsl = slice(T * 512, (T + 1) * 512)
                        tl = slice((T - 2 * half) * 512, (T - 2 * half + 1) * 512)
                        gb = rr["bank"] % 4
                        ub = 4 + rr["bank"] % 3
                        rr["bank"] += 1
                        for gi, bank in ((0, gb), (1, ub)):
                            for k in range(KC):
                                P.add("pe", lambda e, k=k, gi=gi, bank=bank, b=b, fc=fc, tsl=tsl: e.matmul(
                                    ps[bank][:], wgu[b][:, k, gi, fc * 128:(fc + 1) * 128], hT[:, k, tsl],
                                    start=(k == 0), stop=(k == KC - 1)),
                                    r=[("wgu", b, gi), hk(k, T)], w=[PS(bank)])
                        sb_ = rr["bank"] % 2
                        P.add("act", lambda e, gb=gb, sb_=sb_: e.activation(sgt[sb_][:], ps[gb][:], AF.Silu),
                              r=[PS(gb)], w=[("sgt", sb_)])
                        P.add("dve", lambda e, ub=ub, sb_=sb_, f=f, tl=tl: e.tensor_tensor(
                            out=aT[:, f, tl], in0=ps[ub][:], in1=sgt[sb_][:], op=ALU.mult),
                            r=[PS(ub), ("sgt", sb_)], w=[("a", f, T)])
            for T in tiles:
                tsl = slice(T * 512, (T + 1) * 512)
                tl = slice((T - 2 * half) * 512, (T - 2 * half + 1) * 512)
                for f in range(FCH):
                    b = rr.setdefault("wd", 0) % 4
                    rr["wd"] += 1
                    P.add("pool", lambda e, b=b, f=f: e.dma_start(out=wdb[b][:], in_=wd[l][f * 128:(f + 1) * 128, :]),
                          w=[("wdb", b)], dma=True)
                    for d in range(KC):
                        P.add("pe", lambda e, b=b, d=d, f=f, tl=tl: e.matmul(
                            ps[d][:], wdb[b][:, d * 128:(d + 1) * 128], aT[:, f, tl],
                            start=(f == 0), stop=(f == FCH - 1)),
                            r=[("wdb", b), ("a", f, T)], w=[PS(d)])
                for d in range(KC):
                    P.add("dve", lambda e, d=d, tsl=tsl: e.scalar_tensor_tensor(
                        out=xT[:, d, tsl], in0=ps[d][:], scalar=0.5, op0=ALU.mult, in1=xT[:, d, tsl], op1=ALU.add),
                        r=[PS(d), xk(d, T)], w=[xk(d, T)])

    def store_y(s):
        for T in range(4):
            tsl = slice(T * 512, (T + 1) * 512)
            rstd_for_tile(T, eps_t, 1.0 / D)
            for c in range(KC):
                P.add("dve", lambda e, c=c, tsl=tsl: e.scalar_tensor_tensor(
                    out=yTf[:, c, :], in0=xT[:, c, tsl], scalar=nwf[:, c:c + 1], op0=ALU.mult,
                    in1=rstd[:], op1=ALU.mult), r=[xk(c, T), "rstd"], w=[("yTf", c)])
            for j4 in range(4):
                j = 4 * T + j4
                b = j % 2
                for half in range(2):
                    bank = (2 * j + half) % 4
                    for i in range(4):
                        c = 4 * half + i
                        P.add("pe", lambda e, bank=bank, i=i, c=c, j4=j4: e.transpose(
                            ps[bank][:, i * 128:(i + 1) * 128], yTf[:, c, j4 * 128:(j4 + 1) * 128], ident_f[:]),
                            r=[("yTf", c), "ident_f"], w=[PS(bank)])
                    copy_op(evac_eng(), yo[b][:, half * 512:(half + 1) * 512], ps[bank][:],
                            r=[PS(bank)], w=[("yo", b, half)])
                P.add("sp", lambda e, b=b, j=j: e.dma_start(out=y[s, j * 128:(j + 1) * 128, :], in_=yo[b][:]),
                      r=[("yo", b, 0), ("yo", b, 1)], w=[("y", s, j)], dma=True)


    SLOPES = [2.0 ** (-8.0 * (h + 1) / 4) for h in range(4)]
    OQ, OK_, OV, OG = 2064, 2576, 3088, 3600
    o_daT = sb("o_daT", [128, 4, S], BF16, at=R_o)
    o_dnT = sb("o_dnT", [128, 4, S], BF16, at=R_o + 16384)
    a_o = R_in
    wqkv = []
    for b in range(2):
        wqkv.append(sb(f"wqkv{b}", [128, 3, KC, 128], BF16, at=a_o))
        a_o += 6144
    aqh = [sb(f"aqh{b}", [128, S], BF16, at=a_o + 4096 * b) for b in range(2)]
    a_o += 8192
    akh = [sb(f"akh{b}", [128, S], BF16, at=a_o + 4096 * b) for b in range(2)]
    a_o += 8192
    avh = [sb(f"avh{b}", [128, 16, 128], BF16, at=a_o + 4096 * b) for b in range(2)]
    a_o += 8192
    pT = [sb(f"pT{b}", [128, 512], BF16, at=a_o + 1024 * b) for b in range(4)]
    a_o += 4096
    tmpf = [sb(f"tmpf{b}", [128, 512], F32, at=a_o + 2048 * b) for b in range(3)]
    a_o += 6144
    assert a_o <= R_in + 48 * 1024
    f_o = R_free + 4096
    rz = sb("rz", [128, 512], F32, at=f_o)
    o0 = sb("o0", [128, 512], F32, at=f_o + 2048)
    t1 = sb("t1", [128, 512], F32, at=f_o + 4096)
    oc = sb("oc", [128, 512], F32, at=f_o + 6144)
    sqa = sb("sqa", [128, 512], F32, at=f_o + 8192)
    f_o += 10240
    assert f_o <= 229344, f_o

    def attention(l):
        for h in HEADS:
            b = h % 2
            for wi, o_col in enumerate((OQ, OK_, OV)):
                P.add("pool", lambda e, b=b, wi=wi, o_col=o_col, h=h: e.dma_start(
                    out=wqkv[b][:, wi, :, :],
                    in_=W["w_in"][l][:, o_col + h * 128:o_col + (h + 1) * 128].rearrange("(k p) f -> p k f", p=128)),
                    w=[("wqkv", b, wi)], dma=True)
            for wi, dst, scl in ((0, aqh[b], 0.125), (1, akh[b], 1.0)):
                for T in range(4):
                    tsl = slice(T * 512, (T + 1) * 512)
                    bank = rr["bank"] % 3
                    rr["bank"] += 1
                    for k in range(KC):
                        P.add("pe", lambda e, k=k, bank=bank, wi=wi, b=b, tsl=tsl: e.matmul(
                            ps[bank][:], wqkv[b][:, wi, k, :], hT[:, k, tsl], start=(k == 0), stop=(k == KC - 1)),
                            r=[("wqkv", b, wi), hk(k, T)], w=[PS(bank)])
                    P.add("act", lambda e, dst=dst, tsl=tsl, bank=bank, scl=scl: e.activation(
                        dst[:, tsl], ps[bank][:], AF.Copy, scale=scl), r=[PS(bank)], w=[("aqk", wi, b, T)])
            for g in range(4):
                bank = rr["bank"] % 3
                rr["bank"] += 1
                for jj in range(4):
                    j = 4 * g + jj
                    for k in range(KC):
                        P.add("pe", lambda e, k=k, bank=bank, jj=jj, j=j, b=b: e.matmul(
                            ps[bank][:, jj * 128:(jj + 1) * 128], hT[:, k, j * 128:(j + 1) * 128], wqkv[b][:, 2, k, :],
                            start=(k == 0), stop=(k == KC - 1)),
                            r=[("wqkv", b, 2), hk(k, j // 4)], w=[PS(bank)])
                P.add("dve", lambda e, bank=bank, g=g, b=b: e.tensor_copy(
                    avh[b][:, 4 * g:4 * g + 4, :], ps[bank][:].rearrange("p (a t) -> p a t", a=4)),
                    r=[PS(bank)], w=[("avh", b, g)])
            slope = SLOPES[h]
            tiles = []
            for G in range(4):
                gsl = slice(G * 512, (G + 1) * 512)
                for m in range(2):
                    msl = slice(64 * m, 64 * m + 64)
                    ob, zb = 3 + m, 5 + m
                    plan = []
                    for j in range(16):
                        r_ = j - 4 * G
                        off_ = 512 * G - 128 * j
                        if 0 <= r_ <= 3:
                            src = abs_t[:, 384 - 128 * r_:896 - 128 * r_]
                            coef, cb = -slope, 0.0
                            dmin, dmax = 0, max(128 * r_ + 127, 511 - 128 * r_)
                        elif off_ > 0:
                            src = bs_t[:]
                            coef, cb = -slope, -slope * off_
                            dmin, dmax = off_ - 127, off_ + 511
                        else:
                            src = bs_t[:]
                            coef, cb = slope, slope * off_
                            dmin, dmax = -off_ - 511, -off_ + 127
                        if -slope * dmin < -80.0:
                            continue
                        plan.append((j, src, coef, cb, (-slope * dmax) < -80.0))
                    for (j, src, coef, cb, clamp) in plan:
                        first = (j == plan[0][0])
                        last = (j == plan[-1][0])
                        ti = len(tiles)
                        sbk, tb, pb = ti % 3, ti % 3, ti % 4

                        def fA(sbk=sbk, msl=msl, j=j, gsl=gsl, G=G, b=b):
                            P.add("pe", lambda e: e.matmul(
                                ps[sbk][:], akh[b][msl, j * 128:(j + 1) * 128], aqh[b][msl, gsl], start=True, stop=True),
                                r=[("aqk", 0, b, G), ("aqk", 1, b, j // 4)], w=[PS(sbk)])

                        def fB(src=src, coef=coef, sbk=sbk, tb=tb, pb=pb, cb=cb, clamp=clamp):
                            P.add("dve", lambda e: e.scalar_tensor_tensor(
                                out=tmpf[tb][:], in0=src, scalar=coef, op0=ALU.mult, in1=ps[sbk][:], op1=ALU.add),
                                r=[PS(sbk)], w=[("tmpf", tb)])
                            if clamp:
                                P.add("dve", lambda e: e.tensor_scalar(
                                    out=tmpf[tb][:], in0=tmpf[tb][:], scalar1=-80.0 - cb, scalar2=None, op0=ALU.max),
                                    r=[("tmpf", tb)], w=[("tmpf", tb)])
                            P.add("act", lambda e: e.activation(
                                pT[pb][:], tmpf[tb][:], AF.Exp, bias=cb), r=[("tmpf", tb)], w=[("pT", pb)])

                        def fC(ob=ob, zb=zb, j=j, pb=pb, first=first, last=last, b=b):
                            P.add("pe", lambda e: e.matmul(
                                ps[ob][:], avh[b][:, j, :], pT[pb][:], start=first, stop=last),
                                r=[("avh", b, j // 4), ("pT", pb)], w=[PS(ob)])
                            P.add("pe", lambda e: e.matmul(
                                ps[zb][:], ones_b[:], pT[pb][:], start=first, stop=last),
                                r=[("pT", pb)], w=[PS(zb)])

                        fE = None
                        if last:
                            def fE(ob=ob, zb=zb, m=m, gsl=gsl, G=G, h=h):
                                P.add("dve", lambda e: e.reciprocal(rz[:], ps[zb][:]), r=[PS(zb)], w=["rz"])
                                if m == 0:
                                    P.add("dve", lambda e: e.tensor_tensor(out=o0[:], in0=ps[ob][:], in1=rz[:], op=ALU.mult),
                                          r=[PS(ob), "rz"], w=["o0"])
                                    return
                                P.add("dve", lambda e: e.tensor_tensor(out=t1[:], in0=ps[ob][:], in1=rz[:], op=ALU.mult),
                                      r=[PS(ob), "rz"], w=["t1"])
                                P.add("dve", lambda e: e.scalar_tensor_tensor(
                                    out=oc[:], in0=t1[:], scalar=neglam[:, l:l + 1], op0=ALU.mult, in1=o0[:], op1=ALU.add),
                                    r=["t1", "o0", "neglam"], w=["oc"])
                                P.add("act", lambda e: e.activation(sqa[:], oc[:], AF.Square), r=["oc"], w=["sqa"])
                                P.add("pe", lambda e: e.matmul(ps[7][:], ones_f[:], sqa[:], start=True, stop=True),
                                      r=["sqa"], w=[PS(7)])
                                P.add("act", lambda e: e.activation(lnt[:], ps[7][:], AF.Ln, bias=eps5_t[:], scale=1.0 / 128),
                                      r=[PS(7)], w=["lnt"])
                                P.add("act", lambda e: e.activation(rstd[:], lnt[:], AF.Exp, scale=-0.5), r=["lnt"], w=["rstd"])
                                P.add("dve", lambda e: e.scalar_tensor_tensor(
                                    out=o_daT[:, h, gsl], in0=oc[:], scalar=sublnw[:, l:l + 1], op0=ALU.mult, in1=rstd[:],
                                    op1=ALU.mult), r=["oc", "rstd", "sublnw"], w=[("oda", h, G)])
                        tiles.append((fA, fB, fC, fE))
            LA, ED = ATT_LA, ATT_ED
            nt = len(tiles)
            pend = []
            for idx in range(nt + LA + ED + 1):
                if idx < nt:
                    tiles[idx][0]()
                i = idx - LA
                if 0 <= i < nt:
                    tiles[i][1]()
                    tiles[i][2]()
                    if tiles[i][3] is not None:
                        pend.append((idx + ED, tiles[i][3]))
                while pend and pend[0][0] <= idx:
                    pend.pop(0)[1]()

    mT = sb("mT", [128, KC, S], BF16, at=R_in)
    wbr = [sb(f"wbr{i}", [128, 4, D], BF16, at=R_in + 32768 + 8192 * i) for i in range(2)]
    m_o = R_free + 4096
    gw = [sb(f"gw{b}", [128, KC, 2, 128], BF16, at=m_o + 4096 * b) for b in range(2)]
    m_o += 8192
    sg_ = [sb(f"sg{b}", [128, 512], F32, at=m_o + 2048 * b) for b in range(4)]
    m_o += 8192
    assert m_o <= 229344, m_o

    def merge(l, use_dn, use_da):
        for i, nm in enumerate(("w_branch_dn", "w_branch_da")):
            P.add("pool", lambda e, i=i, nm=nm: e.dma_start(
                out=wbr[i][:], in_=W[nm][l].rearrange("(k p) f -> p k f", p=128)), w=[("wbr", i)], dma=True)
        for dc in range(KC):
            b = dc % 2
            for gi in range(2):
                P.add("pool", lambda e, b=b, gi=gi, dc=dc: e.dma_start(
                    out=gw[b][:, :, gi, :],
                    in_=W["w_in"][l][:, OG + gi * D + dc * 128:OG + gi * D + (dc + 1) * 128].rearrange(
                        "(k p) f -> p k f", p=128)), w=[("gw", b, gi)], dma=True)
            for T in range(4):
                tsl = slice(T * 512, (T + 1) * 512)
                base = 4 * (rr["bank"] % 2)
                rr["bank"] += 1
                for i, (src, key) in enumerate(((o_dnT, "odn"), (o_daT, "oda"))):
                    for hc in range(4):
                        P.add("pe", lambda e, i=i, hc=hc, src=src, base=base, dc=dc, tsl=tsl: e.matmul(
                            ps[base + i][:], wbr[i][:, hc, dc * 128:(dc + 1) * 128], src[:, hc, tsl],
                            start=(hc == 0), stop=(hc == 3)),
                            r=[("wbr", i), (key, hc, T)], w=[PS(base + i)])
                for gi in range(2):
                    for k in range(KC):
                        P.add("pe", lambda e, gi=gi, k=k, b=b, base=base, tsl=tsl: e.matmul(
                            ps[base + 2 + gi][:], gw[b][:, k, gi, :], hT[:, k, tsl], start=(k == 0), stop=(k == KC - 1)),
                            r=[("gw", b, gi), hk(k, T)], w=[PS(base + 2 + gi)])
                for gi in range(2):
                    P.add("act", lambda e, gi=gi, base=base: e.activation(sg_[gi][:], ps[base + 2 + gi][:], AF.Sigmoid),
                          r=[PS(base + 2 + gi)], w=[("sg", gi)])
                for gi in range(2):
                    P.add("dve", lambda e, gi=gi, base=base: e.tensor_tensor(
                        out=sg_[2 + gi][:], in0=ps[base + gi][:], in1=sg_[gi][:], op=ALU.mult),
                        r=[PS(base + gi), ("sg", gi)], w=[("sg", 2 + gi)])
                P.add("dve", lambda e, dc=dc, tsl=tsl: e.tensor_tensor(
                    out=mT[:, dc, tsl], in0=sg_[2][:], in1=sg_[3][:], op=ALU.add),
                    r=[("sg", 2), ("sg", 3)], w=[("m", dc, T)])
        for do in range(KC):
            b = do % 2
            P.add("pool", lambda e, b=b, do=do: e.dma_start(
                out=gw[b][:].rearrange("p k g f -> p k (g f)")[:, :, 0:128],
                in_=W["w_out"][l][:, do * 128:(do + 1) * 128].rearrange("(k p) f -> p k f", p=128)),
                w=[("gw", b, 0), ("gw", b, 1)], dma=True)
            for T in range(4):
                tsl = slice(T * 512, (T + 1) * 512)
                bank = rr["bank"] % 7
                rr["bank"] += 1
                for k in range(KC):
                    P.add("pe", lambda e, k=k, b=b, bank=bank, tsl=tsl: e.matmul(
                        ps[bank][:], gw[b][:].rearrange("p k g f -> p k (g f)")[:, k, 0:128], mT[:, k, tsl],
                        start=(k == 0), stop=(k == KC - 1)),
                        r=[("gw", b, 0), ("gw", b, 1), ("m", k, T)], w=[PS(bank)])
                P.add("dve", lambda e, do=do, tsl=tsl, bank=bank: e.tensor_tensor(
                    out=xT[:, do, tsl], in0=ps[bank][:], in1=xT[:, do, tsl], op=ALU.add),
                    r=[PS(bank), xk(do, T)], w=[xk(do, T)])


    qT = sb("qT", [128, 4, S], BF16, at=R_in)
    kT = sb("kT", [128, 4, S], BF16, at=R_in + 16384)
    vT = sb("vT", [128, 4, S], BF16, at=R_in + 32768)
    segs = [[R_o, R_o + 16384], [R_free + 4096, 229344]]

    def dalloc(segs_, name, shape, dtype):
        esz = 4 if dtype == F32 else 2
        nb = esz
        for s_ in shape[1:]:
            nb *= s_
        nb = ((nb + 31) // 32) * 32
        for sg in segs_:
            if sg[0] + nb <= sg[1]:
                t = nc.alloc_sbuf_tensor_at(name, list(shape), dtype, offset=sg[0])
                sg[0] += nb
                return t
        raise RuntimeError("dn scratch full: " + name)

    ba = dalloc(segs[1:], "dn_ba", [128, 16, 16], F32)
    ctail = segs[1][0]
    wc = [dalloc(segs, f"dn_wc{b}", [128, KC, 128], BF16) for b in range(2)]
    pc = dalloc(segs, "dn_pc", [128, S + 4], F32)
    cv = dalloc(segs, "dn_cv", [128, S], F32)
    wba = dalloc(segs, "dn_wba", [128, KC, 16], BF16)
    dsq = dalloc(segs, "dn_sq", [128, 512], F32)
    csegs = [[R_h, R_h + 32768], [R_o, R_o + 16384], [R_free, R_free + 4096], [ctail, 229344]]
    Sf = dalloc(csegs, "dn_Sf", [128, 8, 128], F32)
    Sb = dalloc(csegs, "dn_Sb", [128, 8, 128], BF16)
    mk_f = dalloc(csegs, "dn_mkf", [128, 2, 128], F32)
    mk4 = dalloc(csegs, "dn_mk4", [128, 4, 512], BF16)
    sc = {}
    for nm in ("beta", "nbeta", "g", "gc", "gtot", "egc", "negc", "ekd", "egl", "tmpa"):
        sc[nm] = dalloc(csegs, "dn_" + nm, [128, 16, 8], F32)
    abc = dalloc(csegs, "dn_abc", [128, 2, 8], F32)
    Wt = [[dalloc(csegs, f"dn_W{d}{b}", [128, 4, 128], BF16) for b in range(2)] for d in range(2)]
    aTt = [[dalloc(csegs, f"dn_aT{d}{b}", [128, 4, 128], BF16) for b in range(2)] for d in range(2)]
    kdt = [[dalloc(csegs, f"dn_kd{d}{b}", [128, 4, 128], BF16) for b in range(2)] for d in range(2)]
    vtk = [[dalloc(csegs, f"dn_vt{d}{b}", [128, 4, 128], BF16) for b in range(2)] for d in range(2)]
    Mt = [[dalloc(csegs, f"dn_M{d}{b}", [128, 4, 128], F32) for b in range(2)] for d in range(2)]
    Nt = [[dalloc(csegs, f"dn_N{d}{b}", [128, 4, 128], F32) for b in range(2)] for d in range(2)]
    Wf = [dalloc(csegs, f"dn_Wf{d}", [128, 4, 128], F32) for d in range(2)]
    Gbc = [dalloc(csegs, f"dn_Gbc{d}", [128, 4, 128], F32) for d in range(2)]
    Et = [dalloc(csegs, f"dn_Et{d}", [128, 4, 128], F32) for d in range(2)]
    tinc = Et
    tstr = [dalloc(csegs, f"dn_tstr{d}", [128, 4, 128], F32) for d in range(2)]
    r0t = [dalloc(csegs, f"dn_r0{d}", [128, 4, 128], BF16) for d in range(2)]
    dlt = [dalloc(csegs, f"dn_dl{d}", [128, 4, 128], BF16) for d in range(2)]
    qsg = Gbc
    otf = tstr
    junk = Gbc
    onb = r0t
    ssn = dalloc(csegs, "dn_ssn", [128, 2, 8], F32)
    wz = [sb(f"dn_wz{b}", [128, KC, 128], BF16, at=R_o + 2048 * b) for b in range(2)]
    zt = [sb(f"dn_zt{b}", [128, 512], F32, at=R_o + 4096 + 2048 * b) for b in range(2)]

    def psb(b_):
        return ps[b_][:].bitcast(BF16)

    def nb_():
        rr["bank"] += 1
        return rr["bank"] % 8

    def deltanet(l):
        P.add("dve", lambda e: e.memset(pc[:, 0:2], 0.0), w=["pcpad"])
        P.add("dve", lambda e: e.memset(pc[:, S + 2:S + 4], 0.0), w=["pcpad"])
        for cc in range(12):
            wb = cc % 2
            h = cc % 4
            P.add("pool", lambda e, wb=wb, cc=cc: e.dma_start(
                out=wc[wb][:], in_=W["w_in"][l][:, cc * 128:(cc + 1) * 128].rearrange("(k p) f -> p k f", p=128)),
                w=[("dwc", wb)], dma=True)
            for T in range(4):
                tsl = slice(T * 512, (T + 1) * 512)
                bank = nb_() % 4
                for k in range(KC):
                    P.add("pe", lambda e, k=k, bank=bank, wb=wb, tsl=tsl: e.matmul(
                        ps[bank][:], wc[wb][:, k, :], hT[:, k, tsl], start=(k == 0), stop=(k == KC - 1)),
                        r=[("dwc", wb), hk(k, T)], w=[PS(bank)])
                P.add("act", lambda e, bank=bank, T=T: e.copy(pc[:, 2 + T * 512:2 + (T + 1) * 512], ps[bank][:]),
                      r=[PS(bank)], w=[("pc", T)])
            pck = [("pc", T) for T in range(4)] + ["pcpad"]
            P.add("dve", lambda e, cc=cc: e.tensor_scalar(out=cv[:], in0=pc[:, 0:S], scalar1=convw[:, l, cc, 0:1],
                                                         scalar2=None, op0=ALU.mult), r=pck, w=["cv"])
            for kk in range(1, 5):
                P.add("dve", lambda e, cc=cc, kk=kk: e.scalar_tensor_tensor(
                    out=cv[:], in0=pc[:, kk:kk + S], scalar=convw[:, l, cc, kk:kk + 1], op0=ALU.mult,
                    in1=cv[:], op1=ALU.add), r=pck + ["cv"], w=["cv"])
            if cc >= 8:
                P.add("act", lambda e, h=h: e.activation(vT[:, h, :], cv[:], AF.Silu), r=["cv"],
                      w=[("vT", h, T) for T in range(4)])
            else:
                dst = qT if cc < 4 else kT
                nm = "qT" if cc < 4 else "kT"
                scl = (128.0 ** -0.5) if cc < 4 else 1.0
                P.add("act", lambda e: e.activation(cv[:], cv[:], AF.Silu), r=["cv"], w=["cv"])
                for T in range(4):
                    tsl = slice(T * 512, (T + 1) * 512)
                    P.add("act", lambda e, tsl=tsl: e.activation(dsq[:], cv[:, tsl], AF.Square), r=["cv"], w=["dsq"])
                    P.add("pe", lambda e: e.matmul(ps[7][:], ones_f[:], dsq[:], start=True, stop=True),
                          r=["dsq"], w=[PS(7)])
                    P.add("act", lambda e: e.activation(lnt[:], ps[7][:], AF.Ln, bias=eps_t[:], scale=1.0),
                          r=[PS(7)], w=["lnt"])
                    P.add("act", lambda e: e.activation(rstd[:], lnt[:], AF.Exp, scale=-0.5), r=["lnt"], w=["rstd"])
                    P.add("dve", lambda e, dst=dst, h=h, tsl=tsl, scl=scl: e.scalar_tensor_tensor(
                        out=dst[:, h, tsl], in0=cv[:, tsl], scalar=scl, op0=ALU.mult, in1=rstd[:], op1=ALU.mult),
                        r=["cv", "rstd"], w=[(nm, h, T)])
        P.add("pool", lambda e: e.dma_start(out=wba[:], in_=W["w_in"][l][:, 2048:2064].rearrange("(k p) f -> p k f", p=128)),
              w=["wba"], dma=True)
        for j in range(16):
            for k in range(KC):
                P.add("pe", lambda e, j=j, k=k: e.matmul(ps[3][:, j * 16:(j + 1) * 16], hT[:, k, j * 128:(j + 1) * 128],
                                                        wba[:, k, :], start=(k == 0), stop=(k == KC - 1)),
                      r=["wba", hk(k, j // 4)], w=[PS(3)])
        P.add("dve", lambda e: e.tensor_copy(ba[:].rearrange("p j c -> p (j c)"), ps[3][:, 0:256]), r=[PS(3)], w=["ba"])
        P.barrier(bar[:])
        P.add("sp", lambda e: e.dma_start(out=mk_f[:, 0, :], in_=c_masks[0]), w=["mk_f"], dma=True)
        P.add("sp", lambda e: e.dma_start(out=mk_f[:, 1, :], in_=c_masks[2]), w=["mk_f"], dma=True)
        for mi in range(4):
            for h in range(4):
                P.add("pool", lambda e, mi=mi, h=h: e.dma_start(out=mk4[:, mi, h * 128:(h + 1) * 128], in_=c_masks[mi]),
                      w=["mk4"], dma=True)
        P.add("sp", lambda e: e.dma_start(out=abc[:, 0, :], in_=W["dn_a_log"][l].rearrange("a b -> (a b)").partition_broadcast(128)),
              w=["abc"], dma=True)
        P.add("sp", lambda e: e.dma_start(out=abc[:, 1, :], in_=W["dn_dt_bias"][l].rearrange("a b -> (a b)").partition_broadcast(128)),
              w=["abc"], dma=True)
        P.add("act", lambda e: e.activation(abc[:, 0, :], abc[:, 0, :], AF.Exp), r=["abc"], w=["abc"])
        P.add("dve", lambda e: e.tensor_scalar(out=abc[:, 0, :], in0=abc[:, 0, :], scalar1=-1.0, scalar2=None, op0=ALU.mult),
              r=["abc"], w=["abc"])
        for j in range(16):
            P.add("dve", lambda e, j=j: e.tensor_tensor(out=sc["tmpa"][:, j, :], in0=ba[:, j, 8:16], in1=abc[:, 1, :], op=ALU.add),
                  r=["ba", "abc"], w=["tmpa"])
        P.add("act", lambda e: e.activation(sc["tmpa"][:], sc["tmpa"][:], AF.Exp), r=["tmpa"], w=["tmpa"])
        P.add("act", lambda e: e.activation(sc["tmpa"][:], sc["tmpa"][:], AF.Ln, bias=1.0), r=["tmpa"], w=["tmpa"])
        for j in range(16):
            P.add("dve", lambda e, j=j: e.tensor_tensor(out=sc["g"][:, j, :], in0=sc["tmpa"][:, j, :], in1=abc[:, 0, :], op=ALU.mult),
                  r=["tmpa", "abc"], w=["g"])
        for j in range(16):
            P.add("act", lambda e, j=j: e.activation(sc["beta"][:, j, :], ba[:, j, 0:8], AF.Exp, scale=-1.0), r=["ba"], w=["beta"])
        P.add("dve", lambda e: e.tensor_scalar(out=sc["beta"][:], in0=sc["beta"][:], scalar1=1.0, scalar2=None, op0=ALU.add),
              r=["beta"], w=["beta"])
        P.add("dve", lambda e: e.reciprocal(sc["beta"][:], sc["beta"][:]), r=["beta"], w=["beta"])
        P.add("dve", lambda e: e.tensor_scalar(out=sc["nbeta"][:], in0=sc["beta"][:], scalar1=-1.0, scalar2=None, op0=ALU.mult),
              r=["beta"], w=["nbeta"])
        for j in range(16):
            P.add("pe", lambda e, j=j: e.matmul(ps[3][:, j * 8:j * 8 + 4], mk_f[:, 0, :], sc["g"][:, j, 0:4], start=True, stop=True),
                  r=["g", "mk_f"], w=[PS(3)])
            P.add("pe", lambda e, j=j: e.matmul(ps[3][:, j * 8 + 4:j * 8 + 8], mk_f[:, 1, :], sc["g"][:, j, 4:8], start=True, stop=True),
                  r=["g", "mk_f"], w=[PS(3)])
            P.add("pe", lambda e, j=j: e.matmul(ps[4][:, j * 8:j * 8 + 8], ones_f[:], sc["g"][:, j, :], start=True, stop=True),
                  r=["g"], w=[PS(4)])
        P.add("dve", lambda e: e.tensor_copy(sc["gc"][:].rearrange("p j c -> p (j c)"), ps[3][:, 0:128]), r=[PS(3)], w=["gc"])
        P.add("dve", lambda e: e.tensor_copy(sc["gtot"][:].rearrange("p j c -> p (j c)"), ps[4][:, 0:128]), r=[PS(4)], w=["gtot"])
        P.add("act", lambda e: e.activation(sc["egc"][:], sc["gc"][:], AF.Exp), r=["gc"], w=["egc"])
        P.add("dve", lambda e: e.tensor_scalar(out=sc["negc"][:], in0=sc["egc"][:], scalar1=-1.0, scalar2=None, op0=ALU.mult),
              r=["egc"], w=["negc"])
        P.add("dve", lambda e: e.tensor_tensor(out=sc["ekd"][:], in0=sc["gtot"][:], in1=sc["gc"][:], op=ALU.subtract),
              r=["gtot", "gc"], w=["ekd"])
        P.add("act", lambda e: e.activation(sc["ekd"][:], sc["ekd"][:], AF.Exp), r=["ekd"], w=["ekd"])
        P.add("act", lambda e: e.activation(sc["egl"][:], sc["gtot"][:], AF.Exp), r=["gtot"], w=["egl"])
        P.add("dve", lambda e: e.memset(Sf[:], 0.0), w=[("Sf", d) for d in range(2)])
        P.add("dve", lambda e: e.memset(Sb[:], 0.0), w=[("Sb", d) for d in range(2)])

        def pre(dr, j, bf):
            jsl = slice(j * 128, (j + 1) * 128)
            Tj = j // 4
            inc4 = mk4[:, 0 + 2 * dr, :]
            str4 = mk4[:, 1 + 2 * dr, :]
            bk = nb_()
            for h in range(4):
                P.add("pe", lambda e, h=h, bk=bk: e.transpose(psb(bk)[:, h * 128:(h + 1) * 128], kT[:, h, jsl], ident_b[:]),
                      r=[("kT", h, Tj)], w=[PS(bk)])
            for h in range(4):
                P.add("act", lambda e, h=h, bk=bk: e.activation(kdt[dr][bf][:, h, :], psb(bk)[:, h * 128:(h + 1) * 128], AF.Copy,
                                                              scale=sc["ekd"][:, j, dr * 4 + h:dr * 4 + h + 1]),
                      r=[PS(bk), "ekd"], w=[("kd", dr, bf)])
            bv = nb_()
            for h in range(4):
                P.add("pe", lambda e, h=h, bv=bv: e.transpose(psb(bv)[:, h * 128:(h + 1) * 128], vT[:, h, jsl], ident_b[:]),
                      r=[("vT", h, Tj)], w=[PS(bv)])
            P.add("dve", lambda e, bv=bv: e.tensor_copy(vtk[dr][bf][:].rearrange("p h e -> p (h e)"), psb(bv)[:, 0:512]),
                  r=[PS(bv)], w=[("vt", dr, bf)])
            for h in range(4):
                P.add("dve", lambda e, h=h: e.tensor_scalar(out=Gbc[dr][:, h, :], in0=ones_f[:], scalar1=sc["g"][:, j, dr * 4 + h:dr * 4 + h + 1],
                                                          scalar2=None, op0=ALU.mult), r=["g"], w=[("Gbc", dr)])
            bg = nb_()
            for h in range(4):
                P.add("pe", lambda e, h=h, bg=bg: e.matmul(ps[bg][:, h * 128:(h + 1) * 128], Gbc[dr][:, h, :], mk_f[:, dr, :],
                                                          start=True, stop=True), r=[("Gbc", dr), "mk_f"], w=[PS(bg)])
            yield
            for h in range(4):
                P.add("dve", lambda e, h=h, bg=bg: e.tensor_scalar(
                    out=Et[dr][:, h, :], in0=ps[bg][:, h * 128:(h + 1) * 128], scalar1=sc["gc"][:, j, dr * 4 + h:dr * 4 + h + 1],
                    scalar2=0.0, op0=ALU.subtract, op1=ALU.min), r=[PS(bg), "gc"], w=[("Et", dr)])
            P.add("act", lambda e: e.activation(Et[dr][:], Et[dr][:], AF.Exp), r=[("Et", dr)], w=[("Et", dr)])
            yield
            Etf = Et[dr][:].rearrange("p h c -> p (h c)")
            P.add("dve", lambda e: e.tensor_tensor(out=tstr[dr][:].rearrange("p h c -> p (h c)"), in0=Etf, in1=str4, op=ALU.mult),
                  r=[("Et", dr), "mk4"], w=[("tstr", dr)])
            P.add("dve", lambda e: e.tensor_tensor(out=Etf, in0=Etf, in1=inc4, op=ALU.mult),
                  r=[("Et", dr), "mk4"], w=[("Et", dr)])
            bkk, bkq = nb_(), nb_()
            for h in range(4):
                P.add("pe", lambda e, h=h, bkk=bkk: e.matmul(ps[bkk][:, h * 128:(h + 1) * 128], kT[:, h, jsl], kT[:, h, jsl],
                                                            start=True, stop=True), r=[("kT", h, Tj)], w=[PS(bkk)])
            for h in range(4):
                P.add("pe", lambda e, h=h, bkq=bkq: e.matmul(ps[bkq][:, h * 128:(h + 1) * 128], kT[:, h, jsl], qT[:, h, jsl],
                                                            start=True, stop=True), r=[("kT", h, Tj), ("qT", h, Tj)], w=[PS(bkq)])
            yield
            P.add("dve", lambda e, bkq=bkq: e.tensor_tensor(out=aTt[dr][bf][:].rearrange("p h c -> p (h c)"), in0=ps[bkq][:],
                                                          in1=tinc[dr][:].rearrange("p h c -> p (h c)"), op=ALU.mult),
                  r=[PS(bkq), ("Et", dr)], w=[("aT", dr, bf)])
            for h in range(4):
                P.add("dve", lambda e, h=h, bkk=bkk: e.scalar_tensor_tensor(
                    out=Mt[dr][0][:, h, :], in0=ps[bkk][:, h * 128:(h + 1) * 128], scalar=sc["nbeta"][:, j, dr * 4 + h:dr * 4 + h + 1],
                    op0=ALU.mult, in1=tstr[dr][:, h, :], op1=ALU.mult), r=[PS(bkk), ("tstr", dr), "nbeta"], w=[("M", dr, 0)])
            yield
            bn = nb_()
            for h in range(4):
                P.add("pe", lambda e, h=h, bn=bn: e.transpose(ps[bn][:, h * 128:(h + 1) * 128], Mt[dr][0][:, h, :], ident_f[:]),
                      r=[("M", dr, 0)], w=[PS(bn)])
            yield
            P.add("act", lambda e, bn=bn: e.copy(Nt[dr][0][:].rearrange("p h c -> p (h c)"), ps[bn][:]), r=[PS(bn)], w=[("N", dr, 0)])
            Wc = Wf[dr]
            for h in range(4):
                P.add("dve", lambda e, h=h: e.tensor_tensor(out=Wc[:, h, :], in0=Mt[dr][0][:, h, :], in1=ident_f[:], op=ALU.add),
                      r=[("M", dr, 0)], w=[("Wf", dr)])
            yield
            cur = 0
            for lev in range(6):
                nxt = 1 - cur
                if lev < 5:
                    bx = nb_()
                    for h in range(4):
                        P.add("pe", lambda e, h=h, bx=bx, cur=cur: e.matmul(ps[bx][:, h * 128:(h + 1) * 128], Nt[dr][cur][:, h, :], Mt[dr][cur][:, h, :],
                                                                          start=True, stop=True), r=[("M", dr, cur), ("N", dr, cur)], w=[PS(bx)])
                by = nb_()
                for h in range(4):
                    P.add("pe", lambda e, h=h, by=by, cur=cur: e.matmul(ps[by][:, h * 128:(h + 1) * 128], Mt[dr][cur][:, h, :], Nt[dr][cur][:, h, :],
                                                                      start=True, stop=True), r=[("M", dr, cur), ("N", dr, cur)], w=[PS(by)])
                yield
                if lev < 5:
                    P.add("dve", lambda e, bx=bx, nxt=nxt: e.tensor_copy(Mt[dr][nxt][:].rearrange("p h c -> p (h c)"), ps[bx][:]),
                          r=[PS(bx)], w=[("M", dr, nxt)])
                P.add("act", lambda e, by=by, nxt=nxt: e.copy(Nt[dr][nxt][:].rearrange("p h c -> p (h c)"), ps[by][:]),
                      r=[PS(by)], w=[("N", dr, nxt)])
                yield
                bz = nb_()
                for h in range(4):
                    P.add("pe", lambda e, h=h, bz=bz, nxt=nxt: e.matmul(ps[bz][:, h * 128:(h + 1) * 128], Nt[dr][nxt][:, h, :], Wc[:, h, :],
                                                                      start=True, stop=True), r=[("N", dr, nxt), ("Wf", dr)], w=[PS(bz)])
                yield
                P.add("dve", lambda e, bz=bz: e.tensor_tensor(out=Wc[:].rearrange("p h c -> p (h c)"), in0=ps[bz][:],
                                                            in1=Wc[:].rearrange("p h c -> p (h c)"), op=ALU.add),
                      r=[PS(bz), ("Wf", dr)], w=[("Wf", dr)])
                yield
                cur = nxt
            P.add("act", lambda e: e.copy(Wt[dr][bf][:], Wf[dr][:]), r=[("Wf", dr)], w=[("W", dr, bf)])

        def chain(dr, j, bf, second):
            jsl = slice(j * 128, (j + 1) * 128)
            Tj = j // 4
            ba_, bb_ = nb_(), nb_()
            for h in range(4):
                P.add("pe", lambda e, h=h, ba_=ba_: e.matmul(ps[ba_][:, h * 128:(h + 1) * 128], kT[:, h, jsl], Sb[:, dr * 4 + h, :],
                                                            start=True, stop=True), r=[("kT", h, Tj), ("Sb", dr)], w=[PS(ba_)])
            for h in range(4):
                P.add("pe", lambda e, h=h, bb_=bb_: e.matmul(ps[bb_][:, h * 128:(h + 1) * 128], qT[:, h, jsl], Sb[:, dr * 4 + h, :],
                                                            start=True, stop=True), r=[("qT", h, Tj), ("Sb", dr)], w=[PS(bb_)])
            yield
            for h in range(4):
                P.add("dve", lambda e, h=h, ba_=ba_: e.scalar_tensor_tensor(
                    out=r0t[dr][:, h, :], in0=ps[ba_][:, h * 128:(h + 1) * 128], scalar=sc["negc"][:, j, dr * 4 + h:dr * 4 + h + 1],
                    op0=ALU.mult, in1=vtk[dr][bf][:, h, :], op1=ALU.add), r=[PS(ba_), ("vt", dr, bf), "negc"], w=[("r0", dr)])
            yield
            bc_ = nb_()
            for h in range(4):
                P.add("pe", lambda e, h=h, bc_=bc_: e.matmul(ps[bc_][:, h * 128:(h + 1) * 128], Wt[dr][bf][:, h, :], r0t[dr][:, h, :],
                                                            start=True, stop=True), r=[("W", dr, bf), ("r0", dr)], w=[PS(bc_)])
            yield
            for h in range(4):
                P.add("act", lambda e, h=h, bc_=bc_: e.activation(dlt[dr][:, h, :], ps[bc_][:, h * 128:(h + 1) * 128], AF.Copy,
                                                                scale=sc["beta"][:, j, dr * 4 + h:dr * 4 + h + 1]),
                      r=[PS(bc_), "beta"], w=[("dl", dr)])
            yield
            bd_, be_ = nb_(), nb_()
            for h in range(4):
                P.add("pe", lambda e, h=h, bd_=bd_: e.matmul(ps[bd_][:, h * 128:(h + 1) * 128], kdt[dr][bf][:, h, :], dlt[dr][:, h, :],
                                                            start=True, stop=True), r=[("kd", dr, bf), ("dl", dr)], w=[PS(bd_)])
            for h in range(4):
                P.add("pe", lambda e, h=h, be_=be_: e.matmul(ps[be_][:, h * 128:(h + 1) * 128], aTt[dr][bf][:, h, :], dlt[dr][:, h, :],
                                                            start=True, stop=True), r=[("aT", dr, bf), ("dl", dr)], w=[PS(be_)])
            for h in range(4):
                P.add("act", lambda e, h=h, bb_=bb_: e.activation(qsg[dr][:, h, :], ps[bb_][:, h * 128:(h + 1) * 128], AF.Copy,
                                                                scale=sc["egc"][:, j, dr * 4 + h:dr * 4 + h + 1]),
                      r=[PS(bb_), "egc"], w=[("Gbc", dr)])
            yield
            if not second:
                P.add("dve", lambda e, be_=be_: e.tensor_tensor(out=o_dnT[:, :, jsl], in0=ps[be_][:].rearrange("p (h c) -> p h c", h=4), in1=qsg[dr][:],
                                                              op=ALU.add), r=[PS(be_), ("Gbc", dr)], w=[("otok", j)])
            else:
                P.add("dve", lambda e, be_=be_: e.tensor_tensor(out=otf[dr][:].rearrange("p h c -> p (h c)"), in0=ps[be_][:],
                                                              in1=qsg[dr][:].rearrange("p h c -> p (h c)"), op=ALU.add),
                      r=[PS(be_), ("Gbc", dr)], w=[("tstr", dr)])
                P.add("dve", lambda e: e.tensor_tensor(out=otf[dr][:], in0=otf[dr][:],
                                                     in1=o_dnT[:, :, jsl], op=ALU.add), r=[("tstr", dr), ("otok", j)], w=[("tstr", dr)])
            for h in range(4):
                P.add("dve", lambda e, h=h, bd_=bd_: e.scalar_tensor_tensor(
                    out=Sf[:, dr * 4 + h, :], in0=Sf[:, dr * 4 + h, :], scalar=sc["egl"][:, j, dr * 4 + h:dr * 4 + h + 1],
                    op0=ALU.mult, in1=ps[bd_][:, h * 128:(h + 1) * 128], op1=ALU.add), r=[PS(bd_), ("Sf", dr), "egl"], w=[("Sf", dr)])
            P.add("act", lambda e: e.copy(Sb[:, dr * 4:dr * 4 + 4, :], Sf[:, dr * 4:dr * 4 + 4, :]), r=[("Sf", dr)], w=[("Sb", dr)])
            yield
            if second:
                for h in range(4):
                    P.add("act", lambda e, h=h: e.activation(junk[dr][:, h, :], otf[dr][:, h, :], AF.Square, accum_out=ssn[:, dr, h:h + 1]),
                          r=[("tstr", dr)], w=[("Gbc", dr), ("ssn", dr)])
                P.add("act", lambda e: e.activation(ssn[:, dr, 4:8], ssn[:, dr, 0:4], AF.Ln, bias=eps_t[:], scale=1.0 / 128),
                      r=[("ssn", dr)], w=[("ssn", dr)])
                P.add("act", lambda e: e.activation(ssn[:, dr, 4:8], ssn[:, dr, 4:8], AF.Exp, scale=-0.5), r=[("ssn", dr)], w=[("ssn", dr)])
                yield
                for h in range(4):
                    P.add("dve", lambda e, h=h: e.tensor_scalar(out=onb[dr][:, h, :], in0=otf[dr][:, h, :], scalar1=ssn[:, dr, 4 + h:5 + h],
                                                              scalar2=None, op0=ALU.mult), r=[("tstr", dr), ("ssn", dr)], w=[("r0", dr)])
                bt = nb_()
                for h in range(4):
                    P.add("pe", lambda e, h=h, bt=bt: e.transpose(psb(bt)[:, h * 128:(h + 1) * 128], onb[dr][:, h, :], ident_b[:]),
                          r=[("r0", dr)], w=[PS(bt)])
                yield
                P.add("dve", lambda e, bt=bt: e.tensor_scalar(
                    out=o_dnT[:, :, jsl], in0=psb(bt)[:, 0:512].rearrange("p (h c) -> p h c", h=4), scalar1=wdn[:, l:l + 1],
                    scalar2=None, op0=ALU.mult), r=[PS(bt), "wdn"], w=[("odn", h, Tj) for h in range(4)] + [("otok", j)])

        def stream(dr):
            jm = (lambda st: st) if dr == 0 else (lambda st: 15 - st)
            for step in range(17):
                if step < 16:
                    yield from pre(dr, jm(step), step % 2)
                if step > 0:
                    st = step - 1
                    yield from chain(dr, jm(st), st % 2, st >= 8)

        alive = [stream(0), stream(1)]
        while alive:
            for g_ in list(alive):
                try:
                    next(g_)
                except StopIteration:
                    alive.remove(g_)
        if l == 0:
            dump("qT", qT[:].rearrange("p h s -> p (h s)")); dump("kT", kT[:].rearrange("p h s -> p (h s)"))
            dump("vT", vT[:].rearrange("p h s -> p (h s)"))
            for nm_ in ("beta", "g", "gc", "gtot", "egc", "ekd", "egl"):
                dump("sc_" + nm_, sc[nm_][:].rearrange("p j c -> p (j c)"))
            dump("ba", ba[:].rearrange("p j c -> p (j c)"))
            dump("W00", Wt[0][0][:].rearrange("p h c -> p (h c)")); dump("aT00", aTt[0][0][:].rearrange("p h c -> p (h c)"))
            dump("kd00", kdt[0][0][:].rearrange("p h c -> p (h c)")); dump("Sf", Sf[:].rearrange("p h c -> p (h c)"))
            dump("odnT", o_dnT[:].rearrange("p h s -> p (h s)"))
            dump("Et", Et[0][:].rearrange("p h c -> p (h c)")); dump("M0", Mt[0][0][:].rearrange("p h c -> p (h c)"))
        P.barrier(bar[:])
        for T in range(4):
            norm_to_h(T, nwm[:, l, :], sqm)
        for h in range(4):
            b = h % 2
            P.add("pool", lambda e, b=b, h=h: e.dma_start(
                out=wz[b][:], in_=W["w_in"][l][:, 1536 + h * 128:1536 + (h + 1) * 128].rearrange("(k p) f -> p k f", p=128)),
                w=[("wz", b)], dma=True)
            for T in range(4):
                tsl = slice(T * 512, (T + 1) * 512)
                bank = nb_() % 4
                zb_ = nb_() % 2
                for k in range(KC):
                    P.add("pe", lambda e, k=k, bank=bank, b=b, tsl=tsl: e.matmul(
                        ps[bank][:], wz[b][:, k, :], hT[:, k, tsl], start=(k == 0), stop=(k == KC - 1)),
                        r=[("wz", b), hk(k, T)], w=[PS(bank)])
                P.add("act", lambda e, bank=bank, zb_=zb_: e.activation(zt[zb_][:], ps[bank][:], AF.Silu), r=[PS(bank)], w=[("zt", zb_)])
                P.add("dve", lambda e, h=h, tsl=tsl, zb_=zb_: e.tensor_tensor(out=o_dnT[:, h, tsl], in0=o_dnT[:, h, tsl], in1=zt[zb_][:],
                                                                           op=ALU.mult), r=[("zt", zb_), ("odn", h, T)], w=[("odn", h, T)])

    def mixer(l):
        for T in range(4):
            norm_to_h(T, nwm[:, l, :], sqm)
        P.barrier(bar[:])
        if do_dn:
            deltanet(l)
            if l == 0:
                dump("odnT2", o_dnT[:].rearrange("p h s -> p (h s)")); dump("zt0", zt[0][:]); dump("hT2", hT[:].rearrange("p k s -> p (k s)"))
        else:
            for h in range(4):
                for T in range(4):
                    P.add("dve", lambda e, h=h, T=T: e.memset(o_dnT[:, h, T * 512:(T + 1) * 512], 0.0),
                          w=[("odn", h, T)])
        P.barrier(bar[:])
        if do_da:
            attention(l)
        else:
            for h in range(4):
                for T in range(4):
                    P.add("dve", lambda e, h=h, T=T: e.memset(o_daT[:, h, T * 512:(T + 1) * 512], 0.0),
                          w=[("oda", h, T)])
        P.barrier(bar[:])
        if l == 0:
            dump("wq", wqkv[0][:, 0, :, :].rearrange("p k f -> p (k f)")); dump("wv", wqkv[0][:, 2, :, :].rearrange("p k f -> p (k f)"))
            dump("aq0", aqh[0][:]); dump("ak0", akh[0][:]); dump("av0", avh[0][:].rearrange("p a b -> p (a b)"))
            dump("pT0", pT[0][:]); dump("tmpf0", tmpf[0][:]); dump("rz", rz[:]); dump("o0", o0[:]); dump("oc", oc[:])
            dump("t1", t1[:]); dump("rstd", rstd[:]); dump("odaT", o_daT[:].rearrange("p h s -> p (h s)"))
            dump("hT", hT[:].rearrange("p k s -> p (k s)")); dump("neglam", neglam[:]); dump("sublnw", sublnw[:])
            P.barrier(bar[:])
        merge(l, do_dn, do_da)
        P.barrier(bar[:])

    for s in range(nseq):
        load_x(s)
        P.barrier(bar[:])
        for l in range(depth):
            if do_ffn:
                ffn(l, nw1, W["ffn1_wg"], W["ffn1_wu"], W["ffn1_wd"])
                P.barrier(bar[:])
            if do_mix:
                mixer(l)
            if do_ffn:
                ffn(l, nw2, W["ffn2_wg"], W["ffn2_wu"], W["ffn2_wd"])
                P.barrier(bar[:])
        store_y(s)

    P.emit(same_engine_sync=same_engine_sync)
    return nc, len(P.ops)


def make_consts():
    k = np.arange(128, dtype=np.float32)[:, None]
    q = np.arange(512, dtype=np.float32)[None, :]
    jj = np.arange(896, dtype=np.float32)[None, :]
    i = np.arange(128)
    incu = (i[:, None] <= i[None, :]).astype(np.float32)
    stru = (i[:, None] < i[None, :]).astype(np.float32)
    masks = np.stack([incu, stru, incu.T.copy(), stru.T.copy()]).astype(np.float32)
    return {"c_ident": np.eye(128, dtype=np.float32), "c_bs": np.ascontiguousarray(q - k),
            "c_abs": np.ascontiguousarray(np.abs(jj - 384.0 - k)), "c_masks": masks}


_CACHE = {}


def kernel(**inputs):
    xs = np.concatenate([np.asarray(inputs["x_prompt"], np.float32), np.asarray(inputs["x_sample"], np.float32)], axis=0)
    nb_p = inputs["x_prompt"].shape[0]
    if "nc" not in _CACHE:
        _CACHE["nc"] = build()[0]
    nc = _CACHE["nc"]
    wmap = {name: np.ascontiguousarray(np.asarray(inputs[name], np.float32)) for name, _ in WSHAPES}
    consts = make_consts()
    in_maps = []
    for c in range(NCORES):
        m = {"x": np.ascontiguousarray(xs[c * NSEQ:(c + 1) * NSEQ])}
        m.update(wmap)
        m.update(consts)
        in_maps.append(m)
    res = run_bass_kernel_spmd(nc, in_maps, core_ids=list(range(NCORES)))
    yfull = np.concatenate([np.asarray(r["y"], np.float32) for r in res.results], axis=0)
    return (np.ascontiguousarray(yfull[:nb_p]), np.ascontiguousarray(yfull[nb_p:]))
```

```python
import math
import numpy as np
import concourse.bass as bass
import concourse.mybir as mybir
from concourse.bass_utils import run_bass_kernel_spmd

F32 = mybir.dt.float32
BF16 = mybir.dt.bfloat16
AF = mybir.ActivationFunctionType
ALU = mybir.AluOpType

D = 1024
S = 2048
DFF = 2816
NIN = 5648
DEPTH = 4
NCORES = 8
NSEQ = 5
KC = 8
FCH = 22
NDSLOT = 8
HEADS = [0, 1, 2, 3]
ATT_LA, ATT_ED = 3, 2

WSHAPES = [
    ("ffn1_norm", [DEPTH, D]), ("ffn1_wg", [DEPTH, D, DFF]), ("ffn1_wu", [DEPTH, D, DFF]),
    ("ffn1_wd", [DEPTH, DFF, D]), ("mix_norm", [DEPTH, D]), ("w_in", [DEPTH, D, NIN]),
    ("conv_w", [DEPTH, 5, 1536]), ("dn_a_log", [DEPTH, 2, 4]), ("dn_dt_bias", [DEPTH, 2, 4]),
    ("dn_out_norm", [DEPTH, 128]), ("diff_lambda", [DEPTH, 4, 64]), ("diff_subln", [DEPTH, 128]),
    ("w_branch_dn", [DEPTH, 512, D]), ("w_branch_da", [DEPTH, 512, D]), ("w_out", [DEPTH, D, D]),
    ("ffn2_norm", [DEPTH, D]), ("ffn2_wg", [DEPTH, D, DFF]), ("ffn2_wu", [DEPTH, D, DFF]),
    ("ffn2_wd", [DEPTH, DFF, D]), ("final_norm", [D]),
]


class Prog:
    def __init__(self, nc):
        self.nc = nc
        self.ops = []
        self.keys = set()

    def add(self, eng, fn, r=(), w=(), dma=False):
        self.ops.append((eng, fn, tuple(r), tuple(w), dma))
        self.keys.update(r)
        self.keys.update(w)

    def barrier(self, ap):
        self.add("dve", lambda e: e.memset(ap, 0.0), w=list(self.keys) + ["__BAR__"])

    def emit(self, same_engine_sync=True):
        nc = self.nc
        ops = self.ops
        n = len(ops)
        last_w = {}
        rd = {}
        deps = [None] * n
        needed = [False] * n
        cur_bar = None
        for i, (eng, fn, r, w, dma) in enumerate(ops):
            d = set()
            for k in r:
                j = last_w.get(k, cur_bar)
                if j is not None:
                    d.add(j)
            for k in w:
                j = last_w.get(k, cur_bar)
                if j is not None:
                    d.add(j)
                rr = rd.get(k)
                if rr:
                    d.update(rr.values())
            d.discard(i)
            dd = []
            for j in d:
                je = ops[j][0]
                jd = ops[j][4]
                if not jd and not dma and je == eng:
                    if eng == "pe" or not same_engine_sync:
                        continue
                dd.append(j)
            deps[i] = dd
            for j in dd:
                needed[j] = True
            tag = ("d", i) if dma else eng
            for k in r:
                rd.setdefault(k, {})[tag] = i
            for k in w:
                last_w[k] = i
                rd[k] = {}
            if "__BAR__" in w:
                cur_bar = i
                last_w = {}
                rd = {}
        cnt = {"pe": 0, "dve": 0, "act": 0, "pool": 0}
        dcnt = {"sp": 0, "pool": 0, "act": 0}
        sig = [None] * n
        dslot = [None] * n
        for i, (eng, fn, r, w, dma) in enumerate(ops):
            if dma:
                k = dcnt[eng]
                dcnt[eng] += 1
                slot = k % NDSLOT
                sig[i] = (("dma", eng, slot), 16 * (k // NDSLOT + 1))
                dslot[i] = (slot, 16 * (k // NDSLOT))
            elif needed[i]:
                cnt[eng] += 1
                sig[i] = (("c", eng), cnt[eng])
        per_eng = {"pe": [], "dve": [], "act": [], "pool": [], "sp": []}
        for i, op in enumerate(ops):
            per_eng[op[0]].append(i)
        semkeys = [("c", e) for e in ("pe", "dve", "act", "pool")]
        for q in ("sp", "pool", "act"):
            if dcnt[q]:
                semkeys += [("dma", q, s) for s in range(NDSLOT)]
        from contextlib import ExitStack
        with ExitStack() as es:
            sems = {}
            for sk in semkeys:
                sems[sk] = es.enter_context(nc.semaphore("s_" + "_".join(str(t) for t in sk)))
            block = es.enter_context(nc.Block())

            def run_engine(ename, e):
                waited = {}
                final = {}
                for i in per_eng[ename]:
                    eng, fn, r, w, dma = ops[i]
                    if dma:
                        slot, prev = dslot[i]
                        sk = ("dma", eng, slot)
                        if prev > 0 and waited.get(sk, 0) < prev:
                            e.wait_ge(sems[sk], prev)
                            waited[sk] = prev
                    need = {}
                    for j in deps[i]:
                        sk, v = sig[j]
                        if need.get(sk, 0) < v:
                            need[sk] = v
                    for sk, v in need.items():
                        if waited.get(sk, 0) < v:
                            e.wait_ge(sems[sk], v)
                            waited[sk] = v
                    ins = fn(e)
                    if sig[i] is not None:
                        sk, v = sig[i]
                        if dma:
                            ins.then_inc(sems[sk], 16)
                            final[sk] = v
                        else:
                            ins.then_inc(sems[sk], 1)
                for sk, v in final.items():
                    if waited.get(sk, 0) < v:
                        e.wait_ge(sems[sk], v)

            @block.tensor
            def _(e):
                run_engine("pe", e)

            @block.vector
            def _(e):
                run_engine("dve", e)

            @block.scalar
            def _(e):
                run_engine("act", e)

            @block.gpsimd
            def _(e):
                run_engine("pool", e)

            @block.sync
            def _(e):
                run_engine("sp", e)


def build(nseq=NSEQ, depth=DEPTH, do_ffn=True, do_mix=True, do_dn=True, do_da=True, same_engine_sync=True, debug=False):
    nc = bass.Bass("TRN2", target_bir_lowering=False)
    x = nc.dram_tensor("x", [nseq, S, D], F32, kind="ExternalInput").ap()
    y = nc.dram_tensor("y", [nseq, S, D], F32, kind="ExternalOutput").ap()
    W = {}
    for name, shape in WSHAPES:
        W[name] = nc.dram_tensor(name, shape, F32, kind="ExternalInput").ap()
    c_ident = nc.dram_tensor("c_ident", [128, 128], F32, kind="ExternalInput").ap()
    c_bs = nc.dram_tensor("c_bs", [128, 512], F32, kind="ExternalInput").ap()
    c_abs = nc.dram_tensor("c_abs", [128, 896], F32, kind="ExternalInput").ap()
    c_masks = nc.dram_tensor("c_masks", [4, 128, 128], F32, kind="ExternalInput").ap()
    c_kaug = nc.dram_tensor("c_kaug", [128, 128], F32, kind="ExternalInput").ap()
    c_qaug = nc.dram_tensor("c_qaug", [4, 2, 128, 512], F32, kind="ExternalInput").ap()

    P = Prog(nc)

    def dump(nm, ap):
        if not debug:
            return
        dt_ = nc.dram_tensor("dbg_" + nm, [ap.shape[0], ap.shape[1]], F32, kind="ExternalOutput").ap()
        P.add("pool", lambda e: e.dma_start(out=dt_, in_=ap), r=list(P.keys), w=[("dbg", nm)], dma=True)
    off = [16512]

    def sb(name, shape, dtype, at=None):
        esz = 4 if dtype == F32 else 2
        nb = esz
        for s_ in shape[1:]:
            nb *= s_
        o = off[0] if at is None else at
        t = nc.alloc_sbuf_tensor_at(name, list(shape), dtype, offset=o)
        if at is None:
            off[0] = o + ((nb + 31) // 32) * 32
        return t

    ident_f = sb("ident_f", [128, 128], F32)
    ident_b = sb("ident_b", [128, 128], BF16)
    ones_f = sb("ones_f", [128, 128], F32)
    ones_b = sb("ones_b", [128, 128], BF16)
    eps_t = sb("eps_t", [128, 1], F32)
    nw1 = sb("nw1", [128, DEPTH, KC], F32)
    nwm = sb("nwm", [128, DEPTH, KC], F32)
    nw2 = sb("nw2", [128, DEPTH, KC], F32)
    nwf = sb("nwf", [128, KC], F32)
    bar = sb("bar", [128, 8], F32)
    eps5_t = sb("eps5_t", [128, 1], F32)
    bs_t = sb("bs_t", [128, 512], F32)
    abs_t = sb("abs_t", [128, 896], F32)
    sublnw = sb("sublnw", [128, DEPTH], F32)
    neglam = sb("neglam", [128, DEPTH], F32)
    lam_s = sb("lam_s", [128, 4], F32)
    kaug_t = sb("kaug_t", [128, 128], BF16)
    convw = sb("convw", [128, DEPTH, 12, 5], F32)
    wdn = sb("wdn", [128, DEPTH], F32)
    R_x = off[0]
    xT = sb("xT", [128, KC, S], F32)
    R_h = off[0]
    hT = sb("hT", [128, KC, S], BF16)
    R_in = off[0]
    off[0] += 48 * 1024
    R_o = off[0]
    off[0] += 32 * 1024
    R_free = off[0]
    assert off[0] <= 229344, off[0]
    aT = sb("aT", [128, FCH, 1024], BF16, at=R_in)
    o_ = R_o
    wgu = []
    for b in range(2):
        wgu.append(sb(f"wgu{b}", [128, KC, 2, 256], BF16, at=o_))
        o_ += 8192
    wdb = []
    for b in range(4):
        wdb.append(sb(f"wdb{b}", [128, D], BF16, at=o_))
        o_ += 2048
    sq = [sb(f"sq{b}", [128, 512], F32, at=o_ + 2048 * b) for b in range(2)]
    o_ += 4096
    sgt = [sb(f"sgt{b}", [128, 512], F32, at=o_ + 2048 * b) for b in range(2)]
    o_ += 4096
    assert o_ <= R_free
    lnt = sb("lnt", [128, 512], F32, at=R_free)
    rstd = sb("rstd", [128, 512], F32, at=R_free + 2048)
    xin = [sb(f"xin{b}", [128, D], F32, at=R_in + 4096 * b) for b in range(2)]
    yo = [sb(f"yo{b}", [128, D], F32, at=R_in + 8192 + 4096 * b) for b in range(2)]
    yTf = sb("yTf", [128, KC, 512], F32, at=R_h)

    ps = [nc.alloc_psum_tensor(f"ps{b}", [128, 512], F32) for b in range(8)]

    def PS(b):
        return ("ps", b)

    P.add("sp", lambda e: e.dma_start(out=ident_f[:], in_=c_ident), w=["ident_f"], dma=True)
    P.add("pool", lambda e: e.dma_start(out=ident_b[:], in_=c_ident), w=["ident_b"], dma=True)
    P.add("dve", lambda e: e.memset(ones_f[:], 1.0), w=["ones_f"])
    P.add("dve", lambda e: e.memset(ones_b[:], 1.0), w=["ones_b"])
    P.add("dve", lambda e: e.memset(eps_t[:], 1e-6), w=["eps_t"])
    for t_, nm in ((nw1, "ffn1_norm"), (nwm, "mix_norm"), (nw2, "ffn2_norm")):
        P.add("sp", lambda e, t_=t_, nm=nm: e.dma_start(
            out=t_[:], in_=W[nm].rearrange("l (c p) -> p l c", p=128), allow_slow_non_contiguous=True),
            w=[nm], dma=True)
    P.add("sp", lambda e: e.dma_start(out=nwf[:], in_=W["final_norm"].rearrange("(c p) -> p c", p=128),
                                      allow_slow_non_contiguous=True), w=["final_norm"], dma=True)

    P.add("dve", lambda e: e.memset(eps5_t[:], 1e-5), w=["eps5_t"])
    P.add("pool", lambda e: e.dma_start(out=kaug_t[:], in_=c_kaug), w=["kaug_t"], dma=True)
    P.add("sp", lambda e: e.dma_start(out=bs_t[:], in_=c_bs), w=["bs_t"], dma=True)
    P.add("sp", lambda e: e.dma_start(out=abs_t[:], in_=c_abs), w=["abs_t"], dma=True)
    P.add("sp", lambda e: e.dma_start(out=sublnw[:], in_=W["diff_subln"].rearrange("l p -> p l"),
                                      allow_slow_non_contiguous=True), w=["sublnw"], dma=True)
    for l_ in range(DEPTH):
        for kk_ in range(5):
            P.add("sp", lambda e, l_=l_, kk_=kk_: e.dma_start(out=convw[:, l_, :, kk_], in_=W["conv_w"][l_, kk_].rearrange("(c p) -> p c", p=128),
                                                  allow_slow_non_contiguous=True), w=["convw"], dma=True)
    P.add("sp", lambda e: e.dma_start(out=wdn[:], in_=W["dn_out_norm"].rearrange("l p -> p l"),
                                      allow_slow_non_contiguous=True), w=["wdn"], dma=True)
    lpb = sb("lpb", [128, DEPTH, 4, 64], F32, at=R_free)
    lpt = sb("lpt", [128, 64], F32, at=R_free + 4096)
    P.add("sp", lambda e: e.dma_start(out=lpb[:].rearrange("p l a d -> p (l a d)"),
                                      in_=W["diff_lambda"].rearrange("l a d -> (l a d)").partition_broadcast(128)),
          w=["lpb"], dma=True)
    for l in range(DEPTH):
        li = 0.8 - 0.6 * math.exp(-0.3 * l)
        for a in range(2):
            P.add("dve", lambda e, l=l, a=a: e.tensor_tensor(out=lpt[:], in0=lpb[:, l, 2 * a, :], in1=lpb[:, l, 2 * a + 1, :],
                                                          op=ALU.mult), r=["lpb"], w=["lpt"])
            P.add("dve", lambda e, a=a: e.reduce_sum(lam_s[:, a:a + 1], lpt[:], axis=mybir.AxisListType.X),
                  r=["lpt"], w=["lam_s"])
        P.add("act", lambda e: e.activation(lam_s[:, 2:4], lam_s[:, 0:2], AF.Exp), r=["lam_s"], w=["lam_s"])
        P.add("dve", lambda e, l=l: e.tensor_tensor(out=neglam[:, l:l + 1], in0=lam_s[:, 3:4], in1=lam_s[:, 2:3],
                                                 op=ALU.subtract), r=["lam_s"], w=["neglam"])
        P.add("dve", lambda e, l=l, li=li: e.tensor_scalar(out=neglam[:, l:l + 1], in0=neglam[:, l:l + 1], scalar1=-li,
                                                        scalar2=None, op0=ALU.add), r=["neglam"], w=["neglam"])
        P.add("dve", lambda e, l=l, li=li: e.tensor_scalar(out=sublnw[:, l:l + 1], in0=sublnw[:, l:l + 1], scalar1=1.0 - li,
                                                        scalar2=None, op0=ALU.mult), r=["sublnw"], w=["sublnw"])
    P.barrier(bar[:])

    rr = {"evac": 0, "bank": 0}

    def evac_eng():
        rr["evac"] ^= 1
        return "dve" if rr["evac"] else "act"

    def copy_op(eng, out, in_, r, w):
        if eng == "act":
            P.add("act", lambda e: e.copy(out, in_), r=r, w=w)
        else:
            P.add(eng, lambda e: e.tensor_copy(out, in_), r=r, w=w)

    def xk(c, t):
        return ("x", c, t)

    def hk(c, t):
        return ("h", c, t)

    def load_x(s):
        for j in range(16):
            b = j % 2
            T = j // 4
            P.add("sp", lambda e, b=b, j=j: e.dma_start(out=xin[b][:], in_=x[s, j * 128:(j + 1) * 128, :]),
                  r=(), w=[("xin", b)], dma=True)
            for half in range(2):
                bank = (2 * j + half) % 4
                for i in range(4):
                    c = 4 * half + i
                    P.add("pe", lambda e, bank=bank, i=i, b=b, c=c: e.transpose(
                        ps[bank][:, i * 128:(i + 1) * 128], xin[b][:, c * 128:(c + 1) * 128], ident_f[:]),
                        r=[("xin", b), "ident_f"], w=[PS(bank)])
                copy_op(evac_eng(), xT[:, 4 * half:4 * half + 4, j * 128:(j + 1) * 128],
                        ps[bank][:].rearrange("p (a t) -> p a t", a=4),
                        r=[PS(bank)], w=[xk(4 * half + i, T) for i in range(4)])

    sqm = [sb(f"sqm{b}", [128, 512], F32, at=R_free + 4096 + 2048 * b) for b in range(2)]

    def rstd_for_tile(T, eps_tile, scale, sq=sq):
        tsl = slice(T * 512, (T + 1) * 512)
        for c in range(KC):
            b = c % 2
            P.add("act", lambda e, c=c, b=b: e.activation(sq[b][:], xT[:, c, tsl], AF.Square),
                  r=[xk(c, T)], w=[("sq", b)])
            P.add("pe", lambda e, c=c, b=b: e.matmul(ps[7][:], ones_f[:], sq[b][:], start=(c == 0), stop=(c == KC - 1)),
                  r=[("sq", b), "ones_f"], w=[PS(7)])
        P.add("act", lambda e: e.activation(lnt[:], ps[7][:], AF.Ln, bias=eps_tile[:], scale=scale),
              r=[PS(7), "eps_t"], w=["lnt"])
        P.add("act", lambda e: e.activation(rstd[:], lnt[:], AF.Exp, scale=-0.5), r=["lnt"], w=["rstd"])

    def norm_to_h(T, nw_ap, sq=sq):
        tsl = slice(T * 512, (T + 1) * 512)
        rstd_for_tile(T, eps_t, 1.0 / D, sq)
        for c in range(KC):
            P.add("dve", lambda e, c=c: e.scalar_tensor_tensor(
                out=hT[:, c, tsl], in0=xT[:, c, tsl], scalar=nw_ap[:, c:c + 1], op0=ALU.mult,
                in1=rstd[:], op1=ALU.mult), r=[xk(c, T), "rstd"], w=[hk(c, T)])

    def ffn(l, nw, wg, wu, wd):
        nw_ap = nw[:, l, :]
        for half in range(2):
            tiles = [2 * half, 2 * half + 1]
            for T in tiles:
                norm_to_h(T, nw_ap)
            for fg in range(FCH // 2):
                b = fg % 2
                for gi, wsrc in enumerate((wg, wu)):
                    P.add("pool", lambda e, b=b, gi=gi, wsrc=wsrc, fg=fg: e.dma_start(
                        out=wgu[b][:, :, gi, :],
                        in_=wsrc[l][:, fg * 256:(fg + 1) * 256].rearrange("(k p) f -> p k f", p=128)),
                        w=[("wgu", b, gi)], dma=True)
                for fc in range(2):
                    f = 2 * fg + fc
                    for T in tiles:
                        tsl = slice(T * 512, (T + 1) * 512)
                        tl = slice((T - 2 * half) * 512, (T - 2 * half + 1) * 512)
                        gb = rr["bank"] % 4
                        ub = 4 + rr["bank"] % 3
                        rr["bank"] += 1
                        for gi, bank in ((0, gb), (1, ub)):
                            for k in range(KC):
                                P.add("pe", lambda e, k=k, gi=gi, bank=bank, b=b, fc=fc, tsl=tsl: e.matmul(
                                    ps[bank][:], wgu[b][:, k, gi, fc * 128:(fc + 1) * 128], hT[:, k, tsl],
                                    start=(k == 0), stop=(k == KC - 1)),
                                    r=[("wgu", b, gi), hk(k, T)], w=[PS(bank)])
                        sb_ = rr["bank"] % 2
                        P.add("act", lambda e, gb=gb, sb_=sb_: e.activation(sgt[sb_][:], ps[gb][:], AF.Silu),
                              r=[PS(gb)], w=[("sgt", sb_)])
                        P.add("dve", lambda e, ub=ub, sb_=sb_, f=f, tl=tl: e.tensor_tensor(
                            out=aT[:, f, tl], in0=ps[ub][:], in1=sgt[sb_][:], op=ALU.mult),
                            r=[PS(ub), ("sgt", sb_)], w=[("a", f, T)])
            for T in tiles:
                tsl = slice(T * 512, (T + 1) * 512)
                tl = slice((T - 2 * half) * 512, (T - 2 * half + 1) * 512)
                for f in range(FCH):
                    b = rr.setdefault("wd", 0) % 4
                    rr["wd"] += 1
                    P.add("pool", lambda e, b=b, f=f: e.dma_start(out=wdb[b][:], in_=wd[l][f * 128:(f + 1) * 128, :]),
                          w=[("wdb", b)], dma=True)
                    for d in range(KC):
                        P.add("pe", lambda e, b=b, d=d, f=f, tl=tl: e.matmul(
                            ps[d][:], wdb[b][:, d * 128:(d + 1) * 128], aT[:, f, tl],
                            start=(f == 0), stop=(f == FCH - 1)),
                            r=[("wdb", b), ("a", f, T)], w=[PS(d)])
                for d in range(KC):
                    P.add("dve", lambda e, d=d, tsl=tsl: e.scalar_tensor_tensor(
                        out=xT[:, d, tsl], in0=ps[d][:], scalar=0.5, op0=ALU.mult, in1=xT[:, d, tsl], op1=ALU.add),
                        r=[PS(d), xk(d, T)], w=[xk(d, T)])

    def store_y(s):
        for T in range(4):
            tsl = slice(T * 512, (T + 1) * 512)
            rstd_for_tile(T, eps_t, 1.0 / D)
            for c in range(KC):
                P.add("dve", lambda e, c=c, tsl=tsl: e.scalar_tensor_tensor(
                    out=yTf[:, c, :], in0=xT[:, c, tsl], scalar=nwf[:, c:c + 1], op0=ALU.mult,
                    in1=rstd[:], op1=ALU.mult), r=[xk(c, T), "rstd"], w=[("yTf", c)])
            for j4 in range(4):
                j = 4 * T + j4
                b = j % 2
                for half in range(2):
                    bank = (2 * j + half) % 4
                    for i in range(4):
                        c = 4 * half + i
                        P.add("pe", lambda e, bank=bank, i=i, c=c, j4=j4: e.transpose(
                            ps[bank][:, i * 128:(i + 1) * 128], yTf[:, c, j4 * 128:(j4 + 1) * 128], ident_f[:]),
                            r=[("yTf", c), "ident_f"], w=[PS(bank)])
                    copy_op(evac_eng(), yo[b][:, half * 512:(half + 1) * 512], ps[bank][:],
                            r=[PS(bank)], w=[("yo", b, half)])
                P.add("sp", lambda e, b=b, j=j: e.dma_start(out=y[s, j * 128:(j + 1) * 128, :], in_=yo[b][:]),
                      r=[("yo", b, 0), ("yo", b, 1)], w=[("y", s, j)], dma=True)


    SLOPES = [2.0 ** (-8.0 * (h + 1) / 4) for h in range(4)]
    OQ, OK_, OV, OG = 2064, 2576, 3088, 3600
    o_daT = sb("o_daT", [128, 4, S], BF16, at=R_o)
    o_dnT = sb("o_dnT", [128, 4, S], BF16, at=R_o + 16384)
    a_o = R_in
    wqkv = []
    for b in range(2):
        wqkv.append(sb(f"wqkv{b}", [128, 3, KC, 128], BF16, at=a_o))
        a_o += 6144
    aqh1 = sb("aqh", [128, S], BF16, at=a_o)
    a_o += 4096
    akm = [sb(f"akm{m}", [128, S], BF16, at=a_o + 4096 * m) for m in range(2)]
    a_o += 8192
    avh1 = sb("avh", [128, 16, 128], BF16, at=a_o)
    a_o += 4096
    aqh = [aqh1, aqh1]
    avh = [avh1, avh1]
    pT = [sb(f"pT{b}", [128, 512], BF16, at=a_o + 1024 * b) for b in range(4)]
    a_o += 4096
    tmpf = [sb(f"tmpf{b}", [128, 512], F32, at=a_o + 2048 * b) for b in range(3)]
    a_o += 6144
    qaug_t = sb("qaug_t", [128, 2, 512], BF16, at=a_o)
    a_o += 2048
    assert a_o <= R_in + 48 * 1024, a_o
    f_o = R_free + 4096
    rz = sb("rz", [128, 512], F32, at=f_o)
    o0 = sb("o0", [128, 512], F32, at=f_o + 2048)
    t1 = sb("t1", [128, 512], F32, at=f_o + 4096)
    oc = sb("oc", [128, 512], F32, at=f_o + 6144)
    sqa = sb("sqa", [128, 512], F32, at=f_o + 8192)
    f_o += 10240
    tmpf.append(sb("tmpf3", [128, 512], F32, at=f_o))
    f_o += 2048
    for b_ in range(4, 6):
        pT.append(sb(f"pT{b_}", [128, 512], BF16, at=f_o))
        f_o += 1024
    assert f_o <= 229344, f_o

    def attention(l):
        P.add("dve", lambda e: e.memset(akm[0][64:128, :], 0.0), w=[("akz", 0)])
        P.add("dve", lambda e: e.memset(akm[1][0:64, :], 0.0), w=[("akz", 1)])
        for h in HEADS:
            b = h % 2
            for wi, o_col in enumerate((OQ, OK_, OV)):
                P.add("pool", lambda e, b=b, wi=wi, o_col=o_col, h=h: e.dma_start(
                    out=wqkv[b][:, wi, :, :],
                    in_=W["w_in"][l][:, o_col + h * 128:o_col + (h + 1) * 128].rearrange("(k p) f -> p k f", p=128)),
                    w=[("wqkv", b, wi)], dma=True)
            P.add("pool", lambda e, h=h: e.dma_start(out=qaug_t[:], in_=c_qaug[h].rearrange("s p q -> p s q")),
                  w=["qaug"], dma=True)
            for wi, dst, scl in ((0, aqh[b], 0.125), (1, None, 1.0)):
                for T in range(4):
                    tsl = slice(T * 512, (T + 1) * 512)
                    bank = rr["bank"] % 3
                    rr["bank"] += 1
                    for k in range(KC):
                        P.add("pe", lambda e, k=k, bank=bank, wi=wi, b=b, tsl=tsl: e.matmul(
                            ps[bank][:], wqkv[b][:, wi, k, :], hT[:, k, tsl], start=(k == 0), stop=(k == KC - 1)),
                            r=[("wqkv", b, wi), hk(k, T)], w=[PS(bank)])
                    if dst is not None:
                        P.add("act", lambda e, dst=dst, tsl=tsl, bank=bank, scl=scl: e.activation(
                            dst[:, tsl], ps[bank][:], AF.Copy, scale=scl), r=[PS(bank)], w=[("aqk", wi, 0, T)])
                    else:
                        P.add("act", lambda e, tsl=tsl, bank=bank: e.copy(akm[0][0:64, tsl], ps[bank][0:64, :]),
                              r=[PS(bank)], w=[("aqk", 1, 0, T)])
                        P.add("dve", lambda e, tsl=tsl, bank=bank: e.tensor_copy(akm[1][64:128, tsl], ps[bank][64:128, :]),
                              r=[PS(bank)], w=[("aqk", 1, 1, T)])
            for g in range(4):
                bank = rr["bank"] % 3
                rr["bank"] += 1
                for jj in range(4):
                    j = 4 * g + jj
                    for k in range(KC):
                        P.add("pe", lambda e, k=k, bank=bank, jj=jj, j=j, b=b: e.matmul(
                            ps[bank][:, jj * 128:(jj + 1) * 128], hT[:, k, j * 128:(j + 1) * 128], wqkv[b][:, 2, k, :],
                            start=(k == 0), stop=(k == KC - 1)),
                            r=[("wqkv", b, 2), hk(k, j // 4)], w=[PS(bank)])
                P.add("dve", lambda e, bank=bank, g=g, b=b: e.tensor_copy(
                    avh[b][:, 4 * g:4 * g + 4, :], ps[bank][:].rearrange("p (a t) -> p a t", a=4)),
                    r=[PS(bank)], w=[("avh", 0, g)])
            slope = SLOPES[h]
            tiles = []
            for G in range(4):
                gsl = slice(G * 512, (G + 1) * 512)
                for m in range(2):
                    msl = slice(64 * m, 64 * m + 64)
                    ob, zb = 4 + m, 6 + m
                    plan = []
                    for j in range(16):
                        r_ = j - 4 * G
                        off_ = 512 * G - 128 * j
                        if 0 <= r_ <= 3:
                            src = abs_t[:, 384 - 128 * r_:896 - 128 * r_]
                            coef, cb = -slope, 0.0
                            dmin, dmax = 0, max(128 * r_ + 127, 511 - 128 * r_)
                            sgn = None
                        elif off_ > 0:
                            src = bs_t[:]
                            coef, cb = -slope, -slope * off_
                            dmin, dmax = off_ - 127, off_ + 511
                            sgn = 0
                        else:
                            src = bs_t[:]
                            coef, cb = slope, slope * off_
                            dmin, dmax = -off_ - 511, -off_ + 127
                            sgn = 1
                        if -slope * dmin < -80.0:
                            continue
                        plan.append((j, src, coef, cb, sgn))
                    for (j, src, coef, cb, sgn) in plan:
                        first = (j == plan[0][0])
                        last = (j == plan[-1][0])
                        ti = len(tiles)
                        sbk, tb, pb = ti % 4, ti % 4, ti % 6

                        def fA(sbk=sbk, msl=msl, j=j, gsl=gsl, G=G, b=b, sgn=sgn):
                            mm_ = msl.start // 64
                            P.add("pe", lambda e: e.matmul(
                                ps[sbk][:], akm[mm_][:, j * 128:(j + 1) * 128], aqh[b][:, gsl], start=True, stop=(sgn is None)),
                                r=[("aqk", 0, 0, G), ("aqk", 1, mm_, j // 4), ("akz", mm_)], w=[PS(sbk)])
                            if sgn is not None:
                                P.add("pe", lambda e: e.matmul(
                                    ps[sbk][:], kaug_t[:, :], qaug_t[:, sgn, :], start=False, stop=True),
                                    r=["kaug_t", "qaug"], w=[PS(sbk)])

                        def fB(src=src, coef=coef, sbk=sbk, tb=tb, pb=pb, cb=cb, sgn=sgn):
                            if sgn is None:
                                P.add("dve", lambda e: e.scalar_tensor_tensor(
                                    out=tmpf[tb][:], in0=src, scalar=coef, op0=ALU.mult, in1=ps[sbk][:], op1=ALU.add),
                                    r=[PS(sbk)], w=[("tmpf", tb)])
                                P.add("act", lambda e: e.activation(
                                    pT[pb][:], tmpf[tb][:], AF.Exp, bias=cb), r=[("tmpf", tb)], w=[("pT", pb)])
                            else:
                                P.add("act", lambda e: e.activation(
                                    pT[pb][:], ps[sbk][:], AF.Exp, bias=cb), r=[PS(sbk)], w=[("pT", pb)])

                        def fC(ob=ob, zb=zb, j=j, pb=pb, first=first, last=last, b=b):
                            P.add("pe", lambda e: e.matmul(
                                ps[ob][:], avh[b][:, j, :], pT[pb][:], start=first, stop=last),
                                r=[("avh", 0, j // 4), ("pT", pb)], w=[PS(ob)])
                            P.add("pe", lambda e: e.matmul(
                                ps[zb][:], ones_b[:], pT[pb][:], start=first, stop=last),
                                r=[("pT", pb)], w=[PS(zb)])

                        fE = None
                        if last:
                            def fE(ob=ob, zb=zb, m=m, gsl=gsl, G=G, h=h):
                                P.add("dve", lambda e: e.reciprocal(rz[:], ps[zb][:]), r=[PS(zb)], w=["rz"])
                                if m == 0:
                                    P.add("dve", lambda e: e.tensor_tensor(out=o0[:], in0=ps[ob][:], in1=rz[:], op=ALU.mult),
                                          r=[PS(ob), "rz"], w=["o0"])
                                    return
                                P.add("dve", lambda e: e.tensor_tensor(out=t1[:], in0=ps[ob][:], in1=rz[:], op=ALU.mult),
                                      r=[PS(ob), "rz"], w=["t1"])
                                P.add("dve", lambda e: e.scalar_tensor_tensor(
                                    out=oc[:], in0=t1[:], scalar=neglam[:, l:l + 1], op0=ALU.mult, in1=o0[:], op1=ALU.add),
                                    r=["t1", "o0", "neglam"], w=["oc"])
                                P.add("act", lambda e: e.activation(sqa[:], oc[:], AF.Square), r=["oc"], w=["sqa"])
                                P.add("pe", lambda e: e.matmul(ps[5][:], ones_f[:], sqa[:], start=True, stop=True),
                                      r=["sqa"], w=[PS(5)])
                                P.add("act", lambda e: e.activation(lnt[:], ps[5][:], AF.Ln, bias=eps5_t[:], scale=1.0 / 128),
                                      r=[PS(5)], w=["lnt"])
                                P.add("act", lambda e: e.activation(rstd[:], lnt[:], AF.Exp, scale=-0.5), r=["lnt"], w=["rstd"])
                                P.add("dve", lambda e: e.scalar_tensor_tensor(
                                    out=o_daT[:, h, gsl], in0=oc[:], scalar=sublnw[:, l:l + 1], op0=ALU.mult, in1=rstd[:],
                                    op1=ALU.mult), r=["oc", "rstd", "sublnw"], w=[("oda", h, G)])
                        tiles.append((fA, fB, fC, fE))
            LA, ED = ATT_LA, ATT_ED
            nt = len(tiles)
            pend = []
            for idx in range(nt + LA + ED + 1):
                if idx < nt:
                    tiles[idx][0]()
                i = idx - LA
                if 0 <= i < nt:
                    tiles[i][1]()
                    tiles[i][2]()
                    if tiles[i][3] is not None:
                        pend.append((idx + ED, tiles[i][3]))
                while pend and pend[0][0] <= idx:
                    pend.pop(0)[1]()

    mT = sb("mT", [128, KC, S], BF16, at=R_in)
    wbr = [sb(f"wbr{i}", [128, 4, D], BF16, at=R_in + 32768 + 8192 * i) for i in range(2)]
    m_o = R_free + 4096
    gw = [sb(f"gw{b}", [128, KC, 2, 128], BF16, at=m_o + 4096 * b) for b in range(2)]
    m_o += 8192
    sg_ = [sb(f"sg{b}", [128, 512], F32, at=m_o + 2048 * b) for b in range(4)]
    m_o += 8192
    assert m_o <= 229344, m_o

    def merge(l, use_dn, use_da):
        for i, nm in enumerate(("w_branch_dn", "w_branch_da")):
            P.add("pool", lambda e, i=i, nm=nm: e.dma_start(
                out=wbr[i][:], in_=W[nm][l].rearrange("(k p) f -> p k f", p=128)), w=[("wbr", i)], dma=True)
        for dc in range(KC):
            b = dc % 2
            for gi in range(2):
                P.add("pool", lambda e, b=b, gi=gi, dc=dc: e.dma_start(
                    out=gw[b][:, :, gi, :],
                    in_=W["w_in"][l][:, OG + gi * D + dc * 128:OG + gi * D + (dc + 1) * 128].rearrange(
                        "(k p) f -> p k f", p=128)), w=[("gw", b, gi)], dma=True)
            for T in range(4):
                tsl = slice(T * 512, (T + 1) * 512)
                base = 4 * (rr["bank"] % 2)
                rr["bank"] += 1
                for i, (src, key) in enumerate(((o_dnT, "odn"), (o_daT, "oda"))):
                    for hc in range(4):
                        P.add("pe", lambda e, i=i, hc=hc, src=src, base=base, dc=dc, tsl=tsl: e.matmul(
                            ps[base + i][:], wbr[i][:, hc, dc * 128:(dc + 1) * 128], src[:, hc, tsl],
                            start=(hc == 0), stop=(hc == 3)),
                            r=[("wbr", i), (key, hc, T)], w=[PS(base + i)])
                for gi in range(2):
                    for k in range(KC):
                        P.add("pe", lambda e, gi=gi, k=k, b=b, base=base, tsl=tsl: e.matmul(
                            ps[base + 2 + gi][:], gw[b][:, k, gi, :], hT[:, k, tsl], start=(k == 0), stop=(k == KC - 1)),
                            r=[("gw", b, gi), hk(k, T)], w=[PS(base + 2 + gi)])
                for gi in range(2):
                    P.add("act", lambda e, gi=gi, base=base: e.activation(sg_[gi][:], ps[base + 2 + gi][:], AF.Sigmoid),
                          r=[PS(base + 2 + gi)], w=[("sg", gi)])
                for gi in range(2):
                    P.add("dve", lambda e, gi=gi, base=base: e.tensor_tensor(
                        out=sg_[2 + gi][:], in0=ps[base + gi][:], in1=sg_[gi][:], op=ALU.mult),
                        r=[PS(base + gi), ("sg", gi)], w=[("sg", 2 + gi)])
                P.add("dve", lambda e, dc=dc, tsl=tsl: e.tensor_tensor(
                    out=mT[:, dc, tsl], in0=sg_[2][:], in1=sg_[3][:], op=ALU.add),
                    r=[("sg", 2), ("sg", 3)], w=[("m", dc, T)])
        for do in range(KC):
            b = do % 2
            P.add("pool", lambda e, b=b, do=do: e.dma_start(
                out=gw[b][:].rearrange("p k g f -> p k (g f)")[:, :, 0:128],
                in_=W["w_out"][l][:, do * 128:(do + 1) * 128].rearrange("(k p) f -> p k f", p=128)),
                w=[("gw", b, 0), ("gw", b, 1)], dma=True)
            for T in range(4):
                tsl = slice(T * 512, (T + 1) * 512)
                bank = rr["bank"] % 7
                rr["bank"] += 1
                for k in range(KC):
                    P.add("pe", lambda e, k=k, b=b, bank=bank, tsl=tsl: e.matmul(
                        ps[bank][:], gw[b][:].rearrange("p k g f -> p k (g f)")[:, k, 0:128], mT[:, k, tsl],
                        start=(k == 0), stop=(k == KC - 1)),
                        r=[("gw", b, 0), ("gw", b, 1), ("m", k, T)], w=[PS(bank)])
                P.add("dve", lambda e, do=do, tsl=tsl, bank=bank: e.tensor_tensor(
                    out=xT[:, do, tsl], in0=ps[bank][:], in1=xT[:, do, tsl], op=ALU.add),
                    r=[PS(bank), xk(do, T)], w=[xk(do, T)])


    qT = sb("qT", [128, 4, S], BF16, at=R_in)
    kT = sb("kT", [128, 4, S], BF16, at=R_in + 16384)
    vT = sb("vT", [128, 4, S], BF16, at=R_in + 32768)
    segs = [[R_o, R_o + 16384], [R_free + 4096, 229344]]

    def dalloc(segs_, name, shape, dtype):
        esz = 4 if dtype == F32 else 2
        nb = esz
        for s_ in shape[1:]:
            nb *= s_
        nb = ((nb + 31) // 32) * 32
        for sg in segs_:
            if sg[0] + nb <= sg[1]:
                t = nc.alloc_sbuf_tensor_at(name, list(shape), dtype, offset=sg[0])
                sg[0] += nb
                return t
        raise RuntimeError("dn scratch full: " + name)

    ba = dalloc(segs[1:], "dn_ba", [128, 16, 16], F32)
    ctail = segs[1][0]
    wc = [dalloc(segs, f"dn_wc{b}", [128, KC, 128], BF16) for b in range(2)]
    pc = dalloc(segs, "dn_pc", [128, S + 4], F32)
    cv = dalloc(segs, "dn_cv", [128, S], F32)
    wba = dalloc(segs, "dn_wba", [128, KC, 16], BF16)
    dsq = dalloc(segs, "dn_sq", [128, 512], F32)
    csegs = [[R_h, R_h + 32768], [R_o, R_o + 16384], [R_free, R_free + 4096], [ctail, 229344]]
    Sf = dalloc(csegs, "dn_Sf", [128, 8, 128], F32)
    Sb = dalloc(csegs, "dn_Sb", [128, 8, 128], BF16)
    mk_f = dalloc(csegs, "dn_mkf", [128, 2, 128], F32)
    mk4 = dalloc(csegs, "dn_mk4", [128, 4, 512], BF16)
    sc = {}
    for nm in ("beta", "nbeta", "g", "gc", "gtot", "egc", "negc", "ekd", "egl", "tmpa"):
        sc[nm] = dalloc(csegs, "dn_" + nm, [128, 16, 8], F32)
    abc = dalloc(csegs, "dn_abc", [128, 2, 8], F32)
    Wt = [[dalloc(csegs, f"dn_W{d}{b}", [128, 4, 128], BF16) for b in range(2)] for d in range(2)]
    aTt = [[dalloc(csegs, f"dn_aT{d}{b}", [128, 4, 128], BF16) for b in range(2)] for d in range(2)]
    kdt = [[dalloc(csegs, f"dn_kd{d}{b}", [128, 4, 128], BF16) for b in range(2)] for d in range(2)]
    vtk = [[dalloc(csegs, f"dn_vt{d}{b}", [128, 4, 128], BF16) for b in range(2)] for d in range(2)]
    Mt = [[dalloc(csegs, f"dn_M{d}{b}", [128, 4, 128], F32) for b in range(2)] for d in range(2)]
    Nt = [[dalloc(csegs, f"dn_N{d}{b}", [128, 4, 128], F32) for b in range(2)] for d in range(2)]
    Wf = [dalloc(csegs, f"dn_Wf{d}", [128, 4, 128], F32) for d in range(2)]
    Gbc = [dalloc(csegs, f"dn_Gbc{d}", [128, 4, 128], F32) for d in range(2)]
    Et = [dalloc(csegs, f"dn_Et{d}", [128, 4, 128], F32) for d in range(2)]
    tinc = Et
    tstr = [dalloc(csegs, f"dn_tstr{d}", [128, 4, 128], F32) for d in range(2)]
    r0t = [dalloc(csegs, f"dn_r0{d}", [128, 4, 128], BF16) for d in range(2)]
    dlt = [dalloc(csegs, f"dn_dl{d}", [128, 4, 128], BF16) for d in range(2)]
    qsg = Gbc
    otf = tstr
    junk = Gbc
    onb = r0t
    ssn = dalloc(csegs, "dn_ssn", [128, 2, 8], F32)
    wz = [sb(f"dn_wz{b}", [128, KC, 128], BF16, at=R_o + 2048 * b) for b in range(2)]
    zt = [sb(f"dn_zt{b}", [128, 512], F32, at=R_o + 4096 + 2048 * b) for b in range(2)]

    def psb(b_):
        return ps[b_][:].bitcast(BF16)

    def nb_():
        rr["bank"] += 1
        return rr["bank"] % 8

    def deltanet(l):
        P.add("dve", lambda e: e.memset(pc[:, 0:2], 0.0), w=["pcpad"])
        P.add("dve", lambda e: e.memset(pc[:, S + 2:S + 4], 0.0), w=["pcpad"])
        for cc in range(12):
            wb = cc % 2
            h = cc % 4
            P.add("pool", lambda e, wb=wb, cc=cc: e.dma_start(
                out=wc[wb][:], in_=W["w_in"][l][:, cc * 128:(cc + 1) * 128].rearrange("(k p) f -> p k f", p=128)),
                w=[("dwc", wb)], dma=True)
            for T in range(4):
                tsl = slice(T * 512, (T + 1) * 512)
                bank = nb_() % 4
                for k in range(KC):
                    P.add("pe", lambda e, k=k, bank=bank, wb=wb, tsl=tsl: e.matmul(
                        ps[bank][:], wc[wb][:, k, :], hT[:, k, tsl], start=(k == 0), stop=(k == KC - 1)),
                        r=[("dwc", wb), hk(k, T)], w=[PS(bank)])
                P.add("act", lambda e, bank=bank, T=T: e.copy(pc[:, 2 + T * 512:2 + (T + 1) * 512], ps[bank][:]),
                      r=[PS(bank)], w=[("pc", T)])
            pck = [("pc", T) for T in range(4)] + ["pcpad"]
            P.add("dve", lambda e, cc=cc: e.tensor_scalar(out=cv[:], in0=pc[:, 0:S], scalar1=convw[:, l, cc, 0:1],
                                                         scalar2=None, op0=ALU.mult), r=pck, w=["cv"])
            for kk in range(1, 5):
                P.add("dve", lambda e, cc=cc, kk=kk: e.scalar_tensor_tensor(
                    out=cv[:], in0=pc[:, kk:kk + S], scalar=convw[:, l, cc, kk:kk + 1], op0=ALU.mult,
                    in1=cv[:], op1=ALU.add), r=pck + ["cv"], w=["cv"])
            if cc >= 8:
                P.add("act", lambda e, h=h: e.activation(vT[:, h, :], cv[:], AF.Silu), r=["cv"],
                      w=[("vT", h, T) for T in range(4)])
            else:
                dst = qT if cc < 4 else kT
                nm = "qT" if cc < 4 else "kT"
                scl = (128.0 ** -0.5) if cc < 4 else 1.0
                P.add("act", lambda e: e.activation(cv[:], cv[:], AF.Silu), r=["cv"], w=["cv"])
                for T in range(4):
                    tsl = slice(T * 512, (T + 1) * 512)
                    P.add("act", lambda e, tsl=tsl: e.activation(dsq[:], cv[:, tsl], AF.Square), r=["cv"], w=["dsq"])
                    P.add("pe", lambda e: e.matmul(ps[7][:], ones_f[:], dsq[:], start=True, stop=True),
                          r=["dsq"], w=[PS(7)])
                    P.add("act", lambda e: e.activation(lnt[:], ps[7][:], AF.Ln, bias=eps_t[:], scale=1.0),
                          r=[PS(7)], w=["lnt"])
                    P.add("act", lambda e: e.activation(rstd[:], lnt[:], AF.Exp, scale=-0.5), r=["lnt"], w=["rstd"])
                    P.add("dve", lambda e, dst=dst, h=h, tsl=tsl, scl=scl: e.scalar_tensor_tensor(
                        out=dst[:, h, tsl], in0=cv[:, tsl], scalar=scl, op0=ALU.mult, in1=rstd[:], op1=ALU.mult),
                        r=["cv", "rstd"], w=[(nm, h, T)])
        P.add("pool", lambda e: e.dma_start(out=wba[:], in_=W["w_in"][l][:, 2048:2064].rearrange("(k p) f -> p k f", p=128)),
              w=["wba"], dma=True)
        for j in range(16):
            for k in range(KC):
                P.add("pe", lambda e, j=j, k=k: e.matmul(ps[3][:, j * 16:(j + 1) * 16], hT[:, k, j * 128:(j + 1) * 128],
                                                        wba[:, k, :], start=(k == 0), stop=(k == KC - 1)),
                      r=["wba", hk(k, j // 4)], w=[PS(3)])
        P.add("dve", lambda e: e.tensor_copy(ba[:].rearrange("p j c -> p (j c)"), ps[3][:, 0:256]), r=[PS(3)], w=["ba"])
        P.barrier(bar[:])
        P.add("sp", lambda e: e.dma_start(out=mk_f[:, 0, :], in_=c_masks[0]), w=["mk_f"], dma=True)
        P.add("sp", lambda e: e.dma_start(out=mk_f[:, 1, :], in_=c_masks[2]), w=["mk_f"], dma=True)
        for mi in range(4):
            for h in range(4):
                P.add("pool", lambda e, mi=mi, h=h: e.dma_start(out=mk4[:, mi, h * 128:(h + 1) * 128], in_=c_masks[mi]),
                      w=["mk4"], dma=True)
        P.add("sp", lambda e: e.dma_start(out=abc[:, 0, :], in_=W["dn_a_log"][l].rearrange("a b -> (a b)").partition_broadcast(128)),
              w=["abc"], dma=True)
        P.add("sp", lambda e: e.dma_start(out=abc[:, 1, :], in_=W["dn_dt_bias"][l].rearrange("a b -> (a b)").partition_broadcast(128)),
              w=["abc"], dma=True)
        P.add("act", lambda e: e.activation(abc[:, 0, :], abc[:, 0, :], AF.Exp), r=["abc"], w=["abc"])
        P.add("dve", lambda e: e.tensor_scalar(out=abc[:, 0, :], in0=abc[:, 0, :], scalar1=-1.0, scalar2=None, op0=ALU.mult),
              r=["abc"], w=["abc"])
        for j in range(16):
            P.add("dve", lambda e, j=j: e.tensor_tensor(out=sc["tmpa"][:, j, :], in0=ba[:, j, 8:16], in1=abc[:, 1, :], op=ALU.add),
                  r=["ba", "abc"], w=["tmpa"])
        P.add("act", lambda e: e.activation(sc["tmpa"][:], sc["tmpa"][:], AF.Exp), r=["tmpa"], w=["tmpa"])
        P.add("act", lambda e: e.activation(sc["tmpa"][:], sc["tmpa"][:], AF.Ln, bias=1.0), r=["tmpa"], w=["tmpa"])
        for j in range(16):
            P.add("dve", lambda e, j=j: e.tensor_tensor(out=sc["g"][:, j, :], in0=sc["tmpa"][:, j, :], in1=abc[:, 0, :], op=ALU.mult),
                  r=["tmpa", "abc"], w=["g"])
        for j in range(16):
            P.add("act", lambda e, j=j: e.activation(sc["beta"][:, j, :], ba[:, j, 0:8], AF.Exp, scale=-1.0), r=["ba"], w=["beta"])
        P.add("dve", lambda e: e.tensor_scalar(out=sc["beta"][:], in0=sc["beta"][:], scalar1=1.0, scalar2=None, op0=ALU.add),
              r=["beta"], w=["beta"])
        P.add("dve", lambda e: e.reciprocal(sc["beta"][:], sc["beta"][:]), r=["beta"], w=["beta"])
        P.add("dve", lambda e: e.tensor_scalar(out=sc["nbeta"][:], in0=sc["beta"][:], scalar1=-1.0, scalar2=None, op0=ALU.mult),
              r=["beta"], w=["nbeta"])
        for j in range(16):
            P.add("pe", lambda e, j=j: e.matmul(ps[3][:, j * 8:j * 8 + 4], mk_f[:, 0, :], sc["g"][:, j, 0:4], start=True, stop=True),
                  r=["g", "mk_f"], w=[PS(3)])
            P.add("pe", lambda e, j=j: e.matmul(ps[3][:, j * 8 + 4:j * 8 + 8], mk_f[:, 1, :], sc["g"][:, j, 4:8], start=True, stop=True),
                  r=["g", "mk_f"], w=[PS(3)])
            P.add("pe", lambda e, j=j: e.matmul(ps[4][:, j * 8:j * 8 + 8], ones_f[:], sc["g"][:, j, :], start=True, stop=True),
                  r=["g"], w=[PS(4)])
        P.add("dve", lambda e: e.tensor_copy(sc["gc"][:].rearrange("p j c -> p (j c)"), ps[3][:, 0:128]), r=[PS(3)], w=["gc"])
        P.add("dve", lambda e: e.tensor_copy(sc["gtot"][:].rearrange("p j c -> p (j c)"), ps[4][:, 0:128]), r=[PS(4)], w=["gtot"])
        P.add("act", lambda e: e.activation(sc["egc"][:], sc["gc"][:], AF.Exp), r=["gc"], w=["egc"])
        P.add("dve", lambda e: e.tensor_scalar(out=sc["negc"][:], in0=sc["egc"][:], scalar1=-1.0, scalar2=None, op0=ALU.mult),
              r=["egc"], w=["negc"])
        P.add("dve", lambda e: e.tensor_tensor(out=sc["ekd"][:], in0=sc["gtot"][:], in1=sc["gc"][:], op=ALU.subtract),
              r=["gtot", "gc"], w=["ekd"])
        P.add("act", lambda e: e.activation(sc["ekd"][:], sc["ekd"][:], AF.Exp), r=["ekd"], w=["ekd"])
        P.add("act", lambda e: e.activation(sc["egl"][:], sc["gtot"][:], AF.Exp), r=["gtot"], w=["egl"])
        P.add("dve", lambda e: e.memset(Sf[:], 0.0), w=[("Sf", d) for d in range(2)])
        P.add("dve", lambda e: e.memset(Sb[:], 0.0), w=[("Sb", d) for d in range(2)])

        def pre(dr, j, bf):
            jsl = slice(j * 128, (j + 1) * 128)
            Tj = j // 4
            inc4 = mk4[:, 0 + 2 * dr, :]
            str4 = mk4[:, 1 + 2 * dr, :]
            bk = nb_()
            for h in range(4):
                P.add("pe", lambda e, h=h, bk=bk: e.transpose(psb(bk)[:, h * 128:(h + 1) * 128], kT[:, h, jsl], ident_b[:]),
                      r=[("kT", h, Tj)], w=[PS(bk)])
            for h in range(4):
                P.add("act", lambda e, h=h, bk=bk: e.activation(kdt[dr][bf][:, h, :], psb(bk)[:, h * 128:(h + 1) * 128], AF.Copy,
                                                              scale=sc["ekd"][:, j, dr * 4 + h:dr * 4 + h + 1]),
                      r=[PS(bk), "ekd"], w=[("kd", dr, bf)])
            bv = nb_()
            for h in range(4):
                P.add("pe", lambda e, h=h, bv=bv: e.transpose(psb(bv)[:, h * 128:(h + 1) * 128], vT[:, h, jsl], ident_b[:]),
                      r=[("vT", h, Tj)], w=[PS(bv)])
            P.add("dve", lambda e, bv=bv: e.tensor_copy(vtk[dr][bf][:].rearrange("p h e -> p (h e)"), psb(bv)[:, 0:512]),
                  r=[PS(bv)], w=[("vt", dr, bf)])
            for h in range(4):
                P.add("dve", lambda e, h=h: e.tensor_scalar(out=Gbc[dr][:, h, :], in0=ones_f[:], scalar1=sc["g"][:, j, dr * 4 + h:dr * 4 + h + 1],
                                                          scalar2=None, op0=ALU.mult), r=["g"], w=[("Gbc", dr)])
            bg = nb_()
            for h in range(4):
                P.add("pe", lambda e, h=h, bg=bg: e.matmul(ps[bg][:, h * 128:(h + 1) * 128], Gbc[dr][:, h, :], mk_f[:, dr, :],
                                                          start=True, stop=True), r=[("Gbc", dr), "mk_f"], w=[PS(bg)])
            yield
            for h in range(4):
                P.add("dve", lambda e, h=h, bg=bg: e.tensor_scalar(
                    out=Et[dr][:, h, :], in0=ps[bg][:, h * 128:(h + 1) * 128], scalar1=sc["gc"][:, j, dr * 4 + h:dr * 4 + h + 1],
                    scalar2=0.0, op0=ALU.subtract, op1=ALU.min), r=[PS(bg), "gc"], w=[("Et", dr)])
            P.add("act", lambda e: e.activation(Et[dr][:], Et[dr][:], AF.Exp), r=[("Et", dr)], w=[("Et", dr)])
            yield
            Etf = Et[dr][:].rearrange("p h c -> p (h c)")
            P.add("dve", lambda e: e.tensor_tensor(out=tstr[dr][:].rearrange("p h c -> p (h c)"), in0=Etf, in1=str4, op=ALU.mult),
                  r=[("Et", dr), "mk4"], w=[("tstr", dr)])
            P.add("dve", lambda e: e.tensor_tensor(out=Etf, in0=Etf, in1=inc4, op=ALU.mult),
                  r=[("Et", dr), "mk4"], w=[("Et", dr)])
            bkk, bkq = nb_(), nb_()
            for h in range(4):
                P.add("pe", lambda e, h=h, bkk=bkk: e.matmul(ps[bkk][:, h * 128:(h + 1) * 128], kT[:, h, jsl], kT[:, h, jsl],
                                                            start=True, stop=True), r=[("kT", h, Tj)], w=[PS(bkk)])
            for h in range(4):
                P.add("pe", lambda e, h=h, bkq=bkq: e.matmul(ps[bkq][:, h * 128:(h + 1) * 128], kT[:, h, jsl], qT[:, h, jsl],
                                                            start=True, stop=True), r=[("kT", h, Tj), ("qT", h, Tj)], w=[PS(bkq)])
            yield
            P.add("dve", lambda e, bkq=bkq: e.tensor_tensor(out=aTt[dr][bf][:].rearrange("p h c -> p (h c)"), in0=ps[bkq][:],
                                                          in1=tinc[dr][:].rearrange("p h c -> p (h c)"), op=ALU.mult),
                  r=[PS(bkq), ("Et", dr)], w=[("aT", dr, bf)])
            for h in range(4):
                P.add("dve", lambda e, h=h, bkk=bkk: e.scalar_tensor_tensor(
                    out=Mt[dr][0][:, h, :], in0=ps[bkk][:, h * 128:(h + 1) * 128], scalar=sc["nbeta"][:, j, dr * 4 + h:dr * 4 + h + 1],
                    op0=ALU.mult, in1=tstr[dr][:, h, :], op1=ALU.mult), r=[PS(bkk), ("tstr", dr), "nbeta"], w=[("M", dr, 0)])
            yield
            bn = nb_()
            for h in range(4):
                P.add("pe", lambda e, h=h, bn=bn: e.transpose(ps[bn][:, h * 128:(h + 1) * 128], Mt[dr][0][:, h, :], ident_f[:]),
                      r=[("M", dr, 0)], w=[PS(bn)])
            yield
            P.add("act", lambda e, bn=bn: e.copy(Nt[dr][0][:].rearrange("p h c -> p (h c)"), ps[bn][:]), r=[PS(bn)], w=[("N", dr, 0)])
            Wc = Wf[dr]
            for h in range(4):
                P.add("dve", lambda e, h=h: e.tensor_tensor(out=Wc[:, h, :], in0=Mt[dr][0][:, h, :], in1=ident_f[:], op=ALU.add),
                      r=[("M", dr, 0)], w=[("Wf", dr)])
            yield
            cur = 0
            for lev in range(6):
                nxt = 1 - cur
                if lev < 5:
                    bx = nb_()
                    for h in range(4):
                        P.add("pe", lambda e, h=h, bx=bx, cur=cur: e.matmul(ps[bx][:, h * 128:(h + 1) * 128], Nt[dr][cur][:, h, :], Mt[dr][cur][:, h, :],
                                                                          start=True, stop=True), r=[("M", dr, cur), ("N", dr, cur)], w=[PS(bx)])
                by = nb_()
                for h in range(4):
                    P.add("pe", lambda e, h=h, by=by, cur=cur: e.matmul(ps[by][:, h * 128:(h + 1) * 128], Mt[dr][cur][:, h, :], Nt[dr][cur][:, h, :],
                                                                      start=True, stop=True), r=[("M", dr, cur), ("N", dr, cur)], w=[PS(by)])
                yield
                if lev < 5:
                    P.add("dve", lambda e, bx=bx, nxt=nxt: e.tensor_copy(Mt[dr][nxt][:].rearrange("p h c -> p (h c)"), ps[bx][:]),
                          r=[PS(bx)], w=[("M", dr, nxt)])
                P.add("act", lambda e, by=by, nxt=nxt: e.copy(Nt[dr][nxt][:].rearrange("p h c -> p (h c)"), ps[by][:]),
                      r=[PS(by)], w=[("N", dr, nxt)])
                yield
                bz = nb_()
                for h in range(4):
                    P.add("pe", lambda e, h=h, bz=bz, nxt=nxt: e.matmul(ps[bz][:, h * 128:(h + 1) * 128], Nt[dr][nxt][:, h, :], Wc[:, h, :],
                                                                      start=True, stop=True), r=[("N", dr, nxt), ("Wf", dr)], w=[PS(bz)])
                yield
                P.add("dve", lambda e, bz=bz: e.tensor_tensor(out=Wc[:].rearrange("p h c -> p (h c)"), in0=ps[bz][:],
                                                            in1=Wc[:].rearrange("p h c -> p (h c)"), op=ALU.add),
                      r=[PS(bz), ("Wf", dr)], w=[("Wf", dr)])
                yield
                cur = nxt
            P.add("act", lambda e: e.copy(Wt[dr][bf][:], Wf[dr][:]), r=[("Wf", dr)], w=[("W", dr, bf)])

        def chain(dr, j, bf, second):
            jsl = slice(j * 128, (j + 1) * 128)
            Tj = j // 4
            ba_, bb_ = nb_(), nb_()
            for h in range(4):
                P.add("pe", lambda e, h=h, ba_=ba_: e.matmul(ps[ba_][:, h * 128:(h + 1) * 128], kT[:, h, jsl], Sb[:, dr * 4 + h, :],
                                                            start=True, stop=True), r=[("kT", h, Tj), ("Sb", dr)], w=[PS(ba_)])
            for h in range(4):
                P.add("pe", lambda e, h=h, bb_=bb_: e.matmul(ps[bb_][:, h * 128:(h + 1) * 128], qT[:, h, jsl], Sb[:, dr * 4 + h, :],
                                                            start=True, stop=True), r=[("qT", h, Tj), ("Sb", dr)], w=[PS(bb_)])
            yield
            for h in range(4):
                P.add("dve", lambda e, h=h, ba_=ba_: e.scalar_tensor_tensor(
                    out=r0t[dr][:, h, :], in0=ps[ba_][:, h * 128:(h + 1) * 128], scalar=sc["negc"][:, j, dr * 4 + h:dr * 4 + h + 1],
                    op0=ALU.mult, in1=vtk[dr][bf][:, h, :], op1=ALU.add), r=[PS(ba_), ("vt", dr, bf), "negc"], w=[("r0", dr)])
            yield
            bc_ = nb_()
            for h in range(4):
                P.add("pe", lambda e, h=h, bc_=bc_: e.matmul(ps[bc_][:, h * 128:(h + 1) * 128], Wt[dr][bf][:, h, :], r0t[dr][:, h, :],
                                                            start=True, stop=True), r=[("W", dr, bf), ("r0", dr)], w=[PS(bc_)])
            yield
            for h in range(4):
                P.add("act", lambda e, h=h, bc_=bc_: e.activation(dlt[dr][:, h, :], ps[bc_][:, h * 128:(h + 1) * 128], AF.Copy,
                                                                scale=sc["beta"][:, j, dr * 4 + h:dr * 4 + h + 1]),
                      r=[PS(bc_), "beta"], w=[("dl", dr)])
            yield
            bd_, be_ = nb_(), nb_()
            for h in range(4):
                P.add("pe", lambda e, h=h, bd_=bd_: e.matmul(ps[bd_][:, h * 128:(h + 1) * 128], kdt[dr][bf][:, h, :], dlt[dr][:, h, :],
                                                            start=True, stop=True), r=[("kd", dr, bf), ("dl", dr)], w=[PS(bd_)])
            for h in range(4):
                P.add("pe", lambda e, h=h, be_=be_: e.matmul(ps[be_][:, h * 128:(h + 1) * 128], aTt[dr][bf][:, h, :], dlt[dr][:, h, :],
                                                            start=True, stop=True), r=[("aT", dr, bf), ("dl", dr)], w=[PS(be_)])
            for h in range(4):
                P.add("act", lambda e, h=h, bb_=bb_: e.activation(qsg[dr][:, h, :], ps[bb_][:, h * 128:(h + 1) * 128], AF.Copy,
                                                                scale=sc["egc"][:, j, dr * 4 + h:dr * 4 + h + 1]),
                      r=[PS(bb_), "egc"], w=[("Gbc", dr)])
            yield
            if not second:
                P.add("dve", lambda e, be_=be_: e.tensor_tensor(out=o_dnT[:, :, jsl], in0=ps[be_][:].rearrange("p (h c) -> p h c", h=4), in1=qsg[dr][:],
                                                              op=ALU.add), r=[PS(be_), ("Gbc", dr)], w=[("otok", j)])
            else:
                P.add("dve", lambda e, be_=be_: e.tensor_tensor(out=otf[dr][:].rearrange("p h c -> p (h c)"), in0=ps[be_][:],
                                                              in1=qsg[dr][:].rearrange("p h c -> p (h c)"), op=ALU.add),
                      r=[PS(be_), ("Gbc", dr)], w=[("tstr", dr)])
                P.add("dve", lambda e: e.tensor_tensor(out=otf[dr][:], in0=otf[dr][:],
                                                     in1=o_dnT[:, :, jsl], op=ALU.add), r=[("tstr", dr), ("otok", j)], w=[("tstr", dr)])
            for h in range(4):
                P.add("dve", lambda e, h=h, bd_=bd_: e.scalar_tensor_tensor(
                    out=Sf[:, dr * 4 + h, :], in0=Sf[:, dr * 4 + h, :], scalar=sc["egl"][:, j, dr * 4 + h:dr * 4 + h + 1],
                    op0=ALU.mult, in1=ps[bd_][:, h * 128:(h + 1) * 128], op1=ALU.add), r=[PS(bd_), ("Sf", dr), "egl"], w=[("Sf", dr)])
            P.add("act", lambda e: e.copy(Sb[:, dr * 4:dr * 4 + 4, :], Sf[:, dr * 4:dr * 4 + 4, :]), r=[("Sf", dr)], w=[("Sb", dr)])
            yield
            if second:
                for h in range(4):
                    P.add("act", lambda e, h=h: e.activation(junk[dr][:, h, :], otf[dr][:, h, :], AF.Square, accum_out=ssn[:, dr, h:h + 1]),
                          r=[("tstr", dr)], w=[("Gbc", dr), ("ssn", dr)])
                P.add("act", lambda e: e.activation(ssn[:, dr, 4:8], ssn[:, dr, 0:4], AF.Ln, bias=eps_t[:], scale=1.0 / 128),
                      r=[("ssn", dr)], w=[("ssn", dr)])
                P.add("act", lambda e: e.activation(ssn[:, dr, 4:8], ssn[:, dr, 4:8], AF.Exp, scale=-0.5), r=[("ssn", dr)], w=[("ssn", dr)])
                yield
                for h in range(4):
                    P.add("dve", lambda e, h=h: e.tensor_scalar(out=onb[dr][:, h, :], in0=otf[dr][:, h, :], scalar1=ssn[:, dr, 4 + h:5 + h],
                                                              scalar2=None, op0=ALU.mult), r=[("tstr", dr), ("ssn", dr)], w=[("r0", dr)])
                bt = nb_()
                for h in range(4):
                    P.add("pe", lambda e, h=h, bt=bt: e.transpose(psb(bt)[:, h * 128:(h + 1) * 128], onb[dr][:, h, :], ident_b[:]),
                          r=[("r0", dr)], w=[PS(bt)])
                yield
                P.add("dve", lambda e, bt=bt: e.tensor_scalar(
                    out=o_dnT[:, :, jsl], in0=psb(bt)[:, 0:512].rearrange("p (h c) -> p h c", h=4), scalar1=wdn[:, l:l + 1],
                    scalar2=None, op0=ALU.mult), r=[PS(bt), "wdn"], w=[("odn", h, Tj) for h in range(4)] + [("otok", j)])

        def stream(dr):
            jm = (lambda st: st) if dr == 0 else (lambda st: 15 - st)
            for step in range(17):
                if step < 16:
                    yield from pre(dr, jm(step), step % 2)
                if step > 0:
                    st = step - 1
                    yield from chain(dr, jm(st), st % 2, st >= 8)

        alive = [stream(0), stream(1)]
        while alive:
            for g_ in list(alive):
                try:
                    next(g_)
                except StopIteration:
                    alive.remove(g_)
        if l == 0:
            dump("qT", qT[:].rearrange("p h s -> p (h s)")); dump("kT", kT[:].rearrange("p h s -> p (h s)"))
            dump("vT", vT[:].rearrange("p h s -> p (h s)"))
            for nm_ in ("beta", "g", "gc", "gtot", "egc", "ekd", "egl"):
                dump("sc_" + nm_, sc[nm_][:].rearrange("p j c -> p (j c)"))
            dump("ba", ba[:].rearrange("p j c -> p (j c)"))
            dump("W00", Wt[0][0][:].rearrange("p h c -> p (h c)")); dump("aT00", aTt[0][0][:].rearrange("p h c -> p (h c)"))
            dump("kd00", kdt[0][0][:].rearrange("p h c -> p (h c)")); dump("Sf", Sf[:].rearrange("p h c -> p (h c)"))
            dump("odnT", o_dnT[:].rearrange("p h s -> p (h s)"))
            dump("Et", Et[0][:].rearrange("p h c -> p (h c)")); dump("M0", Mt[0][0][:].rearrange("p h c -> p (h c)"))
        P.barrier(bar[:])
        for T in range(4):
            norm_to_h(T, nwm[:, l, :], sqm)
        for h in range(4):
            b = h % 2
            P.add("pool", lambda e, b=b, h=h: e.dma_start(
                out=wz[b][:], in_=W["w_in"][l][:, 1536 + h * 128:1536 + (h + 1) * 128].rearrange("(k p) f -> p k f", p=128)),
                w=[("wz", b)], dma=True)
            for T in range(4):
                tsl = slice(T * 512, (T + 1) * 512)
                bank = nb_() % 4
                zb_ = nb_() % 2
                for k in range(KC):
                    P.add("pe", lambda e, k=k, bank=bank, b=b, tsl=tsl: e.matmul(
                        ps[bank][:], wz[b][:, k, :], hT[:, k, tsl], start=(k == 0), stop=(k == KC - 1)),
                        r=[("wz", b), hk(k, T)], w=[PS(bank)])
                P.add("act", lambda e, bank=bank, zb_=zb_: e.activation(zt[zb_][:], ps[bank][:], AF.Silu), r=[PS(bank)], w=[("zt", zb_)])
                P.add("dve", lambda e, h=h, tsl=tsl, zb_=zb_: e.tensor_tensor(out=o_dnT[:, h, tsl], in0=o_dnT[:, h, tsl], in1=zt[zb_][:],
                                                                           op=ALU.mult), r=[("zt", zb_), ("odn", h, T)], w=[("odn", h, T)])

    def mixer(l):
        for T in range(4):
            norm_to_h(T, nwm[:, l, :], sqm)
        P.barrier(bar[:])
        if do_dn:
            deltanet(l)
            if l == 0:
                dump("odnT2", o_dnT[:].rearrange("p h s -> p (h s)")); dump("zt0", zt[0][:]); dump("hT2", hT[:].rearrange("p k s -> p (k s)"))
        else:
            for h in range(4):
                for T in range(4):
                    P.add("dve", lambda e, h=h, T=T: e.memset(o_dnT[:, h, T * 512:(T + 1) * 512], 0.0),
                          w=[("odn", h, T)])
        P.barrier(bar[:])
        if do_da:
            attention(l)
        else:
            for h in range(4):
                for T in range(4):
                    P.add("dve", lambda e, h=h, T=T: e.memset(o_daT[:, h, T * 512:(T + 1) * 512], 0.0),
                          w=[("oda", h, T)])
        P.barrier(bar[:])
        if l == 0:
            dump("wq", wqkv[0][:, 0, :, :].rearrange("p k f -> p (k f)")); dump("wv", wqkv[0][:, 2, :, :].rearrange("p k f -> p (k f)"))
            dump("aq0", aqh[0][:]); dump("ak0", akm[0][:]); dump("av0", avh[0][:].rearrange("p a b -> p (a b)"))
            dump("pT0", pT[0][:]); dump("tmpf0", tmpf[0][:]); dump("rz", rz[:]); dump("o0", o0[:]); dump("oc", oc[:])
            dump("t1", t1[:]); dump("rstd", rstd[:]); dump("odaT", o_daT[:].rearrange("p h s -> p (h s)"))
            dump("hT", hT[:].rearrange("p k s -> p (k s)")); dump("neglam", neglam[:]); dump("sublnw", sublnw[:])
            P.barrier(bar[:])
        merge(l, do_dn, do_da)
        P.barrier(bar[:])

    for s in range(nseq):
        load_x(s)
        P.barrier(bar[:])
        for l in range(depth):
            if do_ffn:
                ffn(l, nw1, W["ffn1_wg"], W["ffn1_wu"], W["ffn1_wd"])
                P.barrier(bar[:])
            if do_mix:
                mixer(l)
            if do_ffn:
                ffn(l, nw2, W["ffn2_wg"], W["ffn2_wu"], W["ffn2_wd"])
                P.barrier(bar[:])
        store_y(s)

    P.emit(same_engine_sync=same_engine_sync)
    return nc, len(P.ops)


def make_consts():
    k = np.arange(128, dtype=np.float32)[:, None]
    q = np.arange(512, dtype=np.float32)[None, :]
    jj = np.arange(896, dtype=np.float32)[None, :]
    i = np.arange(128)
    incu = (i[:, None] <= i[None, :]).astype(np.float32)
    stru = (i[:, None] < i[None, :]).astype(np.float32)
    masks = np.stack([incu, stru, incu.T.copy(), stru.T.copy()]).astype(np.float32)
    kaug = np.zeros((128, 128), np.float32)
    qaug = np.zeros((4, 2, 128, 512), np.float32)
    qr = np.arange(512, dtype=np.float32)
    q_lo = np.mod(qr, 256.0)
    q_hi = qr - q_lo
    for base in (0,):
        kaug[base + 0] = np.arange(128, dtype=np.float32)
        kaug[base + 1] = 1.0
        kaug[base + 2] = 1.0
        for h in range(4):
            sl = 2.0 ** (-8.0 * (h + 1) / 4)
            for sg, sign in ((0, 1.0), (1, -1.0)):
                qaug[h, sg, base + 0] = sign * sl
                qaug[h, sg, base + 1] = -sign * sl * q_lo
                qaug[h, sg, base + 2] = -sign * sl * q_hi
    return {"c_ident": np.eye(128, dtype=np.float32), "c_kaug": kaug, "c_qaug": qaug, "c_bs": np.ascontiguousarray(q - k),
            "c_abs": np.ascontiguousarray(np.abs(jj - 384.0 - k)), "c_masks": masks}


_CACHE = {}


def kernel(**inputs):
    xs = np.concatenate([np.asarray(inputs["x_prompt"], np.float32), np.asarray(inputs["x_sample"], np.float32)], axis=0)
    nb_p = inputs["x_prompt"].shape[0]
    if "nc" not in _CACHE:
        _CACHE["nc"] = build()[0]
    nc = _CACHE["nc"]
    wmap = {name: np.ascontiguousarray(np.asarray(inputs[name], np.float32)) for name, _ in WSHAPES}
    consts = make_consts()
    in_maps = []
    for c in range(NCORES):
        m = {"x": np.ascontiguousarray(xs[c * NSEQ:(c + 1) * NSEQ])}
        m.update(wmap)
        m.update(consts)
        in_maps.append(m)
    res = run_bass_kernel_spmd(nc, in_maps, core_ids=list(range(NCORES)))
    yfull = np.concatenate([np.asarray(r["y"], np.float32) for r in res.results], axis=0)
    return (np.ascontiguousarray(yfull[:nb_p]), np.ascontiguousarray(yfull[nb_p:]))
```

```python
import math
import numpy as np
import concourse.bass as bass
import concourse.mybir as mybir
from concourse.bass_utils import run_bass_kernel_spmd

F32 = mybir.dt.float32
BF16 = mybir.dt.bfloat16
AF = mybir.ActivationFunctionType
ALU = mybir.AluOpType

D = 1024
S = 2048
DFF = 2816
NIN = 5648
DEPTH = 4
NCORES = 8
NSEQ = 5
KC = 8
FCH = 22
NDSLOT = 8
HEADS = [0, 1, 2, 3]
ATT_LA, ATT_ED = 3, 2

WSHAPES = [
    ("ffn1_norm", [DEPTH, D]), ("ffn1_wg", [DEPTH, D, DFF]), ("ffn1_wu", [DEPTH, D, DFF]),
    ("ffn1_wd", [DEPTH, DFF, D]), ("mix_norm", [DEPTH, D]), ("w_in", [DEPTH, D, NIN]),
    ("conv_w", [DEPTH, 5, 1536]), ("dn_a_log", [DEPTH, 2, 4]), ("dn_dt_bias", [DEPTH, 2, 4]),
    ("dn_out_norm", [DEPTH, 128]), ("diff_lambda", [DEPTH, 4, 64]), ("diff_subln", [DEPTH, 128]),
    ("w_branch_dn", [DEPTH, 512, D]), ("w_branch_da", [DEPTH, 512, D]), ("w_out", [DEPTH, D, D]),
    ("ffn2_norm", [DEPTH, D]), ("ffn2_wg", [DEPTH, D, DFF]), ("ffn2_wu", [DEPTH, D, DFF]),
    ("ffn2_wd", [DEPTH, DFF, D]), ("final_norm", [D]),
]


class Prog:
    def __init__(self, nc):
        self.nc = nc
        self.ops = []
        self.keys = set()

    def add(self, eng, fn, r=(), w=(), dma=False):
        self.ops.append((eng, fn, tuple(r), tuple(w), dma))
        self.keys.update(r)
        self.keys.update(w)

    def barrier(self, ap):
        self.add("dve", lambda e: e.memset(ap, 0.0), w=list(self.keys) + ["__BAR__"])

    def emit(self, same_engine_sync=True):
        nc = self.nc
        ops = self.ops
        n = len(ops)
        last_w = {}
        rd = {}
        deps = [None] * n
        needed = [False] * n
        cur_bar = None
        for i, (eng, fn, r, w, dma) in enumerate(ops):
            d = set()
            for k in r:
                j = last_w.get(k, cur_bar)
                if j is not None:
                    d.add(j)
            for k in w:
                j = last_w.get(k, cur_bar)
                if j is not None:
                    d.add(j)
                rr = rd.get(k)
                if rr:
                    d.update(rr.values())
            d.discard(i)
            dd = []
            for j in d:
                je = ops[j][0]
                jd = ops[j][4]
                if not jd and not dma and je == eng:
                    if eng == "pe" or not same_engine_sync:
                        continue
                dd.append(j)
            deps[i] = dd
            for j in dd:
                needed[j] = True
            tag = ("d", i) if dma else eng
            for k in r:
                rd.setdefault(k, {})[tag] = i
            for k in w:
                last_w[k] = i
                rd[k] = {}
            if "__BAR__" in w:
                cur_bar = i
                last_w = {}
                rd = {}
        cnt = {"pe": 0, "dve": 0, "act": 0, "pool": 0}
        dcnt = {"sp": 0, "pool": 0, "act": 0}
        sig = [None] * n
        dslot = [None] * n
        for i, (eng, fn, r, w, dma) in enumerate(ops):
            if dma:
                k = dcnt[eng]
                dcnt[eng] += 1
                slot = k % NDSLOT
                sig[i] = (("dma", eng, slot), 16 * (k // NDSLOT + 1))
                dslot[i] = (slot, 16 * (k // NDSLOT))
            elif needed[i]:
                cnt[eng] += 1
                sig[i] = (("c", eng), cnt[eng])
        per_eng = {"pe": [], "dve": [], "act": [], "pool": [], "sp": []}
        for i, op in enumerate(ops):
            per_eng[op[0]].append(i)
        semkeys = [("c", e) for e in ("pe", "dve", "act", "pool")]
        for q in ("sp", "pool", "act"):
            if dcnt[q]:
                semkeys += [("dma", q, s) for s in range(NDSLOT)]
        from contextlib import ExitStack
        with ExitStack() as es:
            sems = {}
            for sk in semkeys:
                sems[sk] = es.enter_context(nc.semaphore("s_" + "_".join(str(t) for t in sk)))
            block = es.enter_context(nc.Block())

            def run_engine(ename, e):
                waited = {}
                final = {}
                for i in per_eng[ename]:
                    eng, fn, r, w, dma = ops[i]
                    if dma:
                        slot, prev = dslot[i]
                        sk = ("dma", eng, slot)
                        if prev > 0 and waited.get(sk, 0) < prev:
                            e.wait_ge(sems[sk], prev)
                            waited[sk] = prev
                    need = {}
                    for j in deps[i]:
                        sk, v = sig[j]
                        if need.get(sk, 0) < v:
                            need[sk] = v
                    for sk, v in need.items():
                        if waited.get(sk, 0) < v:
                            e.wait_ge(sems[sk], v)
                            waited[sk] = v
                    ins = fn(e)
                    if sig[i] is not None:
                        sk, v = sig[i]
                        if dma:
                            ins.then_inc(sems[sk], 16)
                            final[sk] = v
                        else:
                            ins.then_inc(sems[sk], 1)
                for sk, v in final.items():
                    if waited.get(sk, 0) < v:
                        e.wait_ge(sems[sk], v)

            @block.tensor
            def _(e):
                run_engine("pe", e)

            @block.vector
            def _(e):
                run_engine("dve", e)

            @block.scalar
            def _(e):
                run_engine("act", e)

            @block.gpsimd
            def _(e):
                run_engine("pool", e)

            @block.sync
            def _(e):
                run_engine("sp", e)


def build(nseq=NSEQ, depth=DEPTH, do_ffn=True, do_mix=True, do_dn=True, do_da=True, same_engine_sync=True, debug=False):
    nc = bass.Bass("TRN2", target_bir_lowering=False)
    x = nc.dram_tensor("x", [nseq, S, D], F32, kind="ExternalInput").ap()
    y = nc.dram_tensor("y", [nseq, S, D], F32, kind="ExternalOutput").ap()
    W = {}
    for name, shape in WSHAPES:
        W[name] = nc.dram_tensor(name, shape, F32, kind="ExternalInput").ap()
    c_ident = nc.dram_tensor("c_ident", [128, 128], F32, kind="ExternalInput").ap()
    c_bs = nc.dram_tensor("c_bs", [128, 512], F32, kind="ExternalInput").ap()
    c_abs = nc.dram_tensor("c_abs", [128, 896], F32, kind="ExternalInput").ap()
    c_masks = nc.dram_tensor("c_masks", [4, 128, 128], F32, kind="ExternalInput").ap()
    c_kaug = nc.dram_tensor("c_kaug", [128, 128], F32, kind="ExternalInput").ap()
    c_qaug = nc.dram_tensor("c_qaug", [4, 2, 128, 512], F32, kind="ExternalInput").ap()

    P = Prog(nc)

    def dump(nm, ap):
        if not debug:
            return
        dt_ = nc.dram_tensor("dbg_" + nm, [ap.shape[0], ap.shape[1]], F32, kind="ExternalOutput").ap()
        P.add("pool", lambda e: e.dma_start(out=dt_, in_=ap), r=list(P.keys), w=[("dbg", nm)], dma=True)
    off = [16512]

    def sb(name, shape, dtype, at=None):
        esz = 4 if dtype == F32 else 2
        nb = esz
        for s_ in shape[1:]:
            nb *= s_
        o = off[0] if at is None else at
        t = nc.alloc_sbuf_tensor_at(name, list(shape), dtype, offset=o)
        if at is None:
            off[0] = o + ((nb + 31) // 32) * 32
        return t

    ident_f = sb("ident_f", [128, 128], F32)
    ident_b = sb("ident_b", [128, 128], BF16)
    ones_f = sb("ones_f", [128, 128], F32)
    ones_b = sb("ones_b", [128, 128], BF16)
    eps_t = sb("eps_t", [128, 1], F32)
    nw1 = sb("nw1", [128, DEPTH, KC], F32)
    nwm = sb("nwm", [128, DEPTH, KC], F32)
    nw2 = sb("nw2", [128, DEPTH, KC], F32)
    nwf = sb("nwf", [128, KC], F32)
    bar = sb("bar", [128, 8], F32)
    eps5_t = sb("eps5_t", [128, 1], F32)
    bs_t = sb("bs_t", [128, 512], F32)
    abs_t = sb("abs_t", [128, 896], F32)
    sublnw = sb("sublnw", [128, DEPTH], F32)
    neglam = sb("neglam", [128, DEPTH], F32)
    lam_s = sb("lam_s", [128, 4], F32)
    kaug_t = sb("kaug_t", [128, 128], BF16)
    convw = sb("convw", [128, DEPTH, 12, 5], F32)
    wdn = sb("wdn", [128, DEPTH], F32)
    R_x = off[0]
    xT = sb("xT", [128, KC, S], F32)
    R_h = off[0]
    hT = sb("hT", [128, KC, S], BF16)
    R_in = off[0]
    off[0] += 48 * 1024
    R_o = off[0]
    off[0] += 32 * 1024
    R_free = off[0]
    assert off[0] <= 229344, off[0]
    aT = sb("aT", [128, FCH, 1024], BF16, at=R_in)
    o_ = R_o
    wgu = []
    for b in range(2):
        wgu.append(sb(f"wgu{b}", [128, KC, 2, 256], BF16, at=o_))
        o_ += 8192
    wdb = []
    for b in range(4):
        wdb.append(sb(f"wdb{b}", [128, D], BF16, at=o_))
        o_ += 2048
    sq = [sb(f"sq{b}", [128, 512], F32, at=o_ + 2048 * b) for b in range(2)]
    o_ += 4096
    sgt = [sb(f"sgt{b}", [128, 512], F32, at=o_ + 2048 * b) for b in range(2)]
    o_ += 4096
    assert o_ <= R_free
    lnt = sb("lnt", [128, 512], F32, at=R_free)
    rstd = sb("rstd", [128, 512], F32, at=R_free + 2048)
    xin = [sb(f"xin{b}", [128, D], F32, at=R_in + 4096 * b) for b in range(2)]
    yo = [sb(f"yo{b}", [128, D], F32, at=R_in + 8192 + 4096 * b) for b in range(2)]
    yTf = sb("yTf", [128, KC, 512], F32, at=R_h)

    ps = [nc.alloc_psum_tensor(f"ps{b}", [128, 512], F32) for b in range(8)]

    def PS(b):
        return ("ps", b)

    P.add("sp", lambda e: e.dma_start(out=ident_f[:], in_=c_ident), w=["ident_f"], dma=True)
    P.add("pool", lambda e: e.dma_start(out=ident_b[:], in_=c_ident), w=["ident_b"], dma=True)
    P.add("dve", lambda e: e.memset(ones_f[:], 1.0), w=["ones_f"])
    P.add("dve", lambda e: e.memset(ones_b[:], 1.0), w=["ones_b"])
    P.add("dve", lambda e: e.memset(eps_t[:], 1e-6), w=["eps_t"])
    for t_, nm in ((nw1, "ffn1_norm"), (nwm, "mix_norm"), (nw2, "ffn2_norm")):
        P.add("sp", lambda e, t_=t_, nm=nm: e.dma_start(
            out=t_[:], in_=W[nm].rearrange("l (c p) -> p l c", p=128), allow_slow_non_contiguous=True),
            w=[nm], dma=True)
    P.add("sp", lambda e: e.dma_start(out=nwf[:], in_=W["final_norm"].rearrange("(c p) -> p c", p=128),
                                      allow_slow_non_contiguous=True), w=["final_norm"], dma=True)

    P.add("dve", lambda e: e.memset(eps5_t[:], 1e-5), w=["eps5_t"])
    P.add("pool", lambda e: e.dma_start(out=kaug_t[:], in_=c_kaug), w=["kaug_t"], dma=True)
    P.add("sp", lambda e: e.dma_start(out=bs_t[:], in_=c_bs), w=["bs_t"], dma=True)
    P.add("sp", lambda e: e.dma_start(out=abs_t[:], in_=c_abs), w=["abs_t"], dma=True)
    P.add("sp", lambda e: e.dma_start(out=sublnw[:], in_=W["diff_subln"].rearrange("l p -> p l"),
                                      allow_slow_non_contiguous=True), w=["sublnw"], dma=True)
    for l_ in range(DEPTH):
        for kk_ in range(5):
            P.add("sp", lambda e, l_=l_, kk_=kk_: e.dma_start(out=convw[:, l_, :, kk_], in_=W["conv_w"][l_, kk_].rearrange("(c p) -> p c", p=128),
                                                  allow_slow_non_contiguous=True), w=["convw"], dma=True)
    P.add("sp", lambda e: e.dma_start(out=wdn[:], in_=W["dn_out_norm"].rearrange("l p -> p l"),
                                      allow_slow_non_contiguous=True), w=["wdn"], dma=True)
    lpb = sb("lpb", [128, DEPTH, 4, 64], F32, at=R_free)
    lpt = sb("lpt", [128, 64], F32, at=R_free + 4096)
    P.add("sp", lambda e: e.dma_start(out=lpb[:].rearrange("p l a d -> p (l a d)"),
                                      in_=W["diff_lambda"].rearrange("l a d -> (l a d)").partition_broadcast(128)),
          w=["lpb"], dma=True)
    for l in range(DEPTH):
        li = 0.8 - 0.6 * math.exp(-0.3 * l)
        for a in range(2):
            P.add("dve", lambda e, l=l, a=a: e.tensor_tensor(out=lpt[:], in0=lpb[:, l, 2 * a, :], in1=lpb[:, l, 2 * a + 1, :],
                                                          op=ALU.mult), r=["lpb"], w=["lpt"])
            P.add("dve", lambda e, a=a: e.reduce_sum(lam_s[:, a:a + 1], lpt[:], axis=mybir.AxisListType.X),
                  r=["lpt"], w=["lam_s"])
        P.add("act", lambda e: e.activation(lam_s[:, 2:4], lam_s[:, 0:2], AF.Exp), r=["lam_s"], w=["lam_s"])
        P.add("dve", lambda e, l=l: e.tensor_tensor(out=neglam[:, l:l + 1], in0=lam_s[:, 3:4], in1=lam_s[:, 2:3],
                                                 op=ALU.subtract), r=["lam_s"], w=["neglam"])
        P.add("dve", lambda e, l=l, li=li: e.tensor_scalar(out=neglam[:, l:l + 1], in0=neglam[:, l:l + 1], scalar1=-li,
                                                        scalar2=None, op0=ALU.add), r=["neglam"], w=["neglam"])
        P.add("dve", lambda e, l=l, li=li: e.tensor_scalar(out=sublnw[:, l:l + 1], in0=sublnw[:, l:l + 1], scalar1=1.0 - li,
                                                        scalar2=None, op0=ALU.mult), r=["sublnw"], w=["sublnw"])
    P.barrier(bar[:])

    rr = {"evac": 0, "bank": 0}

    def evac_eng():
        rr["evac"] ^= 1
        return "dve" if rr["evac"] else "act"

    def copy_op(eng, out, in_, r, w):
        if eng == "act":
            P.add("act", lambda e: e.copy(out, in_), r=r, w=w)
        else:
            P.add(eng, lambda e: e.tensor_copy(out, in_), r=r, w=w)

    def xk(c, t):
        return ("x", c, t)

    def hk(c, t):
        return ("h", c, t)

    def load_x(s):
        for j in range(16):
            b = j % 2
            T = j // 4
            P.add("sp", lambda e, b=b, j=j: e.dma_start(out=xin[b][:], in_=x[s, j * 128:(j + 1) * 128, :]),
                  r=(), w=[("xin", b)], dma=True)
            for half in range(2):
                bank = (2 * j + half) % 4
                for i in range(4):
                    c = 4 * half + i
                    P.add("pe", lambda e, bank=bank, i=i, b=b, c=c: e.transpose(
                        ps[bank][:, i * 128:(i + 1) * 128], xin[b][:, c * 128:(c + 1) * 128], ident_f[:]),
                        r=[("xin", b), "ident_f"], w=[PS(bank)])
                copy_op(evac_eng(), xT[:, 4 * half:4 * half + 4, j * 128:(j + 1) * 128],
                        ps[bank][:].rearrange("p (a t) -> p a t", a=4),
                        r=[PS(bank)], w=[xk(4 * half + i, T) for i in range(4)])

    sqm = [sb(f"sqm{b}", [128, 512], F32, at=R_free + 4096 + 2048 * b) for b in range(2)]

    def rstd_for_tile(T, eps_tile, scale, sq=sq):
        tsl = slice(T * 512, (T + 1) * 512)
        for c in range(KC):
            b = c % 2
            P.add("act", lambda e, c=c, b=b: e.activation(sq[b][:], xT[:, c, tsl], AF.Square),
                  r=[xk(c, T)], w=[("sq", b)])
            P.add("pe", lambda e, c=c, b=b: e.matmul(ps[7][:], ones_f[:], sq[b][:], start=(c == 0), stop=(c == KC - 1)),
                  r=[("sq", b), "ones_f"], w=[PS(7)])
        P.add("act", lambda e: e.activation(lnt[:], ps[7][:], AF.Ln, bias=eps_tile[:], scale=scale),
              r=[PS(7), "eps_t"], w=["lnt"])
        P.add("act", lambda e: e.activation(rstd[:], lnt[:], AF.Exp, scale=-0.5), r=["lnt"], w=["rstd"])

    def norm_to_h(T, nw_ap, sq=sq):
        tsl = slice(T * 512, (T + 1) * 512)
        rstd_for_tile(T, eps_t, 1.0 / D, sq)
        for c in range(KC):
            P.add("dve", lambda e, c=c: e.scalar_tensor_tensor(
                out=hT[:, c, tsl], in0=xT[:, c, tsl], scalar=nw_ap[:, c:c + 1], op0=ALU.mult,
                in1=rstd[:], op1=ALU.mult), r=[xk(c, T), "rstd"], w=[hk(c, T)])

    def ffn(l, nw, wg, wu, wd):
        nw_ap = nw[:, l, :]
        for half in range(2):
            tiles = [2 * half, 2 * half + 1]
            for T in tiles:
                norm_to_h(T, nw_ap)
            for fg in range(FCH // 2):
                b = fg % 2
                for gi, wsrc in enumerate((wg, wu)):
                    P.add("pool", lambda e, b=b, gi=gi, wsrc=wsrc, fg=fg: e.dma_start(
                        out=wgu[b][:, :, gi, :],
                        in_=wsrc[l][:, fg * 256:(fg + 1) * 256].rearrange("(k p) f -> p k f", p=128)),
                        w=[("wgu", b, gi)], dma=True)
                for fc in range(2):
                    f = 2 * fg + fc
                    for T in tiles:
                        tsl = slice(T * 512, (T + 1) * 512)
                        tl = slice((T - 2 * half) * 512, (T - 2 * half + 1) * 512)
                        gb = rr["bank"] % 4
                        ub = 4 + rr["bank"] % 3
                        rr["bank"] += 1
                        for gi, bank in ((0, gb), (1, ub)):
                            for k in range(KC):
                                P.add("pe", lambda e, k=k, gi=gi, bank=bank, b=b, fc=fc, tsl=tsl: e.matmul(
                                    ps[bank][:], wgu[b][:, k, gi, fc * 128:(fc + 1) * 128], hT[:, k, tsl],
                                    start=(k == 0), stop=(k == KC - 1)),
                                    r=[("wgu", b, gi), hk(k, T)], w=[PS(bank)])
                        sb_ = rr["bank"] % 2
                        P.add("act", lambda e, gb=gb, sb_=sb_: e.activation(sgt[sb_][:], ps[gb][:], AF.Silu),
                              r=[PS(gb)], w=[("sgt", sb_)])
                        P.add("dve", lambda e, ub=ub, sb_=sb_, f=f, tl=tl: e.tensor_tensor(
                            out=aT[:, f, tl], in0=ps[ub][:], in1=sgt[sb_][:], op=ALU.mult),
                            r=[PS(ub), ("sgt", sb_)], w=[("a", f, T)])
            for T in tiles:
                tsl = slice(T * 512, (T + 1) * 512)
                tl = slice((T - 2 * half) * 512, (T - 2 * half + 1) * 512)
                for f in range(FCH):
                    b = rr.setdefault("wd", 0) % 4
                    rr["wd"] += 1
                    P.add("pool", lambda e, b=b, f=f: e.dma_start(out=wdb[b][:], in_=wd[l][f * 128:(f + 1) * 128, :]),
                          w=[("wdb", b)], dma=True)
                    for d in range(KC):
                        P.add("pe", lambda e, b=b, d=d, f=f, tl=tl: e.matmul(
                            ps[d][:], wdb[b][:, d * 128:(d + 1) * 128], aT[:, f, tl],
                            start=(f == 0), stop=(f == FCH - 1)),
                            r=[("wdb", b), ("a", f, T)], w=[PS(d)])
                for d in range(KC):
                    P.add("dve", lambda e, d=d, tsl=tsl: e.scalar_tensor_tensor(
                        out=xT[:, d, tsl], in0=ps[d][:], scalar=0.5, op0=ALU.mult, in1=xT[:, d, tsl], op1=ALU.add),
                        r=[PS(d), xk(d, T)], w=[xk(d, T)])

    def store_y(s):
        for T in range(4):
            tsl = slice(T * 512, (T + 1) * 512)
            rstd_for_tile(T, eps_t, 1.0 / D)
            for c in range(KC):
                P.add("dve", lambda e, c=c, tsl=tsl: e.scalar_tensor_tensor(
                    out=yTf[:, c, :], in0=xT[:, c, tsl], scalar=nwf[:, c:c + 1], op0=ALU.mult,
                    in1=rstd[:], op1=ALU.mult), r=[xk(c, T), "rstd"], w=[("yTf", c)])
            for j4 in range(4):
                j = 4 * T + j4
                b = j % 2
                for half in range(2):
                    bank = (2 * j + half) % 4
                    for i in range(4):
                        c = 4 * half + i
                        P.add("pe", lambda e, bank=bank, i=i, c=c, j4=j4: e.transpose(
                            ps[bank][:, i * 128:(i + 1) * 128], yTf[:, c, j4 * 128:(j4 + 1) * 128], ident_f[:]),
                            r=[("yTf", c), "ident_f"], w=[PS(bank)])
                    copy_op(evac_eng(), yo[b][:, half * 512:(half + 1) * 512], ps[bank][:],
                            r=[PS(bank)], w=[("yo", b, half)])
                P.add("sp", lambda e, b=b, j=j: e.dma_start(out=y[s, j * 128:(j + 1) * 128, :], in_=yo[b][:]),
                      r=[("yo", b, 0), ("yo", b, 1)], w=[("y", s, j)], dma=True)


    SLOPES = [2.0 ** (-8.0 * (h + 1) / 4) for h in range(4)]
    OQ, OK_, OV, OG = 2064, 2576, 3088, 3600
    o_daT = sb("o_daT", [128, 4, S], BF16, at=R_o)
    o_dnT = sb("o_dnT", [128, 4, S], BF16, at=R_o + 16384)
    a_o = R_in
    wqkv = []
    for b in range(2):
        wqkv.append(sb(f"wqkv{b}", [128, 3, KC, 128], BF16, at=a_o))
        a_o += 6144
    aqh1 = sb("aqh", [128, S], BF16, at=a_o)
    a_o += 4096
    akm = [sb(f"akm{m}", [128, S], BF16, at=a_o + 4096 * m) for m in range(2)]
    a_o += 8192
    avh1 = sb("avh", [128, 16, 128], BF16, at=a_o)
    a_o += 4096
    aqh = [aqh1, aqh1]
    avh = [avh1, avh1]
    pT = [sb(f"pT{b}", [128, 512], BF16, at=a_o + 1024 * b) for b in range(4)]
    a_o += 4096
    tmpf = [sb(f"tmpf{b}", [128, 512], F32, at=a_o + 2048 * b) for b in range(3)]
    a_o += 6144
    qaug_t = sb("qaug_t", [128, 2, 512], BF16, at=a_o)
    a_o += 2048
    assert a_o <= R_in + 48 * 1024, a_o
    f_o = R_free + 4096
    rz = sb("rz", [128, 512], F32, at=f_o)
    o0 = sb("o0", [128, 512], F32, at=f_o + 2048)
    t1 = sb("t1", [128, 512], F32, at=f_o + 4096)
    oc = sb("oc", [128, 512], F32, at=f_o + 6144)
    sqa = sb("sqa", [128, 512], F32, at=f_o + 8192)
    f_o += 10240
    tmpf.append(sb("tmpf3", [128, 512], F32, at=f_o))
    f_o += 2048
    for b_ in range(4, 6):
        pT.append(sb(f"pT{b_}", [128, 512], BF16, at=f_o))
        f_o += 1024
    assert f_o <= 229344, f_o

    def attention(l):
        P.add("dve", lambda e: e.memset(akm[0][64:128, :], 0.0), w=[("akz", 0)])
        P.add("dve", lambda e: e.memset(akm[1][0:64, :], 0.0), w=[("akz", 1)])
        for h in HEADS:
            b = h % 2
            for wi, o_col in enumerate((OQ, OK_, OV)):
                P.add("pool", lambda e, b=b, wi=wi, o_col=o_col, h=h: e.dma_start(
                    out=wqkv[b][:, wi, :, :],
                    in_=W["w_in"][l][:, o_col + h * 128:o_col + (h + 1) * 128].rearrange("(k p) f -> p k f", p=128)),
                    w=[("wqkv", b, wi)], dma=True)
            P.add("pool", lambda e, h=h: e.dma_start(out=qaug_t[:], in_=c_qaug[h].rearrange("s p q -> p s q")),
                  w=["qaug"], dma=True)
            for wi, dst, scl in ((0, aqh[b], 0.125), (1, None, 1.0)):
                for T in range(4):
                    tsl = slice(T * 512, (T + 1) * 512)
                    bank = rr["bank"] % 3
                    rr["bank"] += 1
                    for k in range(KC):
                        P.add("pe", lambda e, k=k, bank=bank, wi=wi, b=b, tsl=tsl: e.matmul(
                            ps[bank][:], wqkv[b][:, wi, k, :], hT[:, k, tsl], start=(k == 0), stop=(k == KC - 1)),
                            r=[("wqkv", b, wi), hk(k, T)], w=[PS(bank)])
                    if dst is not None:
                        P.add("act", lambda e, dst=dst, tsl=tsl, bank=bank, scl=scl: e.activation(
                            dst[:, tsl], ps[bank][:], AF.Copy, scale=scl), r=[PS(bank)], w=[("aqk", wi, 0, T)])
                    else:
                        P.add("act", lambda e, tsl=tsl, bank=bank: e.copy(akm[0][0:64, tsl], ps[bank][0:64, :]),
                              r=[PS(bank)], w=[("aqk", 1, 0, T)])
                        P.add("dve", lambda e, tsl=tsl, bank=bank: e.tensor_copy(akm[1][64:128, tsl], ps[bank][64:128, :]),
                              r=[PS(bank)], w=[("aqk", 1, 1, T)])
            for g in range(4):
                bank = rr["bank"] % 3
                rr["bank"] += 1
                for jj in range(4):
                    j = 4 * g + jj
                    for k in range(KC):
                        P.add("pe", lambda e, k=k, bank=bank, jj=jj, j=j, b=b: e.matmul(
                            ps[bank][:, jj * 128:(jj + 1) * 128], hT[:, k, j * 128:(j + 1) * 128], wqkv[b][:, 2, k, :],
                            start=(k == 0), stop=(k == KC - 1)),
                            r=[("wqkv", b, 2), hk(k, j // 4)], w=[PS(bank)])
                P.add("dve", lambda e, bank=bank, g=g, b=b: e.tensor_copy(
                    avh[b][:, 4 * g:4 * g + 4, :], ps[bank][:].rearrange("p (a t) -> p a t", a=4)),
                    r=[PS(bank)], w=[("avh", 0, g)])
            slope = SLOPES[h]
            tiles = []
            for G in range(4):
                gsl = slice(G * 512, (G + 1) * 512)
                for m in range(2):
                    msl = slice(64 * m, 64 * m + 64)
                    ob, zb = 4 + m, 6 + m
                    plan = []
                    for j in range(16):
                        r_ = j - 4 * G
                        off_ = 512 * G - 128 * j
                        if 0 <= r_ <= 3:
                            src = abs_t[:, 384 - 128 * r_:896 - 128 * r_]
                            coef, cb = -slope, 0.0
                            dmin, dmax = 0, max(128 * r_ + 127, 511 - 128 * r_)
                            sgn = None
                        elif off_ > 0:
                            src = bs_t[:]
                            coef, cb = -slope, -slope * off_
                            dmin, dmax = off_ - 127, off_ + 511
                            sgn = 0
                        else:
                            src = bs_t[:]
                            coef, cb = slope, slope * off_
                            dmin, dmax = -off_ - 511, -off_ + 127
                            sgn = 1
                        if -slope * dmin < -80.0:
                            continue
                        plan.append((j, src, coef, cb, sgn))
                    for (j, src, coef, cb, sgn) in plan:
                        first = (j == plan[0][0])
                        last = (j == plan[-1][0])
                        ti = len(tiles)
                        sbk, tb, pb = ti % 4, ti % 4, ti % 6

                        def fA(sbk=sbk, msl=msl, j=j, gsl=gsl, G=G, b=b, sgn=sgn):
                            mm_ = msl.start // 64
                            P.add("pe", lambda e: e.matmul(
                                ps[sbk][:], akm[mm_][:, j * 128:(j + 1) * 128], aqh[b][:, gsl], start=True, stop=(sgn is None)),
                                r=[("aqk", 0, 0, G), ("aqk", 1, mm_, j // 4), ("akz", mm_)], w=[PS(sbk)])
                            if sgn is not None:
                                P.add("pe", lambda e: e.matmul(
                                    ps[sbk][:], kaug_t[:, :], qaug_t[:, sgn, :], start=False, stop=True),
                                    r=["kaug_t", "qaug"], w=[PS(sbk)])

                        def fB(src=src, coef=coef, sbk=sbk, tb=tb, pb=pb, cb=cb, sgn=sgn):
                            if sgn is None:
                                P.add("dve", lambda e: e.scalar_tensor_tensor(
                                    out=tmpf[tb][:], in0=src, scalar=coef, op0=ALU.mult, in1=ps[sbk][:], op1=ALU.add),
                                    r=[PS(sbk)], w=[("tmpf", tb)])
                                P.add("act", lambda e: e.activation(
                                    pT[pb][:], tmpf[tb][:], AF.Exp, bias=cb), r=[("tmpf", tb)], w=[("pT", pb)])
                            else:
                                P.add("act", lambda e: e.activation(
                                    pT[pb][:], ps[sbk][:], AF.Exp, bias=cb), r=[PS(sbk)], w=[("pT", pb)])

                        def fC(ob=ob, zb=zb, j=j, pb=pb, first=first, last=last, b=b):
                            P.add("pe", lambda e: e.matmul(
                                ps[ob][:], avh[b][:, j, :], pT[pb][:], start=first, stop=last),
                                r=[("avh", 0, j // 4), ("pT", pb)], w=[PS(ob)])
                            P.add("pe", lambda e: e.matmul(
                                ps[zb][:], ones_b[:], pT[pb][:], start=first, stop=last),
                                r=[("pT", pb)], w=[PS(zb)])

                        fE = None
                        if last:
                            def fE(ob=ob, zb=zb, m=m, gsl=gsl, G=G, h=h):
                                P.add("dve", lambda e: e.reciprocal(rz[:], ps[zb][:]), r=[PS(zb)], w=["rz"])
                                if m == 0:
                                    P.add("dve", lambda e: e.tensor_tensor(out=o0[:], in0=ps[ob][:], in1=rz[:], op=ALU.mult),
                                          r=[PS(ob), "rz"], w=["o0"])
                                    return
                                P.add("dve", lambda e: e.tensor_tensor(out=t1[:], in0=ps[ob][:], in1=rz[:], op=ALU.mult),
                                      r=[PS(ob), "rz"], w=["t1"])
                                P.add("dve", lambda e: e.scalar_tensor_tensor(
                                    out=oc[:], in0=t1[:], scalar=neglam[:, l:l + 1], op0=ALU.mult, in1=o0[:], op1=ALU.add),
                                    r=["t1", "o0", "neglam"], w=["oc"])
                                P.add("act", lambda e: e.activation(sqa[:], oc[:], AF.Square), r=["oc"], w=["sqa"])
                                P.add("pe", lambda e: e.matmul(ps[5][:], ones_f[:], sqa[:], start=True, stop=True),
                                      r=["sqa"], w=[PS(5)])
                                P.add("act", lambda e: e.activation(lnt[:], ps[5][:], AF.Ln, bias=eps5_t[:], scale=1.0 / 128),
                                      r=[PS(5)], w=["lnt"])
                                P.add("act", lambda e: e.activation(rstd[:], lnt[:], AF.Exp, scale=-0.5), r=["lnt"], w=["rstd"])
                                P.add("dve", lambda e: e.scalar_tensor_tensor(
                                    out=o_daT[:, h, gsl], in0=oc[:], scalar=sublnw[:, l:l + 1], op0=ALU.mult, in1=rstd[:],
                                    op1=ALU.mult), r=["oc", "rstd", "sublnw"], w=[("oda", h, G)])
                        tiles.append((fA, fB, fC, fE))
            LA, ED = ATT_LA, ATT_ED
            nt = len(tiles)
            pend = []
            for idx in range(nt + LA + ED + 1):
                if idx < nt:
                    tiles[idx][0]()
                i = idx - LA
                if 0 <= i < nt:
                    tiles[i][1]()
                    tiles[i][2]()
                    if tiles[i][3] is not None:
                        pend.append((idx + ED, tiles[i][3]))
                while pend and pend[0][0] <= idx:
                    pend.pop(0)[1]()

    mT = sb("mT", [128, KC, S], BF16, at=R_in)
    wbr = [sb(f"wbr{i}", [128, 4, D], BF16, at=R_in + 32768 + 8192 * i) for i in range(2)]
    m_o = R_free + 4096
    gw = [sb(f"gw{b}", [128, KC, 2, 128], BF16, at=m_o + 4096 * b) for b in range(2)]
    m_o += 8192
    sg_ = [sb(f"sg{b}", [128, 512], F32, at=m_o + 2048 * b) for b in range(4)]
    m_o += 8192
    assert m_o <= 229344, m_o

    def merge(l, use_dn, use_da):
        for i, nm in enumerate(("w_branch_dn", "w_branch_da")):
            P.add("pool", lambda e, i=i, nm=nm: e.dma_start(
                out=wbr[i][:], in_=W[nm][l].rearrange("(k p) f -> p k f", p=128)), w=[("wbr", i)], dma=True)
        for dc in range(KC):
            b = dc % 2
            for gi in range(2):
                P.add("pool", lambda e, b=b, gi=gi, dc=dc: e.dma_start(
                    out=gw[b][:, :, gi, :],
                    in_=W["w_in"][l][:, OG + gi * D + dc * 128:OG + gi * D + (dc + 1) * 128].rearrange(
                        "(k p) f -> p k f", p=128)), w=[("gw", b, gi)], dma=True)
            for T in range(4):
                tsl = slice(T * 512, (T + 1) * 512)
                base = 4 * (rr["bank"] % 2)
                rr["bank"] += 1
                for i, (src, key) in enumerate(((o_dnT, "odn"), (o_daT, "oda"))):
                    for hc in range(4):
                        P.add("pe", lambda e, i=i, hc=hc, src=src, base=base, dc=dc, tsl=tsl: e.matmul(
                            ps[base + i][:], wbr[i][:, hc, dc * 128:(dc + 1) * 128], src[:, hc, tsl],
                            start=(hc == 0), stop=(hc == 3)),
                            r=[("wbr", i), (key, hc, T)], w=[PS(base + i)])
                for gi in range(2):
                    for k in range(KC):
                        P.add("pe", lambda e, gi=gi, k=k, b=b, base=base, tsl=tsl: e.matmul(
                            ps[base + 2 + gi][:], gw[b][:, k, gi, :], hT[:, k, tsl], start=(k == 0), stop=(k == KC - 1)),
                            r=[("gw", b, gi), hk(k, T)], w=[PS(base + 2 + gi)])
                for gi in range(2):
                    P.add("act", lambda e, gi=gi, base=base: e.activation(sg_[gi][:], ps[base + 2 + gi][:], AF.Sigmoid),
                          r=[PS(base + 2 + gi)], w=[("sg", gi)])
                for gi in range(2):
                    P.add("dve", lambda e, gi=gi, base=base: e.tensor_tensor(
                        out=sg_[2 + gi][:], in0=ps[base + gi][:], in1=sg_[gi][:], op=ALU.mult),
                        r=[PS(base + gi), ("sg", gi)], w=[("sg", 2 + gi)])
                P.add("dve", lambda e, dc=dc, tsl=tsl: e.tensor_tensor(
                    out=mT[:, dc, tsl], in0=sg_[2][:], in1=sg_[3][:], op=ALU.add),
                    r=[("sg", 2), ("sg", 3)], w=[("m", dc, T)])
        for do in range(KC):
            b = do % 2
            P.add("pool", lambda e, b=b, do=do: e.dma_start(
                out=gw[b][:].rearrange("p k g f -> p k (g f)")[:, :, 0:128],
                in_=W["w_out"][l][:, do * 128:(do + 1) * 128].rearrange("(k p) f -> p k f", p=128)),
                w=[("gw", b, 0), ("gw", b, 1)], dma=True)
            for T in range(4):
                tsl = slice(T * 512, (T + 1) * 512)
                bank = rr["bank"] % 7
                rr["bank"] += 1
                for k in range(KC):
                    P.add("pe", lambda e, k=k, b=b, bank=bank, tsl=tsl: e.matmul(
                        ps[bank][:], gw[b][:].rearrange("p k g f -> p k (g f)")[:, k, 0:128], mT[:, k, tsl],
                        start=(k == 0), stop=(k == KC - 1)),
                        r=[("gw", b, 0), ("gw", b, 1), ("m", k, T)], w=[PS(bank)])
                P.add("dve", lambda e, do=do, tsl=tsl, bank=bank: e.tensor_tensor(
                    out=xT[:, do, tsl], in0=ps[bank][:], in1=xT[:, do, tsl], op=ALU.add),
                    r=[PS(bank), xk(do, T)], w=[xk(do, T)])


    qT = sb("qT", [128, 4, S], BF16, at=R_in)
    kT = sb("kT", [128, 4, S], BF16, at=R_in + 16384)
    vT = sb("vT", [128, 4, S], BF16, at=R_in + 32768)
    segs = [[R_o, R_o + 16384], [R_free + 4096, 229344]]

    def dalloc(segs_, name, shape, dtype):
        esz = 4 if dtype == F32 else 2
        nb = esz
        for s_ in shape[1:]:
            nb *= s_
        nb = ((nb + 31) // 32) * 32
        for sg in segs_:
            if sg[0] + nb <= sg[1]:
                t = nc.alloc_sbuf_tensor_at(name, list(shape), dtype, offset=sg[0])
                sg[0] += nb
                return t
        raise RuntimeError("dn scratch full: " + name)

    ba = dalloc(segs[1:], "dn_ba", [128, 16, 16], F32)
    ctail = segs[1][0]
    wc = [dalloc(segs, f"dn_wc{b}", [128, KC, 128], BF16) for b in range(2)]
    pc = dalloc(segs, "dn_pc", [128, S + 4], F32)
    cv = dalloc(segs, "dn_cv", [128, S], F32)
    wba = dalloc(segs, "dn_wba", [128, KC, 16], BF16)
    dsq = dalloc(segs, "dn_sq", [128, 512], F32)
    segs1 = segs + [[R_o + 16384, R_o + 32768]]
    pcs = [pc, dalloc(segs1, "dn_pc2", [128, S + 4], F32)]
    cvs = [cv, dalloc(segs1, "dn_cv2", [128, S], F32)]
    l2s = [(dsq, lnt, rstd, 7, 0),
           (dalloc(segs1, "dn_sq2", [128, 512], F32), dalloc(segs1, "dn_lnt2", [128, 512], F32),
            dalloc(segs1, "dn_rstd2", [128, 512], F32), 6, 1)]
    csegs = [[R_h, R_h + 32768], [R_o, R_o + 16384], [R_free, R_free + 4096], [ctail, 229344]]
    Sf = dalloc(csegs, "dn_Sf", [128, 8, 128], F32)
    Sb = dalloc(csegs, "dn_Sb", [128, 8, 128], BF16)
    mk_f = dalloc(csegs, "dn_mkf", [128, 2, 128], F32)
    mk4 = dalloc(csegs, "dn_mk4", [128, 4, 512], BF16)
    sc = {}
    for nm in ("beta", "nbeta", "g", "gc", "gtot", "egc", "negc", "ekd", "egl", "tmpa"):
        sc[nm] = dalloc(csegs, "dn_" + nm, [128, 16, 8], F32)
    abc = dalloc(csegs, "dn_abc", [128, 2, 8], F32)
    Wt = [[dalloc(csegs, f"dn_W{d}{b}", [128, 4, 128], BF16) for b in range(2)] for d in range(2)]
    aTt = [[dalloc(csegs, f"dn_aT{d}{b}", [128, 4, 128], BF16) for b in range(2)] for d in range(2)]
    kdt = [[dalloc(csegs, f"dn_kd{d}{b}", [128, 4, 128], BF16) for b in range(2)] for d in range(2)]
    vtk = [[dalloc(csegs, f"dn_vt{d}{b}", [128, 4, 128], BF16) for b in range(2)] for d in range(2)]
    Mt = [[dalloc(csegs, f"dn_M{d}{b}", [128, 4, 128], F32) for b in range(2)] for d in range(2)]
    Nt = [[dalloc(csegs, f"dn_N{d}{b}", [128, 4, 128], F32) for b in range(2)] for d in range(2)]
    Wf = [dalloc(csegs, f"dn_Wf{d}", [128, 4, 128], F32) for d in range(2)]
    Gbc = [dalloc(csegs, f"dn_Gbc{d}", [128, 4, 128], F32) for d in range(2)]
    Et = [dalloc(csegs, f"dn_Et{d}", [128, 4, 128], F32) for d in range(2)]
    tinc = Et
    tstr = [dalloc(csegs, f"dn_tstr{d}", [128, 4, 128], F32) for d in range(2)]
    r0t = [dalloc(csegs, f"dn_r0{d}", [128, 4, 128], BF16) for d in range(2)]
    dlt = [dalloc(csegs, f"dn_dl{d}", [128, 4, 128], BF16) for d in range(2)]
    qsg = Gbc
    otf = tstr
    junk = Gbc
    onb = r0t
    ssn = dalloc(csegs, "dn_ssn", [128, 2, 8], F32)
    wz = [sb(f"dn_wz{b}", [128, KC, 128], BF16, at=R_o + 2048 * b) for b in range(2)]
    zt = [sb(f"dn_zt{b}", [128, 512], F32, at=R_o + 4096 + 2048 * b) for b in range(2)]

    def psb(b_):
        return ps[b_][:].bitcast(BF16)

    def nb_():
        rr["bank"] += 1
        return rr["bank"] % 8

    def deltanet(l):
        for pb_ in range(2):
            P.add("dve", lambda e, pb_=pb_: e.memset(pcs[pb_][:, 0:2], 0.0), w=[("pcpad", pb_)])
            P.add("dve", lambda e, pb_=pb_: e.memset(pcs[pb_][:, S + 2:S + 4], 0.0), w=[("pcpad", pb_)])

        def stA(cc):
            wb = cc % 2
            pcb = pcs[cc % 2]
            P.add("pool", lambda e: e.dma_start(
                out=wc[wb][:], in_=W["w_in"][l][:, cc * 128:(cc + 1) * 128].rearrange("(k p) f -> p k f", p=128)),
                w=[("dwc", wb)], dma=True)
            for T in range(4):
                tsl = slice(T * 512, (T + 1) * 512)
                bank = nb_() % 4
                for k in range(KC):
                    P.add("pe", lambda e, k=k, bank=bank, tsl=tsl: e.matmul(
                        ps[bank][:], wc[wb][:, k, :], hT[:, k, tsl], start=(k == 0), stop=(k == KC - 1)),
                        r=[("dwc", wb), hk(k, T)], w=[PS(bank)])
                P.add("act", lambda e, bank=bank, T=T: e.copy(pcb[:, 2 + T * 512:2 + (T + 1) * 512], ps[bank][:]),
                      r=[PS(bank)], w=[("pc", cc % 2, T)])

        def stB1(cc):
            h = cc % 4
            pcb = pcs[cc % 2]
            cvb = cvs[cc % 2]
            ck = ("cv", cc % 2)
            pck = [("pc", cc % 2, T) for T in range(4)] + [("pcpad", cc % 2)]
            P.add("dve", lambda e: e.tensor_scalar(out=cvb[:], in0=pcb[:, 0:S], scalar1=convw[:, l, cc, 0:1],
                                                  scalar2=None, op0=ALU.mult), r=pck, w=[ck])
            for kk in range(1, 5):
                P.add("dve", lambda e, kk=kk: e.scalar_tensor_tensor(
                    out=cvb[:], in0=pcb[:, kk:kk + S], scalar=convw[:, l, cc, kk:kk + 1], op0=ALU.mult,
                    in1=cvb[:], op1=ALU.add), r=pck + [ck], w=[ck])
            if cc >= 8:
                P.add("act", lambda e: e.activation(vT[:, h, :], cvb[:], AF.Silu), r=[ck],
                      w=[("vT", h, T) for T in range(4)])
            else:
                P.add("act", lambda e: e.activation(cvb[:], cvb[:], AF.Silu), r=[ck], w=[ck])

        def stB2(cc):
            if cc >= 8:
                return
            h = cc % 4
            cvb = cvs[cc % 2]
            ck = ("cv", cc % 2)
            dst = qT if cc < 4 else kT
            nm = "qT" if cc < 4 else "kT"
            scl = (128.0 ** -0.5) if cc < 4 else 1.0
            def sq_op(T):
                tsl = slice(T * 512, (T + 1) * 512)
                sq_, ln_, rs_, bk_, si = l2s[T % 2]
                P.add("act", lambda e: e.activation(sq_[:], cvb[:, tsl], AF.Square), r=[ck], w=[("l2sq", si)])

            sq_op(0)
            sq_op(1)
            for T in range(4):
                sq_, ln_, rs_, bk_, si = l2s[T % 2]
                tsl = slice(T * 512, (T + 1) * 512)
                P.add("pe", lambda e, sq_=sq_, bk_=bk_: e.matmul(ps[bk_][:], ones_f[:], sq_[:], start=True, stop=True),
                      r=[("l2sq", si)], w=[PS(bk_)])
                if T + 2 < 4:
                    sq_op(T + 2)
                P.add("act", lambda e, ln_=ln_, bk_=bk_: e.activation(ln_[:], ps[bk_][:], AF.Ln, bias=eps_t[:], scale=1.0),
                      r=[PS(bk_)], w=[("l2ln", si)])
                P.add("act", lambda e, ln_=ln_, rs_=rs_: e.activation(rs_[:], ln_[:], AF.Exp, scale=-0.5),
                      r=[("l2ln", si)], w=[("l2rs", si)])
                P.add("dve", lambda e, tsl=tsl, T=T, rs_=rs_: e.scalar_tensor_tensor(
                    out=dst[:, h, tsl], in0=cvb[:, tsl], scalar=scl, op0=ALU.mult, in1=rs_[:], op1=ALU.mult),
                    r=[ck, ("l2rs", si)], w=[(nm, h, T)])

        for i_ in range(12 + 2):
            if i_ < 12:
                stA(i_)
            if 0 <= i_ - 1 < 12:
                stB1(i_ - 1)
            if 0 <= i_ - 2 < 12:
                stB2(i_ - 2)
        P.add("pool", lambda e: e.dma_start(out=wba[:], in_=W["w_in"][l][:, 2048:2064].rearrange("(k p) f -> p k f", p=128)),
              w=["wba"], dma=True)
        for j in range(16):
            for k in range(KC):
                P.add("pe", lambda e, j=j, k=k: e.matmul(ps[3][:, j * 16:(j + 1) * 16], hT[:, k, j * 128:(j + 1) * 128],
                                                        wba[:, k, :], start=(k == 0), stop=(k == KC - 1)),
                      r=["wba", hk(k, j // 4)], w=[PS(3)])
        P.add("dve", lambda e: e.tensor_copy(ba[:].rearrange("p j c -> p (j c)"), ps[3][:, 0:256]), r=[PS(3)], w=["ba"])
        P.barrier(bar[:])
        P.add("sp", lambda e: e.dma_start(out=mk_f[:, 0, :], in_=c_masks[0]), w=["mk_f"], dma=True)
        P.add("sp", lambda e: e.dma_start(out=mk_f[:, 1, :], in_=c_masks[2]), w=["mk_f"], dma=True)
        for mi in range(4):
            for h in range(4):
                P.add("pool", lambda e, mi=mi, h=h: e.dma_start(out=mk4[:, mi, h * 128:(h + 1) * 128], in_=c_masks[mi]),
                      w=["mk4"], dma=True)
        P.add("sp", lambda e: e.dma_start(out=abc[:, 0, :], in_=W["dn_a_log"][l].rearrange("a b -> (a b)").partition_broadcast(128)),
              w=["abc"], dma=True)
        P.add("sp", lambda e: e.dma_start(out=abc[:, 1, :], in_=W["dn_dt_bias"][l].rearrange("a b -> (a b)").partition_broadcast(128)),
              w=["abc"], dma=True)
        P.add("act", lambda e: e.activation(abc[:, 0, :], abc[:, 0, :], AF.Exp), r=["abc"], w=["abc"])
        P.add("dve", lambda e: e.tensor_scalar(out=abc[:, 0, :], in0=abc[:, 0, :], scalar1=-1.0, scalar2=None, op0=ALU.mult),
              r=["abc"], w=["abc"])
        for j in range(16):
            P.add("dve", lambda e, j=j: e.tensor_tensor(out=sc["tmpa"][:, j, :], in0=ba[:, j, 8:16], in1=abc[:, 1, :], op=ALU.add),
                  r=["ba", "abc"], w=["tmpa"])
        P.add("act", lambda e: e.activation(sc["tmpa"][:], sc["tmpa"][:], AF.Exp), r=["tmpa"], w=["tmpa"])
        P.add("act", lambda e: e.activation(sc["tmpa"][:], sc["tmpa"][:], AF.Ln, bias=1.0), r=["tmpa"], w=["tmpa"])
        for j in range(16):
            P.add("dve", lambda e, j=j: e.tensor_tensor(out=sc["g"][:, j, :], in0=sc["tmpa"][:, j, :], in1=abc[:, 0, :], op=ALU.mult),
                  r=["tmpa", "abc"], w=["g"])
        for j in range(16):
            P.add("act", lambda e, j=j: e.activation(sc["beta"][:, j, :], ba[:, j, 0:8], AF.Exp, scale=-1.0), r=["ba"], w=["beta"])
        P.add("dve", lambda e: e.tensor_scalar(out=sc["beta"][:], in0=sc["beta"][:], scalar1=1.0, scalar2=None, op0=ALU.add),
              r=["beta"], w=["beta"])
        P.add("dve", lambda e: e.reciprocal(sc["beta"][:], sc["beta"][:]), r=["beta"], w=["beta"])
        P.add("dve", lambda e: e.tensor_scalar(out=sc["nbeta"][:], in0=sc["beta"][:], scalar1=-1.0, scalar2=None, op0=ALU.mult),
              r=["beta"], w=["nbeta"])
        for j in range(16):
            P.add("pe", lambda e, j=j: e.matmul(ps[3][:, j * 8:j * 8 + 4], mk_f[:, 0, :], sc["g"][:, j, 0:4], start=True, stop=True),
                  r=["g", "mk_f"], w=[PS(3)])
            P.add("pe", lambda e, j=j: e.matmul(ps[3][:, j * 8 + 4:j * 8 + 8], mk_f[:, 1, :], sc["g"][:, j, 4:8], start=True, stop=True),
                  r=["g", "mk_f"], w=[PS(3)])
            P.add("pe", lambda e, j=j: e.matmul(ps[4][:, j * 8:j * 8 + 8], ones_f[:], sc["g"][:, j, :], start=True, stop=True),
                  r=["g"], w=[PS(4)])
        P.add("dve", lambda e: e.tensor_copy(sc["gc"][:].rearrange("p j c -> p (j c)"), ps[3][:, 0:128]), r=[PS(3)], w=["gc"])
        P.add("dve", lambda e: e.tensor_copy(sc["gtot"][:].rearrange("p j c -> p (j c)"), ps[4][:, 0:128]), r=[PS(4)], w=["gtot"])
        P.add("act", lambda e: e.activation(sc["egc"][:], sc["gc"][:], AF.Exp), r=["gc"], w=["egc"])
        P.add("dve", lambda e: e.tensor_scalar(out=sc["negc"][:], in0=sc["egc"][:], scalar1=-1.0, scalar2=None, op0=ALU.mult),
              r=["egc"], w=["negc"])
        P.add("dve", lambda e: e.tensor_tensor(out=sc["ekd"][:], in0=sc["gtot"][:], in1=sc["gc"][:], op=ALU.subtract),
              r=["gtot", "gc"], w=["ekd"])
        P.add("act", lambda e: e.activation(sc["ekd"][:], sc["ekd"][:], AF.Exp), r=["ekd"], w=["ekd"])
        P.add("act", lambda e: e.activation(sc["egl"][:], sc["gtot"][:], AF.Exp), r=["gtot"], w=["egl"])
        P.add("dve", lambda e: e.memset(Sf[:], 0.0), w=[("Sf", d) for d in range(2)])
        P.add("dve", lambda e: e.memset(Sb[:], 0.0), w=[("Sb", d) for d in range(2)])

        def pre(dr, j, bf):
            jsl = slice(j * 128, (j + 1) * 128)
            Tj = j // 4
            inc4 = mk4[:, 0 + 2 * dr, :]
            str4 = mk4[:, 1 + 2 * dr, :]
            bk = nb_()
            for h in range(4):
                P.add("pe", lambda e, h=h, bk=bk: e.transpose(psb(bk)[:, h * 128:(h + 1) * 128], kT[:, h, jsl], ident_b[:]),
                      r=[("kT", h, Tj)], w=[PS(bk)])
            for h in range(4):
                P.add("act", lambda e, h=h, bk=bk: e.activation(kdt[dr][bf][:, h, :], psb(bk)[:, h * 128:(h + 1) * 128], AF.Copy,
                                                              scale=sc["ekd"][:, j, dr * 4 + h:dr * 4 + h + 1]),
                      r=[PS(bk), "ekd"], w=[("kd", dr, bf)])
            bv = nb_()
            for h in range(4):
                P.add("pe", lambda e, h=h, bv=bv: e.transpose(psb(bv)[:, h * 128:(h + 1) * 128], vT[:, h, jsl], ident_b[:]),
                      r=[("vT", h, Tj)], w=[PS(bv)])
            P.add("dve", lambda e, bv=bv: e.tensor_copy(vtk[dr][bf][:].rearrange("p h e -> p (h e)"), psb(bv)[:, 0:512]),
                  r=[PS(bv)], w=[("vt", dr, bf)])
            for h in range(4):
                P.add("dve", lambda e, h=h: e.tensor_scalar(out=Gbc[dr][:, h, :], in0=ones_f[:], scalar1=sc["g"][:, j, dr * 4 + h:dr * 4 + h + 1],
                                                          scalar2=None, op0=ALU.mult), r=["g"], w=[("Gbc", dr)])
            bg = nb_()
            for h in range(4):
                P.add("pe", lambda e, h=h, bg=bg: e.matmul(ps[bg][:, h * 128:(h + 1) * 128], Gbc[dr][:, h, :], mk_f[:, dr, :],
                                                          start=True, stop=True), r=[("Gbc", dr), "mk_f"], w=[PS(bg)])
            yield
            for h in range(4):
                P.add("dve", lambda e, h=h, bg=bg: e.tensor_scalar(
                    out=Et[dr][:, h, :], in0=ps[bg][:, h * 128:(h + 1) * 128], scalar1=sc["gc"][:, j, dr * 4 + h:dr * 4 + h + 1],
                    scalar2=0.0, op0=ALU.subtract, op1=ALU.min), r=[PS(bg), "gc"], w=[("Et", dr)])
            P.add("act", lambda e: e.activation(Et[dr][:], Et[dr][:], AF.Exp), r=[("Et", dr)], w=[("Et", dr)])
            yield
            Etf = Et[dr][:].rearrange("p h c -> p (h c)")
            P.add("dve", lambda e: e.tensor_tensor(out=tstr[dr][:].rearrange("p h c -> p (h c)"), in0=Etf, in1=str4, op=ALU.mult),
                  r=[("Et", dr), "mk4"], w=[("tstr", dr)])
            P.add("dve", lambda e: e.tensor_tensor(out=Etf, in0=Etf, in1=inc4, op=ALU.mult),
                  r=[("Et", dr), "mk4"], w=[("Et", dr)])
            bkk, bkq = nb_(), nb_()
            for h in range(4):
                P.add("pe", lambda e, h=h, bkk=bkk: e.matmul(ps[bkk][:, h * 128:(h + 1) * 128], kT[:, h, jsl], kT[:, h, jsl],
                                                            start=True, stop=True), r=[("kT", h, Tj)], w=[PS(bkk)])
            for h in range(4):
                P.add("pe", lambda e, h=h, bkq=bkq: e.matmul(ps[bkq][:, h * 128:(h + 1) * 128], kT[:, h, jsl], qT[:, h, jsl],
                                                            start=True, stop=True), r=[("kT", h, Tj), ("qT", h, Tj)], w=[PS(bkq)])
            yield
            P.add("dve", lambda e, bkq=bkq: e.tensor_tensor(out=aTt[dr][bf][:].rearrange("p h c -> p (h c)"), in0=ps[bkq][:],
                                                          in1=tinc[dr][:].rearrange("p h c -> p (h c)"), op=ALU.mult),
                  r=[PS(bkq), ("Et", dr)], w=[("aT", dr, bf)])
            for h in range(4):
                P.add("dve", lambda e, h=h, bkk=bkk: e.scalar_tensor_tensor(
                    out=Mt[dr][0][:, h, :], in0=ps[bkk][:, h * 128:(h + 1) * 128], scalar=sc["nbeta"][:, j, dr * 4 + h:dr * 4 + h + 1],
                    op0=ALU.mult, in1=tstr[dr][:, h, :], op1=ALU.mult), r=[PS(bkk), ("tstr", dr), "nbeta"], w=[("M", dr, 0)])
            yield
            bn = nb_()
            for h in range(4):
                P.add("pe", lambda e, h=h, bn=bn: e.transpose(ps[bn][:, h * 128:(h + 1) * 128], Mt[dr][0][:, h, :], ident_f[:]),
                      r=[("M", dr, 0)], w=[PS(bn)])
            yield
            P.add("act", lambda e, bn=bn: e.copy(Nt[dr][0][:].rearrange("p h c -> p (h c)"), ps[bn][:]), r=[PS(bn)], w=[("N", dr, 0)])
            Wc = Wf[dr]
            for h in range(4):
                P.add("dve", lambda e, h=h: e.tensor_tensor(out=Wc[:, h, :], in0=Mt[dr][0][:, h, :], in1=ident_f[:], op=ALU.add),
                      r=[("M", dr, 0)], w=[("Wf", dr)])
            yield
            cur = 0
            for lev in range(6):
                nxt = 1 - cur
                if lev < 5:
                    bx = nb_()
                    for h in range(4):
                        P.add("pe", lambda e, h=h, bx=bx, cur=cur: e.matmul(ps[bx][:, h * 128:(h + 1) * 128], Nt[dr][cur][:, h, :], Mt[dr][cur][:, h, :],
                                                                          start=True, stop=True), r=[("M", dr, cur), ("N", dr, cur)], w=[PS(bx)])
                by = nb_()
                for h in range(4):
                    P.add("pe", lambda e, h=h, by=by, cur=cur: e.matmul(ps[by][:, h * 128:(h + 1) * 128], Mt[dr][cur][:, h, :], Nt[dr][cur][:, h, :],
                                                                      start=True, stop=True), r=[("M", dr, cur), ("N", dr, cur)], w=[PS(by)])
                yield
                if lev < 5:
                    P.add("dve", lambda e, bx=bx, nxt=nxt: e.tensor_copy(Mt[dr][nxt][:].rearrange("p h c -> p (h c)"), ps[bx][:]),
                          r=[PS(bx)], w=[("M", dr, nxt)])
                P.add("act", lambda e, by=by, nxt=nxt: e.copy(Nt[dr][nxt][:].rearrange("p h c -> p (h c)"), ps[by][:]),
                      r=[PS(by)], w=[("N", dr, nxt)])
                yield
                bz = nb_()
                for h in range(4):
                    P.add("pe", lambda e, h=h, bz=bz, nxt=nxt: e.matmul(ps[bz][:, h * 128:(h + 1) * 128], Nt[dr][nxt][:, h, :], Wc[:, h, :],
                                                                      start=True, stop=True), r=[("N", dr, nxt), ("Wf", dr)], w=[PS(bz)])
                yield
                P.add("dve", lambda e, bz=bz: e.tensor_tensor(out=Wc[:].rearrange("p h c -> p (h c)"), in0=ps[bz][:],
                                                            in1=Wc[:].rearrange("p h c -> p (h c)"), op=ALU.add),
                      r=[PS(bz), ("Wf", dr)], w=[("Wf", dr)])
                yield
                cur = nxt
            P.add("act", lambda e: e.copy(Wt[dr][bf][:], Wf[dr][:]), r=[("Wf", dr)], w=[("W", dr, bf)])

        def chain(dr, j, bf, second):
            jsl = slice(j * 128, (j + 1) * 128)
            Tj = j // 4
            ba_, bb_ = nb_(), nb_()
            for h in range(4):
                P.add("pe", lambda e, h=h, ba_=ba_: e.matmul(ps[ba_][:, h * 128:(h + 1) * 128], kT[:, h, jsl], Sb[:, dr * 4 + h, :],
                                                            start=True, stop=True), r=[("kT", h, Tj), ("Sb", dr)], w=[PS(ba_)])
            for h in range(4):
                P.add("pe", lambda e, h=h, bb_=bb_: e.matmul(ps[bb_][:, h * 128:(h + 1) * 128], qT[:, h, jsl], Sb[:, dr * 4 + h, :],
                                                            start=True, stop=True), r=[("qT", h, Tj), ("Sb", dr)], w=[PS(bb_)])
            yield
            for h in range(4):
                P.add("dve", lambda e, h=h, ba_=ba_: e.scalar_tensor_tensor(
                    out=r0t[dr][:, h, :], in0=ps[ba_][:, h * 128:(h + 1) * 128], scalar=sc["negc"][:, j, dr * 4 + h:dr * 4 + h + 1],
                    op0=ALU.mult, in1=vtk[dr][bf][:, h, :], op1=ALU.add), r=[PS(ba_), ("vt", dr, bf), "negc"], w=[("r0", dr)])
            yield
            bc_ = nb_()
            for h in range(4):
                P.add("pe", lambda e, h=h, bc_=bc_: e.matmul(ps[bc_][:, h * 128:(h + 1) * 128], Wt[dr][bf][:, h, :], r0t[dr][:, h, :],
                                                            start=True, stop=True), r=[("W", dr, bf), ("r0", dr)], w=[PS(bc_)])
            yield
            for h in range(4):
                P.add("act", lambda e, h=h, bc_=bc_: e.activation(dlt[dr][:, h, :], ps[bc_][:, h * 128:(h + 1) * 128], AF.Copy,
                                                                scale=sc["beta"][:, j, dr * 4 + h:dr * 4 + h + 1]),
                      r=[PS(bc_), "beta"], w=[("dl", dr)])
            yield
            bd_, be_ = nb_(), nb_()
            for h in range(4):
                P.add("pe", lambda e, h=h, bd_=bd_: e.matmul(ps[bd_][:, h * 128:(h + 1) * 128], kdt[dr][bf][:, h, :], dlt[dr][:, h, :],
                                                            start=True, stop=True), r=[("kd", dr, bf), ("dl", dr)], w=[PS(bd_)])
            for h in range(4):
                P.add("pe", lambda e, h=h, be_=be_: e.matmul(ps[be_][:, h * 128:(h + 1) * 128], aTt[dr][bf][:, h, :], dlt[dr][:, h, :],
                                                            start=True, stop=True), r=[("aT", dr, bf), ("dl", dr)], w=[PS(be_)])
            for h in range(4):
                P.add("act", lambda e, h=h, bb_=bb_: e.activation(qsg[dr][:, h, :], ps[bb_][:, h * 128:(h + 1) * 128], AF.Copy,
                                                                scale=sc["egc"][:, j, dr * 4 + h:dr * 4 + h + 1]),
                      r=[PS(bb_), "egc"], w=[("Gbc", dr)])
            yield
            if not second:
                P.add("dve", lambda e, be_=be_: e.tensor_tensor(out=o_dnT[:, :, jsl], in0=ps[be_][:].rearrange("p (h c) -> p h c", h=4), in1=qsg[dr][:],
                                                              op=ALU.add), r=[PS(be_), ("Gbc", dr)], w=[("otok", j)])
            else:
                P.add("dve", lambda e, be_=be_: e.tensor_tensor(out=otf[dr][:].rearrange("p h c -> p (h c)"), in0=ps[be_][:],
                                                              in1=qsg[dr][:].rearrange("p h c -> p (h c)"), op=ALU.add),
                      r=[PS(be_), ("Gbc", dr)], w=[("tstr", dr)])
                P.add("dve", lambda e: e.tensor_tensor(out=otf[dr][:], in0=otf[dr][:],
                                                     in1=o_dnT[:, :, jsl], op=ALU.add), r=[("tstr", dr), ("otok", j)], w=[("tstr", dr)])
            for h in range(4):
                P.add("dve", lambda e, h=h, bd_=bd_: e.scalar_tensor_tensor(
                    out=Sf[:, dr * 4 + h, :], in0=Sf[:, dr * 4 + h, :], scalar=sc["egl"][:, j, dr * 4 + h:dr * 4 + h + 1],
                    op0=ALU.mult, in1=ps[bd_][:, h * 128:(h + 1) * 128], op1=ALU.add), r=[PS(bd_), ("Sf", dr), "egl"], w=[("Sf", dr)])
            P.add("act", lambda e: e.copy(Sb[:, dr * 4:dr * 4 + 4, :], Sf[:, dr * 4:dr * 4 + 4, :]), r=[("Sf", dr)], w=[("Sb", dr)])
            yield
            if second:
                for h in range(4):
                    P.add("act", lambda e, h=h: e.activation(junk[dr][:, h, :], otf[dr][:, h, :], AF.Square, accum_out=ssn[:, dr, h:h + 1]),
                          r=[("tstr", dr)], w=[("Gbc", dr), ("ssn", dr)])
                P.add("act", lambda e: e.activation(ssn[:, dr, 4:8], ssn[:, dr, 0:4], AF.Ln, bias=eps_t[:], scale=1.0 / 128),
                      r=[("ssn", dr)], w=[("ssn", dr)])
                P.add("act", lambda e: e.activation(ssn[:, dr, 4:8], ssn[:, dr, 4:8], AF.Exp, scale=-0.5), r=[("ssn", dr)], w=[("ssn", dr)])
                yield
                for h in range(4):
                    P.add("dve", lambda e, h=h: e.tensor_scalar(out=onb[dr][:, h, :], in0=otf[dr][:, h, :], scalar1=ssn[:, dr, 4 + h:5 + h],
                                                              scalar2=None, op0=ALU.mult), r=[("tstr", dr), ("ssn", dr)], w=[("r0", dr)])
                bt = nb_()
                for h in range(4):
                    P.add("pe", lambda e, h=h, bt=bt: e.transpose(psb(bt)[:, h * 128:(h + 1) * 128], onb[dr][:, h, :], ident_b[:]),
                          r=[("r0", dr)], w=[PS(bt)])
                yield
                P.add("dve", lambda e, bt=bt: e.tensor_scalar(
                    out=o_dnT[:, :, jsl], in0=psb(bt)[:, 0:512].rearrange("p (h c) -> p h c", h=4), scalar1=wdn[:, l:l + 1],
                    scalar2=None, op0=ALU.mult), r=[PS(bt), "wdn"], w=[("odn", h, Tj) for h in range(4)] + [("otok", j)])

        def stream(dr):
            jm = (lambda st: st) if dr == 0 else (lambda st: 15 - st)
            for step in range(17):
                if step < 16:
                    yield from pre(dr, jm(step), step % 2)
                if step > 0:
                    st = step - 1
                    yield from chain(dr, jm(st), st % 2, st >= 8)

        alive = [stream(0), stream(1)]
        while alive:
            for g_ in list(alive):
                try:
                    next(g_)
                except StopIteration:
                    alive.remove(g_)
        if l == 0:
            dump("qT", qT[:].rearrange("p h s -> p (h s)")); dump("kT", kT[:].rearrange("p h s -> p (h s)"))
            dump("vT", vT[:].rearrange("p h s -> p (h s)"))
            for nm_ in ("beta", "g", "gc", "gtot", "egc", "ekd", "egl"):
                dump("sc_" + nm_, sc[nm_][:].rearrange("p j c -> p (j c)"))
            dump("ba", ba[:].rearrange("p j c -> p (j c)"))
            dump("W00", Wt[0][0][:].rearrange("p h c -> p (h c)")); dump("aT00", aTt[0][0][:].rearrange("p h c -> p (h c)"))
            dump("kd00", kdt[0][0][:].rearrange("p h c -> p (h c)")); dump("Sf", Sf[:].rearrange("p h c -> p (h c)"))
            dump("odnT", o_dnT[:].rearrange("p h s -> p (h s)"))
            dump("Et", Et[0][:].rearrange("p h c -> p (h c)")); dump("M0", Mt[0][0][:].rearrange("p h c -> p (h c)"))
        P.barrier(bar[:])
        for T in range(4):
            norm_to_h(T, nwm[:, l, :], sqm)
        for h in range(4):
            b = h % 2
            P.add("pool", lambda e, b=b, h=h: e.dma_start(
                out=wz[b][:], in_=W["w_in"][l][:, 1536 + h * 128:1536 + (h + 1) * 128].rearrange("(k p) f -> p k f", p=128)),
                w=[("wz", b)], dma=True)
            for T in range(4):
                tsl = slice(T * 512, (T + 1) * 512)
                bank = nb_() % 4
                zb_ = nb_() % 2
                for k in range(KC):
                    P.add("pe", lambda e, k=k, bank=bank, b=b, tsl=tsl: e.matmul(
                        ps[bank][:], wz[b][:, k, :], hT[:, k, tsl], start=(k == 0), stop=(k == KC - 1)),
                        r=[("wz", b), hk(k, T)], w=[PS(bank)])
                P.add("act", lambda e, bank=bank, zb_=zb_: e.activation(zt[zb_][:], ps[bank][:], AF.Silu), r=[PS(bank)], w=[("zt", zb_)])
                P.add("dve", lambda e, h=h, tsl=tsl, zb_=zb_: e.tensor_tensor(out=o_dnT[:, h, tsl], in0=o_dnT[:, h, tsl], in1=zt[zb_][:],
                                                                           op=ALU.mult), r=[("zt", zb_), ("odn", h, T)], w=[("odn", h, T)])

    def mixer(l):
        for T in range(4):
            norm_to_h(T, nwm[:, l, :], sqm)
        P.barrier(bar[:])
        if do_dn:
            deltanet(l)
            if l == 0:
                dump("odnT2", o_dnT[:].rearrange("p h s -> p (h s)")); dump("zt0", zt[0][:]); dump("hT2", hT[:].rearrange("p k s -> p (k s)"))
        else:
            for h in range(4):
                for T in range(4):
                    P.add("dve", lambda e, h=h, T=T: e.memset(o_dnT[:, h, T * 512:(T + 1) * 512], 0.0),
                          w=[("odn", h, T)])
        P.barrier(bar[:])
        if do_da:
            attention(l)
        else:
            for h in range(4):
                for T in range(4):
                    P.add("dve", lambda e, h=h, T=T: e.memset(o_daT[:, h, T * 512:(T + 1) * 512], 0.0),
                          w=[("oda", h, T)])
        P.barrier(bar[:])
        if l == 0:
            dump("wq", wqkv[0][:, 0, :, :].rearrange("p k f -> p (k f)")); dump("wv", wqkv[0][:, 2, :, :].rearrange("p k f -> p (k f)"))
            dump("aq0", aqh[0][:]); dump("ak0", akm[0][:]); dump("av0", avh[0][:].rearrange("p a b -> p (a b)"))
            dump("pT0", pT[0][:]); dump("tmpf0", tmpf[0][:]); dump("rz", rz[:]); dump("o0", o0[:]); dump("oc", oc[:])
            dump("t1", t1[:]); dump("rstd", rstd[:]); dump("odaT", o_daT[:].rearrange("p h s -> p (h s)"))
            dump("hT", hT[:].rearrange("p k s -> p (k s)")); dump("neglam", neglam[:]); dump("sublnw", sublnw[:])
            P.barrier(bar[:])
        merge(l, do_dn, do_da)
        P.barrier(bar[:])

    for s in range(nseq):
        load_x(s)
        P.barrier(bar[:])
        for l in range(depth):
            if do_ffn:
                ffn(l, nw1, W["ffn1_wg"], W["ffn1_wu"], W["ffn1_wd"])
                P.barrier(bar[:])
            if do_mix:
                mixer(l)
            if do_ffn:
                ffn(l, nw2, W["ffn2_wg"], W["ffn2_wu"], W["ffn2_wd"])
                P.barrier(bar[:])
        store_y(s)

    P.emit(same_engine_sync=same_engine_sync)
    return nc, len(P.ops)


def make_consts():
    k = np.arange(128, dtype=np.float32)[:, None]
    q = np.arange(512, dtype=np.float32)[None, :]
    jj = np.arange(896, dtype=np.float32)[None, :]
    i = np.arange(128)
    incu = (i[:, None] <= i[None, :]).astype(np.float32)
    stru = (i[:, None] < i[None, :]).astype(np.float32)
    masks = np.stack([incu, stru, incu.T.copy(), stru.T.copy()]).astype(np.float32)
    kaug = np.zeros((128, 128), np.float32)
    qaug = np.zeros((4, 2, 128, 512), np.float32)
    qr = np.arange(512, dtype=np.float32)
    q_lo = np.mod(qr, 256.0)
    q_hi = qr - q_lo
    for base in (0,):
        kaug[base + 0] = np.arange(128, dtype=np.float32)
        kaug[base + 1] = 1.0
        kaug[base + 2] = 1.0
        for h in range(4):
            sl = 2.0 ** (-8.0 * (h + 1) / 4)
            for sg, sign in ((0, 1.0), (1, -1.0)):
                qaug[h, sg, base + 0] = sign * sl
                qaug[h, sg, base + 1] = -sign * sl * q_lo
                qaug[h, sg, base + 2] = -sign * sl * q_hi
    return {"c_ident": np.eye(128, dtype=np.float32), "c_kaug": kaug, "c_qaug": qaug, "c_bs": np.ascontiguousarray(q - k),
            "c_abs": np.ascontiguousarray(np.abs(jj - 384.0 - k)), "c_masks": masks}


_CACHE = {}


def kernel(**inputs):
    xs = np.concatenate([np.asarray(inputs["x_prompt"], np.float32), np.asarray(inputs["x_sample"], np.float32)], axis=0)
    nb_p = inputs["x_prompt"].shape[0]
    if "nc" not in _CACHE:
        _CACHE["nc"] = build()[0]
    nc = _CACHE["nc"]
    wmap = {name: np.ascontiguousarray(np.asarray(inputs[name], np.float32)) for name, _ in WSHAPES}
    consts = make_consts()
    in_maps = []
    for c in range(NCORES):
        m = {"x": np.ascontiguousarray(xs[c * NSEQ:(c + 1) * NSEQ])}
        m.update(wmap)
        m.update(consts)
        in_maps.append(m)
    res = run_bass_kernel_spmd(nc, in_maps, core_ids=list(range(NCORES)))
    yfull = np.concatenate([np.asarray(r["y"], np.float32) for r in res.results], axis=0)
    return (np.ascontiguousarray(yfull[:nb_p]), np.ascontiguousarray(yfull[nb_p:]))
```
